# Optimizing a Trainium2 kernel written in Bass

```python
import math
import jax, jax.numpy as jnp
from jax import lax
import numpy as np

D_MODEL = 2048
BATCH = 1
SEQ = 16384
DEPTH = 1
DEC_BATCH = 16
DEC_SEQ = 16
PAST_LEN = 4096

CHUNK = 64
SSM_EXPAND = 2
SSM_INNER = SSM_EXPAND * D_MODEL
SSM_HEAD_DIM = 64
SSM_HEADS = SSM_INNER // SSM_HEAD_DIM
SSM_GROUPS = 8
SSM_STATE = 128
CONV_WIDTH = 4
CONV_DIM = SSM_INNER + 2 * SSM_GROUPS * SSM_STATE
MLA_HEADS = 16
Q_LORA = 512
KV_LORA = 512
QK_NOPE = 128
QK_ROPE = 64
V_DIM = 128
MLA_INNER = MLA_HEADS * V_DIM
ROPE_THETA = 10000.0
Q_BLOCK = 128
EPS = 1e-6
IN_SIZES = (SSM_INNER, CONV_DIM, SSM_HEADS, Q_LORA, KV_LORA, QK_ROPE, MLA_INNER, D_MODEL, D_MODEL)
IN_DIM = sum(IN_SIZES)

kernel_name = 'hybrid_ssd_mla_streaming_step'


def _rmsnorm(x, w):
    xf = x.astype(jnp.float32)
    y = xf * lax.rsqrt(jnp.mean(xf * xf, axis=-1, keepdims=True) + EPS)
    return (y * w.astype(jnp.float32)).astype(x.dtype)


def _rope(x, pos):
    half = QK_ROPE // 2
    inv = 1.0 / (ROPE_THETA ** (jnp.arange(half, dtype=jnp.float32) * (2.0 / QK_ROPE)))
    ang = pos.astype(jnp.float32)[:, None] * inv[None, :]
    shape = (1, pos.shape[0]) + (1,) * (x.ndim - 3) + (half,)
    cos = jnp.cos(ang).reshape(shape)
    sin = jnp.sin(ang).reshape(shape)
    xf = x.astype(jnp.float32)
    x1, x2 = xf[..., :half], xf[..., half:]
    return jnp.concatenate([x1 * cos - x2 * sin, x1 * sin + x2 * cos], axis=-1).astype(x.dtype)


def _causal_conv(xbc, conv_prev, conv_w, conv_b):
    l = xbc.shape[1]
    xpad = jnp.concatenate([conv_prev.astype(xbc.dtype), xbc], axis=1)
    out = conv_b + xpad[:, 0:l] * conv_w[0]
    for k in range(1, CONV_WIDTH):
        out = out + xpad[:, k:k + l] * conv_w[k]
    return jax.nn.silu(out), xpad[:, -(CONV_WIDTH - 1):]


def _ssd(xh, dt, a, bm, cm, s0):
    b, l, h, p = xh.shape
    g, n = bm.shape[2], bm.shape[3]
    hg = h // g
    q = min(CHUNK, l)
    nc = l // q
    f32 = jnp.float32
    x = xh.astype(f32).reshape(b, nc, q, g, hg, p)
    dtc = dt.reshape(b, nc, q, g, hg)
    bc = bm.astype(f32).reshape(b, nc, q, g, n)
    cc = cm.astype(f32).reshape(b, nc, q, g, n)
    acs = jnp.cumsum(dtc * a.reshape(g, hg), axis=2)
    causal = jnp.tril(jnp.ones((q, q), dtype=bool))[None, None, :, :, None, None]
    seg = acs[:, :, :, None] - acs[:, :, None, :]
    decay = jnp.exp(jnp.where(causal, seg, -jnp.inf))
    cb = jnp.einsum('bcign,bcjgn->bcijg', cc, bc)
    wts = cb[..., None] * decay * dtc[:, :, None]
    y_diag = jnp.einsum('bcijgh,bcjghp->bcighp', wts, x)
    tail = jnp.exp(acs[:, :, -1:] - acs) * dtc
    chunk_states = jnp.einsum('bcjgn,bcjghp->bcghpn', bc, x * tail[..., None])
    chunk_decay = jnp.exp(acs[:, :, -1])

    def step(s, inp):
        st, dec = inp
        return s * dec[..., None, None] + st, s

    s_init = s0.astype(f32).reshape(b, g, hg, p, n)
    s_final, s_prev = lax.scan(step, s_init, (jnp.moveaxis(chunk_states, 1, 0), jnp.moveaxis(chunk_decay, 1, 0)))
    s_prev = jnp.moveaxis(s_prev, 0, 1)
    y_off = jnp.einsum('bcign,bcghpn->bcighp', cc, s_prev) * jnp.exp(acs)[..., None]
    y = (y_diag + y_off).reshape(b, l, h, p)
    return y, s_final.reshape(b, h, p, n)


def _attend_prompt(q_nope, q_rope, ckv, kr, w_uk, w_uv):
    b, s = ckv.shape[0], ckv.shape[1]
    k_nope = jnp.einsum('bsc,chd->bshd', ckv, w_uk)
    v = jnp.einsum('bsc,chd->bshd', ckv, w_uv)
    key_chunk = jnp.arange(s) // CHUNK
    scale = (QK_NOPE + QK_ROPE) ** -0.5

    def block(start):
        qn = lax.dynamic_slice_in_dim(q_nope, start, Q_BLOCK, axis=1)
        qr = lax.dynamic_slice_in_dim(q_rope, start, Q_BLOCK, axis=1)
        sc = (jnp.einsum('bqhd,bkhd->bhqk', qn, k_nope).astype(jnp.float32)
              + jnp.einsum('bqhr,bkr->bhqk', qr, kr).astype(jnp.float32)) * scale
        q_chunk = (start + jnp.arange(Q_BLOCK)) // CHUNK
        mask = key_chunk[None, :] <= q_chunk[:, None]
        pr = jax.nn.softmax(jnp.where(mask, sc, -jnp.inf), axis=-1)
        return jnp.einsum('bhqk,bkhd->bqhd', pr.astype(v.dtype), v)

    starts = jnp.arange(s // Q_BLOCK) * Q_BLOCK
    o = lax.map(block, starts)
    return jnp.transpose(o, (1, 0, 2, 3, 4)).reshape(b, s, MLA_INNER)


def _attend_cached(q_nope, q_rope, ckv_all, kr_all, w_uk, w_uv):
    b, t = q_nope.shape[0], q_nope.shape[1]
    scale = (QK_NOPE + QK_ROPE) ** -0.5
    q_lat = jnp.einsum('bthd,chd->bthc', q_nope, w_uk)
    sc = (jnp.einsum('bthc,bkc->bhtk', q_lat, ckv_all).astype(jnp.float32)
          + jnp.einsum('bthr,bkr->bhtk', q_rope, kr_all).astype(jnp.float32)) * scale
    pr = jax.nn.softmax(sc, axis=-1)
    o_lat = jnp.einsum('bhtk,bkc->bthc', pr.astype(ckv_all.dtype), ckv_all)
    return jnp.einsum('bthc,chd->bthd', o_lat, w_uv).reshape(b, t, MLA_INNER)


def _layer(x, pos, conv_prev, ssm_prev, cache_kv, cache_kr, norm_in_w, w_in, conv_w, conv_b, dt_bias, a_log,
           d_skip, ssm_norm_w, w_ssm_out, q_norm_w, w_q_up, kv_norm_w, w_kv_up, w_mla_out, w_out):
    b, l, _ = x.shape
    f32 = jnp.float32
    h = _rmsnorm(x, norm_in_w)
    offsets = [int(o) for o in np.cumsum(IN_SIZES)[:-1]]
    z, xbc, dt_raw, cq, ckv_raw, kr_raw, g_mla, gate_ssm, gate_mla = jnp.split(h @ w_in, offsets, axis=-1)

    xbc_c, new_conv = _causal_conv(xbc, conv_prev, conv_w, conv_b)
    xs = xbc_c[..., :SSM_INNER].reshape(b, l, SSM_HEADS, SSM_HEAD_DIM)
    bm = xbc_c[..., SSM_INNER:SSM_INNER + SSM_GROUPS * SSM_STATE].reshape(b, l, SSM_GROUPS, SSM_STATE)
    cm = xbc_c[..., SSM_INNER + SSM_GROUPS * SSM_STATE:].reshape(b, l, SSM_GROUPS, SSM_STATE)
    dt = jax.nn.softplus(dt_raw.astype(f32) + dt_bias.astype(f32))
    a = -jnp.exp(a_log.astype(f32))
    y, new_ssm = _ssd(xs, dt, a, bm, cm, ssm_prev)
    y = (y + d_skip.astype(f32)[:, None] * xs.astype(f32)).reshape(b, l, SSM_INNER) * jax.nn.silu(z.astype(f32))
    yg = y.reshape(b, l, SSM_GROUPS, SSM_INNER // SSM_GROUPS)
    yg = yg * lax.rsqrt(jnp.mean(yg * yg, axis=-1, keepdims=True) + EPS)
    y = yg.reshape(b, l, SSM_INNER) * ssm_norm_w.astype(f32)
    y_ssm = y.astype(x.dtype) @ w_ssm_out

    q = (_rmsnorm(cq, q_norm_w) @ w_q_up).reshape(b, l, MLA_HEADS, QK_NOPE + QK_ROPE)
    q_nope = q[..., :QK_NOPE]
    q_rope = _rope(q[..., QK_NOPE:], pos)
    ckv = _rmsnorm(ckv_raw, kv_norm_w)
    kr = _rope(kr_raw, pos)
    w_kv = w_kv_up.reshape(KV_LORA, MLA_HEADS, QK_NOPE + V_DIM)
    w_uk, w_uv = w_kv[..., :QK_NOPE], w_kv[..., QK_NOPE:]
    if cache_kv is None:
        attn = _attend_prompt(q_nope, q_rope, ckv, kr, w_uk, w_uv)
    else:
        ckv_all = jnp.concatenate([cache_kv.astype(ckv.dtype), ckv], axis=1)
        kr_all = jnp.concatenate([cache_kr.astype(kr.dtype), kr], axis=1)
        attn = _attend_cached(q_nope, q_rope, ckv_all, kr_all, w_uk, w_uv)
    y_mla = (attn * jax.nn.silu(g_mla)) @ w_mla_out

    mixed = jax.nn.sigmoid(gate_ssm) * y_ssm + jax.nn.sigmoid(gate_mla) * y_mla
    x_out = x + mixed @ w_out
    return x_out, ckv, kr, new_ssm.astype(x.dtype), new_conv


def setup_inputs(seed: int = 0) -> dict:
    key = jax.random.key(seed)
    ks = jax.random.split(key, 24)

    def nrm(k, shape, scale):
        return jax.random.normal(k, shape, jnp.float32) * scale

    dt0 = jnp.exp(jax.random.uniform(ks[10], (DEPTH, SSM_HEADS), jnp.float32,
                                     minval=math.log(1e-3), maxval=math.log(1e-1)))
    return {
        'x_prompt': nrm(ks[0], (BATCH, SEQ, D_MODEL), 1.0),
        'x_sample': nrm(ks[1], (DEC_BATCH, DEC_SEQ, D_MODEL), 1.0),
        'cache_kv_latent': nrm(ks[2], (DEPTH, DEC_BATCH, PAST_LEN, KV_LORA), 1.0),
        'cache_k_rope': nrm(ks[3], (DEPTH, DEC_BATCH, PAST_LEN, QK_ROPE), 1.0),
        'state_ssm': nrm(ks[4], (DEPTH, DEC_BATCH, SSM_HEADS, SSM_HEAD_DIM, SSM_STATE), 0.1),
        'state_conv': nrm(ks[5], (DEPTH, DEC_BATCH, CONV_WIDTH - 1, CONV_DIM), 1.0),
        'norm_in_w': 1.0 + nrm(ks[6], (DEPTH, D_MODEL), 0.02),
        'w_in': nrm(ks[7], (DEPTH, D_MODEL, IN_DIM), D_MODEL ** -0.5),
        'conv_w': nrm(ks[8], (DEPTH, CONV_WIDTH, CONV_DIM), CONV_WIDTH ** -0.5),
        'conv_b': nrm(ks[9], (DEPTH, CONV_DIM), 0.02),
        'dt_bias': dt0 + jnp.log(-jnp.expm1(-dt0)),
        'a_log': jnp.log(jax.random.uniform(ks[11], (DEPTH, SSM_HEADS), jnp.float32, minval=1.0, maxval=16.0)),
        'd_skip': 1.0 + nrm(ks[12], (DEPTH, SSM_HEADS), 0.1),
        'ssm_norm_w': 1.0 + nrm(ks[13], (DEPTH, SSM_INNER), 0.02),
        'w_ssm_out': nrm(ks[14], (DEPTH, SSM_INNER, D_MODEL), SSM_INNER ** -0.5),
        'q_norm_w': 1.0 + nrm(ks[15], (DEPTH, Q_LORA), 0.02),
        'w_q_up': nrm(ks[16], (DEPTH, Q_LORA, MLA_HEADS * (QK_NOPE + QK_ROPE)), Q_LORA ** -0.5),
        'kv_norm_w': 1.0 + nrm(ks[17], (DEPTH, KV_LORA), 0.02),
        'w_kv_up': nrm(ks[18], (DEPTH, KV_LORA, MLA_HEADS * (QK_NOPE + V_DIM)), KV_LORA ** -0.5),
        'w_mla_out': nrm(ks[19], (DEPTH, MLA_INNER, D_MODEL), MLA_INNER ** -0.5),
        'w_out': nrm(ks[20], (DEPTH, D_MODEL, D_MODEL), D_MODEL ** -0.5),
        'final_norm_w': 1.0 + nrm(ks[21], (D_MODEL,), 0.02),
    }


def reference(x_prompt, x_sample, cache_kv_latent, cache_k_rope, state_ssm, state_conv, norm_in_w, w_in,
              conv_w, conv_b, dt_bias, a_log, d_skip, ssm_norm_w, w_ssm_out, q_norm_w, w_q_up, kv_norm_w,
              w_kv_up, w_mla_out, w_out, final_norm_w):
    b_p, s_p = x_prompt.shape[0], x_prompt.shape[1]
    pos_p = jnp.arange(s_p)
    pos_s = cache_kv_latent.shape[2] + jnp.arange(x_sample.shape[1])
    xp, xs = x_prompt, x_sample
    kvp, krp, ssmp, convp = [], [], [], []
    kvs, krs, ssms, convs = [], [], [], []
    for i in range(DEPTH):
        lw = dict(norm_in_w=norm_in_w[i], w_in=w_in[i], conv_w=conv_w[i], conv_b=conv_b[i], dt_bias=dt_bias[i],
                  a_log=a_log[i], d_skip=d_skip[i], ssm_norm_w=ssm_norm_w[i], w_ssm_out=w_ssm_out[i],
                  q_norm_w=q_norm_w[i], w_q_up=w_q_up[i], kv_norm_w=kv_norm_w[i], w_kv_up=w_kv_up[i],
                  w_mla_out=w_mla_out[i], w_out=w_out[i])
        conv0 = jnp.zeros((b_p, CONV_WIDTH - 1, CONV_DIM), x_prompt.dtype)
        ssm0 = jnp.zeros((b_p, SSM_HEADS, SSM_HEAD_DIM, SSM_STATE), x_prompt.dtype)
        xp, a1, a2, a3, a4 = _layer(xp, pos_p, conv0, ssm0, None, None, **lw)
        kvp.append(a1); krp.append(a2); ssmp.append(a3); convp.append(a4)
        xs, c1, c2, c3, c4 = _layer(xs, pos_s, state_conv[i], state_ssm[i], cache_kv_latent[i], cache_k_rope[i], **lw)
        kvs.append(c1); krs.append(c2); ssms.append(c3); convs.append(c4)
    y_prompt = _rmsnorm(xp, final_norm_w)
    y_sample = _rmsnorm(xs, final_norm_w)
    return (y_prompt, y_sample, jnp.stack(kvp), jnp.stack(krp), jnp.stack(ssmp), jnp.stack(convp),
            jnp.stack(kvs), jnp.stack(krs), jnp.stack(ssms), jnp.stack(convs))
```

```python
import numpy as np
import ml_dtypes
from contextlib import ExitStack
import concourse.bass as bass
import concourse.mybir as mybir
from concourse.bass_utils import run_bass_kernel_spmd

F32 = mybir.dt.float32
BF16 = mybir.dt.bfloat16
I32 = mybir.dt.int32
AF = mybir.ActivationFunctionType
ALU = mybir.AluOpType
AX = mybir.AxisListType

import os
STOP = int(os.environ.get("K_STOP", "99"))
PH = os.environ.get("K_PH", "MCAS")
SAMEWAIT = bool(int(os.environ.get("K_SAMEWAIT", "1")))
SUB = int(os.environ.get("K_SUB", "99"))
NCORES = 8
D = 2048
KC = 16
EPS = 1e-6
TWO_PI = 6.283185307179586
PI = 3.141592653589793


class Tl:
    __slots__ = ("ap", "w", "r")

    def __init__(self, ap):
        self.ap = ap
        self.w = None
        self.r = {}

    def __getitem__(self, k):
        return self.ap[k]


class _Rec:
    def __init__(self):
        self.calls = []

    def __getattr__(self, name):
        def f(*a, **k):
            self.calls.append((name, a, k))
            return self
        return f


class Sch:
    def __init__(self, nc, ndma=40):
        self.nc = nc
        self.names = ("pe", "act", "dve", "pool", "sp")
        self.q = {k: [] for k in self.names}
        self.sem = {k: nc.alloc_semaphore("sm_" + k) for k in self.names}
        self.cnt = {k: 0 for k in self.names}
        self.seen = {k: {} for k in self.names}
        self.dsem = [nc.alloc_semaphore("sd%d" % i) for i in range(ndma)]
        self.dcnt = [0] * ndma
        self.rr = 0
        self.ccsem = nc.alloc_semaphore("sm_cc")
        self.cccnt = 0
        self.rank = {}

    def _wait(self, en, key, val):
        if val is None or val <= 0:
            return
        if key == en and (en == "pe" or not SAMEWAIT) and en in ("pe", "act", "dve"):
            return
        if self.seen[en].get(key, 0) >= val:
            return
        self.seen[en][key] = val
        sem = self.sem[key] if isinstance(key, str) else self.dsem[key]
        self.q[en].append(lambda e, sem=sem, val=val: e.wait_ge(sem, val))

    def _deps(self, en, reads, writes):
        for t in reads:
            if t.w is not None:
                self._wait(en, *t.w)
        for t in writes:
            if t.w is not None:
                self._wait(en, *t.w)
            for k, v in t.r.items():
                self._wait(en, k, v)

    def _mark(self, tk, reads, writes):
        for t in reads:
            if t.r.get(tk[0], 0) < tk[1]:
                t.r[tk[0]] = tk[1]
        for t in writes:
            t.w = tk
            t.r = {}

    def op(self, en, fn, reads=(), writes=()):
        self._deps(en, reads, writes)
        self.cnt[en] += 1
        sem = self.sem[en]
        rec = _Rec()
        fn(rec)
        assert len(rec.calls) == 1
        name, a, k = rec.calls[0]
        self.q[en].append(lambda e, name=name, a=a, k=k: getattr(e, name)(*a, **k).then_inc(sem, 1))
        tk = (en, self.cnt[en])
        self._mark(tk, reads, writes)
        return tk

    def dma(self, en, out, in_, reads=(), writes=(), **kw):
        self._deps(en, reads, writes)
        i = self.rr
        self.rr = (self.rr + 1) % len(self.dsem)
        self._wait(en, i, self.dcnt[i])
        self.dcnt[i] += 16
        dsem = self.dsem[i]

        def emit(e):
            o, n = out, in_
            if callable(o) or callable(n):
                if en not in self.rank:
                    self.rank[en] = {"r": e.partition_id()}
                if callable(o):
                    o = o(self.rank[en])
                if callable(n):
                    n = n(self.rank[en])
            try:
                e.dma_start(out=o, in_=n, **kw).then_inc(dsem, 16)
            except Exception:
                print("DMA FAIL", en, o, n, flush=True)
                raise
        self.q[en].append(emit)
        tk = (i, self.dcnt[i])
        self._mark(tk, reads, writes)
        return tk

    def coll(self, in_ap, out_ap, reads=(), writes=()):
        self._deps("pool", reads, writes)
        self.cccnt += 1
        n = self.cccnt
        sem = self.ccsem

        def emit(e):
            e.collective_compute("AllGather", ALU.bypass, replica_groups=[list(range(NCORES))],
                                 ins=[in_ap.opt()], outs=[out_ap.opt()]).then_inc(sem, 1)
            e.wait_ge(sem, n)
        self.q["pool"].append(emit)

    def barrier(self):
        for en in self.names:
            for k in self.names:
                self._wait(en, k, self.cnt[k])
            for i in range(len(self.dsem)):
                self._wait(en, i, self.dcnt[i])
            if self.cccnt and en != "pool" and self.seen[en].get("cc", 0) < self.cccnt:
                self.seen[en]["cc"] = self.cccnt
                self.q[en].append(lambda e, sem=self.ccsem, val=self.cccnt: e.wait_ge(sem, val))

    def emit(self, use_block=None):
        nc = self.nc
        if use_block is None:
            use_block = bool(int(os.environ.get("K_BLOCK", "1")))
        if not use_block:
            eng = dict(pe=nc.tensor, act=nc.scalar, dve=nc.vector, pool=nc.gpsimd, sp=nc.sync)
            for en in self.names:
                for f in self.q[en]:
                    f(eng[en])
            return
        with nc.Block() as block:
            starters = dict(pe=block.tensor, act=block.scalar, dve=block.vector, pool=block.gpsimd, sp=block.sync)
            for en in self.names:
                fl = self.q[en]
                if not fl:
                    continue

                def body(e, fl=fl):
                    for f in fl:
                        f(e)
                starters[en](body)


class Ctx:
    ARENA = 206 * 1024

    def __init__(self, nc):
        self.nc = nc
        self.s = Sch(nc)
        self.banks = [Tl(nc.alloc_psum_tensor("pb%d" % i, [128, 512], F32).ap()) for i in range(8)]
        self.bi = 0
        self.flip = 0
        self.arena = nc.alloc_sbuf_tensor("arena", [128, self.ARENA], mybir.dt.uint8).ap()
        self.ptr = 0

    def bank(self):
        b = self.banks[self.bi]
        self.bi = (self.bi + 1) % 8
        return b

    def _restore(self, p):
        self.ptr = p

    def sb(self, es, name, shape, dt):
        shape = list(shape)
        esz = {F32: 4, BF16: 2, I32: 4}[dt]
        n = 1
        for d in shape[1:]:
            n *= d
        nbytes = ((n * esz + 31) // 32) * 32
        old = self.ptr
        assert old + nbytes <= self.ARENA, "SBUF arena overflow at %s: %d + %d" % (name, old, nbytes)
        self.ptr = old + nbytes
        es.callback(self._restore, old)
        ap = self.arena[:shape[0], old:old + n * esz].bitcast(dt)
        if len(shape) == 3:
            ap = ap.rearrange("p (a b) -> p a b", a=shape[1])
        elif len(shape) == 4:
            ap = ap.rearrange("p (a b c) -> p a b c", a=shape[1], b=shape[2])
        return Tl(ap)

    def evac_eng(self):
        self.flip ^= 1
        return "act" if self.flip else "dve"


def make_consts(cx, es):
    nc, s = cx.nc, cx.s
    c = {}
    io = cx.sb(es, "c_io", [128, 128], I32)
    s.op("pool", lambda e: e.iota(io.ap, pattern=[[1, 128]], base=0, channel_multiplier=-1), writes=[io])
    iof = cx.sb(es, "c_iof", [128, 128], F32)
    s.op("dve", lambda e: e.tensor_copy(out=iof.ap, in_=io.ap), reads=[io], writes=[iof])
    c["identf"] = cx.sb(es, "c_identf", [128, 128], F32)
    s.op("dve", lambda e: e.tensor_single_scalar(out=c["identf"].ap, in_=iof.ap, scalar=0.0, op=ALU.is_equal),
         reads=[iof], writes=[c["identf"]])
    c["ident"] = cx.sb(es, "c_ident", [128, 128], BF16)
    s.op("dve", lambda e: e.tensor_copy(out=c["ident"].ap, in_=c["identf"].ap), reads=[c["identf"]], writes=[c["ident"]])
    c["U"] = cx.sb(es, "c_U", [128, 128], F32)
    s.op("dve", lambda e: e.tensor_single_scalar(out=c["U"].ap, in_=iof.ap, scalar=0.0, op=ALU.is_ge),
         reads=[iof], writes=[c["U"]])
    c["onesb"] = cx.sb(es, "c_onesb", [128, 128], BF16)
    s.op("dve", lambda e: e.memset(c["onesb"].ap, 1.0), writes=[c["onesb"]])
    c["onesf"] = cx.sb(es, "c_onesf", [128, 128], F32)
    s.op("dve", lambda e: e.memset(c["onesf"].ap, 1.0), writes=[c["onesf"]])
    c["iof"] = iof
    return c


def load_norm_T(cx, c, x_rows_ap, ntok, xt, xs, junk, hT, col0, small):
    s = cx.s
    s.dma("sp", xt.ap[:ntok, :], x_rows_ap, writes=[xt])
    ssq, rstd = small
    s.op("act", lambda e: e.activation(out=junk.ap[:ntok, :], in_=xt.ap[:ntok, :], func=AF.Square, accum_out=ssq.ap[:ntok, :]),
         reads=[xt], writes=[junk, ssq])
    s.op("act", lambda e: e.activation(out=rstd.ap[:ntok, :], in_=ssq.ap[:ntok, :], func=AF.Sqrt, scale=1.0 / D, bias=c["eps"].ap[:ntok, :]),
         reads=[ssq, c["eps"]], writes=[rstd])
    s.op("dve", lambda e: e.reciprocal(out=rstd.ap[:ntok, :], in_=rstd.ap[:ntok, :]), reads=[rstd], writes=[rstd])
    s.op("dve", lambda e: e.tensor_scalar(out=xs.ap[:ntok, :], in0=xt.ap[:ntok, :], scalar1=rstd.ap[:ntok, :], scalar2=None, op0=ALU.mult),
         reads=[xt, rstd], writes=[xs])
    for g in range(2):
        b = cx.bank()
        bv = b.ap.bitcast(BF16)
        for j in range(8):
            kc = g * 8 + j
            s.op("pe", lambda e, j=j, kc=kc: e.transpose(out=bv[:, j * 128:j * 128 + ntok], in_=xs.ap[:ntok, kc * 128:(kc + 1) * 128],
                                                         identity=c["ident"].ap[:ntok, :ntok]),
                 reads=[xs, c["ident"]], writes=[b])
        en = cx.evac_eng()
        src = bv.rearrange("p (j t) -> p j t", j=8)[:, :, :ntok]
        dst = hT.ap[:, g * 8:(g + 1) * 8, col0:col0 + ntok]
        if en == "act":
            s.op("act", lambda e: e.copy(out=dst, in_=src), reads=[b], writes=[hT])
        else:
            s.op("dve", lambda e: e.tensor_copy(out=dst, in_=src), reads=[b], writes=[hT])


def load_weights_bf16(cx, es, name, w_dram, nkc, ncols, scale_tl=None, const_scale=1.0, stage=None):
    s = cx.s
    wt = cx.sb(es, name, [128, nkc, ncols], BF16)
    CH = stage.ap.shape[1]
    for kc in range(nkc):
        for c0 in range(0, ncols, CH):
            cw = min(CH, ncols - c0)
            s.dma("sp", stage.ap[:, :cw], w_dram[:, kc, c0:c0 + cw], writes=[stage])
            if scale_tl is not None:
                s.op("act", lambda e, kc=kc, c0=c0, cw=cw: e.activation(out=wt.ap[:, kc, c0:c0 + cw], in_=stage.ap[:, :cw], func=AF.Copy,
                                                                        scale=scale_tl.ap[:, kc:kc + 1]),
                     reads=[stage, scale_tl], writes=[wt])
            else:
                s.op("act", lambda e, kc=kc, c0=c0, cw=cw: e.activation(out=wt.ap[:, kc, c0:c0 + cw], in_=stage.ap[:, :cw], func=AF.Copy,
                                                                        scale=float(const_scale)),
                     reads=[stage], writes=[wt])
    return wt


def mm_acc(cx, bank_ap, pairs, reads, bank):
    n = len(pairs)
    for i, (l, r) in enumerate(pairs):
        cx.s.op("pe", lambda e, l=l, r=r, i=i: e.matmul(bank_ap, lhsT=l, rhs=r, start=(i == 0), stop=(i == n - 1)),
                reads=reads, writes=[bank])


def rope_tables(cx, c, tabs, pos_tl, nsub):
    s = cx.s
    ang, kf, ki, cos, sin, m1 = tabs
    for (dst, shift) in ((sin, 0.0), (cos, PI / 2)):
        s.op("dve", lambda e: e.tensor_tensor(out=ang.ap[:, :nsub, :], in0=pos_tl.ap[:, :nsub].unsqueeze(2).to_broadcast([128, nsub, 32]),
                                              in1=c["inv"].ap.unsqueeze(1).to_broadcast([128, nsub, 32]), op=ALU.mult),
             reads=[pos_tl, c["inv"]], writes=[ang])
        if shift:
            s.op("dve", lambda e: e.tensor_scalar(out=ang.ap[:, :nsub, :], in0=ang.ap[:, :nsub, :], scalar1=float(shift), scalar2=None, op0=ALU.add),
                 reads=[ang], writes=[ang])
        s.op("dve", lambda e: e.tensor_scalar(out=kf.ap[:, :nsub, :], in0=ang.ap[:, :nsub, :], scalar1=1.0 / TWO_PI, scalar2=None, op0=ALU.mult),
             reads=[ang], writes=[kf])
        s.op("dve", lambda e: e.tensor_copy(out=ki.ap[:, :nsub, :], in_=kf.ap[:, :nsub, :]), reads=[kf], writes=[ki])
        s.op("dve", lambda e: e.tensor_copy(out=kf.ap[:, :nsub, :], in_=ki.ap[:, :nsub, :]), reads=[ki], writes=[kf])
        s.op("dve", lambda e: e.scalar_tensor_tensor(out=ang.ap[:, :nsub, :], in0=kf.ap[:, :nsub, :], scalar=-TWO_PI, in1=ang.ap[:, :nsub, :],
                                                     op0=ALU.mult, op1=ALU.add), reads=[kf, ang], writes=[ang])
        s.op("dve", lambda e: e.tensor_scalar(out=m1.ap[:, :nsub, :], in0=ang.ap[:, :nsub, :], scalar1=PI, scalar2=-TWO_PI, op0=ALU.is_gt, op1=ALU.mult),
             reads=[ang], writes=[m1])
        s.op("dve", lambda e: e.tensor_tensor(out=ang.ap[:, :nsub, :], in0=ang.ap[:, :nsub, :], in1=m1.ap[:, :nsub, :], op=ALU.add),
             reads=[ang, m1], writes=[ang])
        s.op("dve", lambda e: e.tensor_scalar(out=m1.ap[:, :nsub, :], in0=ang.ap[:, :nsub, :], scalar1=-PI, scalar2=TWO_PI, op0=ALU.is_lt, op1=ALU.mult),
             reads=[ang], writes=[m1])
        s.op("dve", lambda e: e.tensor_tensor(out=ang.ap[:, :nsub, :], in0=ang.ap[:, :nsub, :], in1=m1.ap[:, :nsub, :], op=ALU.add),
             reads=[ang, m1], writes=[ang])
        s.op("dve", lambda e: e.tensor_scalar(out=ang.ap[:, :nsub, :], in0=ang.ap[:, :nsub, :], scalar1=PI, scalar2=-PI, op0=ALU.min, op1=ALU.max),
             reads=[ang], writes=[ang])
        s.op("act", lambda e, dst=dst: e.activation(out=dst.ap[:, :nsub, :], in_=ang.ap[:, :nsub, :], func=AF.Sin), reads=[ang], writes=[dst])


def apply_rope(cx, src_ap, src_tl, nh, cos_ap, sin_ap, tabs_tl, out_tl, tmp_tl, ntok):
    s = cx.s
    x1 = src_ap[:, :, 0:32]
    x2 = src_ap[:, :, 32:64]
    cb = cos_ap.unsqueeze(1).to_broadcast([ntok, nh, 32])
    sb_ = sin_ap.unsqueeze(1).to_broadcast([ntok, nh, 32])
    o = out_tl.ap[:ntok].rearrange("p (h d) -> p h d", h=nh)
    t = tmp_tl.ap[:ntok].rearrange("p (h d) -> p h d", h=nh)
    rd = [src_tl] + list(tabs_tl)
    s.op("dve", lambda e: e.tensor_tensor(out=o[:, :, 0:32], in0=x1, in1=cb, op=ALU.mult), reads=rd, writes=[out_tl])
    s.op("dve", lambda e: e.tensor_tensor(out=t[:, :, 0:32], in0=x2, in1=sb_, op=ALU.mult), reads=rd, writes=[tmp_tl])
    s.op("dve", lambda e: e.tensor_tensor(out=o[:, :, 32:64], in0=x1, in1=sb_, op=ALU.mult), reads=rd, writes=[out_tl])
    s.op("dve", lambda e: e.tensor_tensor(out=t[:, :, 32:64], in0=x2, in1=cb, op=ALU.mult), reads=rd, writes=[tmp_tl])
    s.op("dve", lambda e: e.tensor_tensor(out=o[:, :, 0:32], in0=o[:, :, 0:32], in1=t[:, :, 0:32], op=ALU.subtract),
         reads=[out_tl, tmp_tl], writes=[out_tl])
    s.op("dve", lambda e: e.tensor_tensor(out=o[:, :, 32:64], in0=o[:, :, 32:64], in1=t[:, :, 32:64], op=ALU.add),
         reads=[out_tl, tmp_tl], writes=[out_tl])


def setup_small_consts(cx, es, c):
    s = cx.s
    c["eps"] = cx.sb(es, "c_eps", [128, 1], F32)
    s.op("dve", lambda e: e.memset(c["eps"].ap, EPS), writes=[c["eps"]])
    ji = cx.sb(es, "c_ji", [128, 32], I32)
    s.op("pool", lambda e: e.iota(ji.ap, pattern=[[1, 32]], base=0, channel_multiplier=0), writes=[ji])
    jf = cx.sb(es, "c_jf", [128, 32], F32)
    s.op("dve", lambda e: e.tensor_copy(out=jf.ap, in_=ji.ap), reads=[ji], writes=[jf])
    c["inv"] = cx.sb(es, "c_inv", [128, 32], F32)
    s.op("act", lambda e: e.activation(out=c["inv"].ap, in_=jf.ap, func=AF.Exp, scale=-float(np.log(10000.0)) / 32.0),
         reads=[jf], writes=[c["inv"]])
    pi_ = cx.sb(es, "c_pi", [128, 1], I32)
    s.op("pool", lambda e: e.iota(pi_.ap, pattern=[[0, 1]], base=0, channel_multiplier=1), writes=[pi_])
    c["pf"] = cx.sb(es, "c_pf", [128, 1], F32)
    s.op("dve", lambda e: e.tensor_copy(out=c["pf"].ap, in_=pi_.ap), reads=[pi_], writes=[c["pf"]])
    pm = cx.sb(es, "c_pm", [128, 1], I32)
    s.op("dve", lambda e: e.tensor_single_scalar(out=pm.ap, in_=pi_.ap, scalar=15, op=ALU.bitwise_and), reads=[pi_], writes=[pm])
    c["pm16"] = cx.sb(es, "c_pm16", [128, 1], F32)
    s.op("dve", lambda e: e.tensor_copy(out=c["pm16"].ap, in_=pm.ap), reads=[pm], writes=[c["pm16"]])


def evac(cx, dst_ap, dst_tl, src_ap, src_tl, en=None, extra_reads=()):
    en = en or cx.evac_eng()
    if en == "act":
        cx.s.op("act", lambda e: e.copy(out=dst_ap, in_=src_ap), reads=[src_tl] + list(extra_reads), writes=[dst_tl])
    else:
        cx.s.op("dve", lambda e: e.tensor_copy(out=dst_ap, in_=src_ap), reads=[src_tl] + list(extra_reads), writes=[dst_tl])


def rms_rstd(cx, c, src_ap, src_tl, ntok, n, junkf, ssq, rstd):
    s = cx.s
    s.op("act", lambda e: e.activation(out=junkf.ap[:ntok, :n], in_=src_ap, func=AF.Square, accum_out=ssq.ap[:ntok, :]),
         reads=[src_tl], writes=[junkf, ssq])
    s.op("act", lambda e: e.activation(out=rstd.ap[:ntok, :], in_=ssq.ap[:ntok, :], func=AF.Sqrt, scale=1.0 / n, bias=c["eps"].ap[:ntok, :]),
         reads=[ssq, c["eps"]], writes=[rstd])
    s.op("dve", lambda e: e.reciprocal(out=rstd.ap[:ntok, :], in_=rstd.ap[:ntok, :]), reads=[rstd], writes=[rstd])


def transpose_to(cx, c, src_tl, src_ap_fn, nblk, ntok, dst_fn, dst_tl, rows=128):
    s = cx.s
    b = cx.bank()
    bv = b.ap.bitcast(BF16)
    for j in range(nblk):
        s.op("pe", lambda e, j=j: e.transpose(out=bv[:rows, j * 128:j * 128 + ntok], in_=src_ap_fn(j), identity=c["ident"].ap[:ntok, :ntok]),
             reads=[src_tl, c["ident"]], writes=[b])
    for j in range(nblk):
        evac(cx, dst_fn(j), dst_tl, bv[:rows, j * 128:j * 128 + ntok], b)


def kv_from_latent(cx, c, ckvT, krT_unused, wkv, ST, nsub, kst, vst, KT_dram, V_dram, tok0):
    s = cx.s
    for h in range(2):
        b = cx.bank()
        mm_acc(cx, b.ap[:, :ST], [(wkv.ap[:, kc, h * 128:(h + 1) * 128], ckvT.ap[:, kc, :ST]) for kc in range(4)], [wkv, ckvT], b)
        evac(cx, kst.ap[:, h, :ST], kst, b.ap[:, :ST], b)
    s.dma("pool", KT_dram[:, :, tok0:tok0 + ST], kst.ap[:, :, :ST], reads=[kst])
    for sub in range(nsub):
        b = cx.bank()
        mm_acc(cx, b.ap[:, :256], [(ckvT.ap[:, kc, sub * 128:(sub + 1) * 128], wkv.ap[:, kc, 256:512]) for kc in range(4)], [wkv, ckvT], b)
        evac(cx, vst.ap[:, sub, :], vst, b.ap[:, :256], b)
    s.dma("pool", V_dram[tok0:tok0 + ST, :].rearrange("(n p) f -> p n f", p=128), vst.ap[:, :nsub, :], reads=[vst])


def attention_phase(cx, c, es, SEQ, DB, PAST, QT, QRT, KT, KRT, V, GT, KTc, KRTc, Vc, AT):
    s = cx.s
    qn = [cx.sb(es, "a_qn%d" % i, [128, 2, 512], BF16) for i in range(2)]
    qr = [cx.sb(es, "a_qr%d" % i, [64, 2, 512], BF16) for i in range(2)]
    gg = [cx.sb(es, "a_g%d" % i, [128, 2, 512], BF16) for i in range(2)]
    kt = [cx.sb(es, "a_kt%d" % i, [128, 2, 512], BF16) for i in range(3)]
    kr = [cx.sb(es, "a_kr%d" % i, [64, 512], BF16) for i in range(3)]
    vv = [cx.sb(es, "a_v%d" % i, [128, 4, 256], BF16) for i in range(3)]
    pT = [cx.sb(es, "a_p%d" % i, [128, 512], BF16) for i in range(4)]
    msk = [cx.sb(es, "a_m%d" % i, [128, 512], BF16) for i in range(4)]
    rec = cx.sb(es, "a_rec", [128, 512], F32)
    ot = cx.sb(es, "a_o", [128, 512], F32)
    ao = [cx.sb(es, "a_ao%d" % i, [128, 2, 512], BF16) for i in range(2)]
    for j in range(4):
        s.op("dve", lambda e, j=j: e.memset(msk[j].ap, 0.0), writes=[msk[j]])
        if 128 * j < 512:
            s.op("dve", lambda e, j=j: e.memset(msk[j].ap[0:64, 128 * j:], 1.0), writes=[msk[j]])
        if 128 * j + 64 < 512:
            s.op("dve", lambda e, j=j: e.memset(msk[j].ap[64:128, 128 * j + 64:], 1.0), writes=[msk[j]])
    O = [cx.banks[0], cx.banks[1]]
    Dn = [cx.banks[2], cx.banks[3]]
    st = {"srot": 0, "prot": 0, "kld": 0, "qld": 0}

    pend = []

    def flush():
        while pend:
            pend.pop(0)()

    def block(ktl, krl, vl, kt_ap, kr_ap, v_ap, nk, qtl, qrl, qn_ap, qr_ap, nq, first, last, mask):
        for h in range(2):
            S = cx.banks[4 + st["srot"] % 4]
            st["srot"] += 1
            s.op("pe", lambda e: e.matmul(S.ap[:nk, :nq], lhsT=kt_ap(h), rhs=qn_ap(h), start=True, stop=False), reads=[ktl, qtl], writes=[S])
            s.op("pe", lambda e: e.matmul(S.ap[:nk, :nq], lhsT=kr_ap, rhs=qr_ap(h), start=False, stop=True), reads=[krl, qrl], writes=[S])
            p = pT[st["prot"] % 4]
            st["prot"] += 1
            s.op("act", lambda e: e.activation(out=p.ap[:nk, :nq], in_=S.ap[:nk, :nq], func=AF.Exp), reads=[S], writes=[p])
            if mask is not None:
                s.op("dve", lambda e: e.tensor_tensor(out=p.ap[:nk, :nq], in0=p.ap[:nk, :nq], in1=mask.ap[:nk, :nq], op=ALU.mult), reads=[p, mask], writes=[p])

            def pv(h=h, p=p, vap=v_ap(h)):
                s.op("pe", lambda e: e.matmul(O[h].ap[:, :nq], lhsT=vap, rhs=p.ap[:nk, :nq], start=first, stop=last), reads=[vl, p], writes=[O[h]])
                s.op("pe", lambda e: e.matmul(Dn[h].ap[:, :nq], lhsT=c["onesb"].ap[:nk, :], rhs=p.ap[:nk, :nq], start=first, stop=last),
                     reads=[c["onesb"], p], writes=[Dn[h]])
            if len(pend) >= 2:
                pend.pop(0)()
            pend.append(pv)

    NB = (SEQ + DB * 16) // 256

    def finalize(gtl, g_ap, nq, out_tl, q0):
        flush()
        for h in range(2):
            s.op("dve", lambda e: e.reciprocal(out=rec.ap[:, :nq], in_=Dn[h].ap[:, :nq]), reads=[Dn[h]], writes=[rec])
            s.op("dve", lambda e: e.tensor_tensor(out=ot.ap[:, :nq], in0=O[h].ap[:, :nq], in1=rec.ap[:, :nq], op=ALU.mult), reads=[O[h], rec], writes=[ot])
            s.op("dve", lambda e: e.tensor_tensor(out=out_tl.ap[:, h, :nq], in0=ot.ap[:, :nq], in1=g_ap(h), op=ALU.mult), reads=[ot, gtl], writes=[out_tl])
        tb, off = q0 // 256, q0 % 256
        for h in range(2):
            if nq >= 256:
                s.dma("pool", AT[h * NB + tb:h * NB + tb + nq // 256].rearrange("n p t -> p n t"),
                      out_tl.ap[:, h, :nq].rearrange("p (n t) -> p n t", t=256), reads=[out_tl])
            else:
                s.dma("pool", AT[h * NB + tb][:, off:off + nq], out_tl.ap[:, h, :nq], reads=[out_tl])

    def load_keys(KTd, KRTd, Vd, k0, nk):
        i = st["kld"] % 3
        st["kld"] += 1
        s.dma("sp", kt[i].ap[:, :, :nk], KTd[:, :, k0:k0 + nk], writes=[kt[i]])
        s.dma("sp", kr[i].ap[:, :nk], KRTd[:, k0:k0 + nk], writes=[kr[i]])
        if nk >= 128:
            s.dma("sp", vv[i].ap[:, :nk // 128, :], Vd[k0:k0 + nk, :].rearrange("(n p) f -> p n f", p=128), writes=[vv[i]])
        else:
            s.dma("sp", vv[i].ap[:nk, 0, :], Vd[k0:k0 + nk, :], writes=[vv[i]])
        return kt[i], kr[i], vv[i]

    def load_q(q0, nq):
        i = st["qld"] % 2
        st["qld"] += 1
        s.dma("sp", qn[i].ap[:, :, :nq], QT[:, :, q0:q0 + nq], writes=[qn[i]])
        s.dma("sp", qr[i].ap[:, :, :nq], QRT[:, :, q0:q0 + nq], writes=[qr[i]])
        s.dma("sp", gg[i].ap[:, :, :nq], GT[:, :, q0:q0 + nq], writes=[gg[i]])
        return qn[i], qr[i], gg[i], ao[i]

    for qb in range(SEQ // 512):
        q_n, q_r, g_, a_ = load_q(qb * 512, 512)
        for ksb in range(qb + 1):
            k_, r_, v_ = load_keys(KT, KRT, V, ksb * 512, 512)
            for j in range(4):
                block(k_, r_, v_, lambda h, j=j: k_.ap[:, h, j * 128:(j + 1) * 128], r_.ap[:, j * 128:(j + 1) * 128],
                      lambda h, j=j: v_.ap[:, j, h * 128:(h + 1) * 128], 128,
                      q_n, q_r, lambda h: q_n.ap[:, h, :], lambda h: q_r.ap[:, h, :], 512,
                      first=(ksb == 0 and j == 0), last=(ksb == qb and j == 3), mask=(msk[j] if ksb == qb else None))
        finalize(g_, lambda h: g_.ap[:, h, :], 512, a_, qb * 512)
    for b_ in range(DB):
        q0 = SEQ + b_ * 16
        q_n, q_r, g_, a_ = load_q(q0, 16)
        nsb = PAST // 512
        for ksb in range(nsb):
            k_, r_, v_ = load_keys(KTc, KRTc, Vc, b_ * PAST + ksb * 512, 512)
            for j in range(4):
                block(k_, r_, v_, lambda h, j=j: k_.ap[:, h, j * 128:(j + 1) * 128], r_.ap[:, j * 128:(j + 1) * 128],
                      lambda h, j=j: v_.ap[:, j, h * 128:(h + 1) * 128], 128,
                      q_n, q_r, lambda h: q_n.ap[:, h, :16], lambda h: q_r.ap[:, h, :16], 16,
                      first=(ksb == 0 and j == 0), last=False, mask=None)
        k_, r_, v_ = load_keys(KT, KRT, V, q0, 16)
        block(k_, r_, v_, lambda h: k_.ap[:, h, :16], r_.ap[:, :16], lambda h: v_.ap[:16, 0, h * 128:(h + 1) * 128], 16,
              q_n, q_r, lambda h: q_n.ap[:, h, :16], lambda h: q_r.ap[:, h, :16], 16, first=(nsb == 0), last=True, mask=None)
        finalize(g_, lambda h: g_.ap[:, h, :16], 16, a_, q0)


def split_bf(cx, src_ap, src_tl, parts, shape_fn, nterms=2):
    s = cx.s
    s.op("dve", lambda e: e.tensor_copy(out=shape_fn(parts[0]), in_=src_ap), reads=[src_tl], writes=[parts[0]])
    if nterms == 2:
        s.op("dve", lambda e: e.tensor_tensor(out=shape_fn(parts[1]), in0=src_ap, in1=shape_fn(parts[0]), op=ALU.subtract),
             reads=[src_tl, parts[0]], writes=[parts[1]])
    else:
        r = parts[-1]
        s.op("dve", lambda e: e.tensor_tensor(out=shape_fn(r), in0=src_ap, in1=shape_fn(parts[0]), op=ALU.subtract),
             reads=[src_tl, parts[0]], writes=[r])
        s.op("dve", lambda e: e.tensor_copy(out=shape_fn(parts[1]), in_=shape_fn(r)), reads=[r], writes=[parts[1]])
        s.op("dve", lambda e: e.tensor_tensor(out=shape_fn(r), in0=shape_fn(r), in1=shape_fn(parts[1]), op=ALU.subtract),
             reads=[r, parts[1]], writes=[r])
        s.op("dve", lambda e: e.tensor_copy(out=shape_fn(parts[2]), in_=shape_fn(r)), reads=[r], writes=[parts[2]])


def transpose_f32(cx, c, TW, src_ap, src_tl, R, C, dst_ap, dst_tl):
    s = cx.s
    parts = TW["parts"]
    sf = lambda t: t.ap[:R, :C]
    split_bf(cx, src_ap, src_tl, parts, sf, nterms=3)
    b = cx.bank()
    bv = b.ap.bitcast(BF16)
    for i in range(3):
        s.op("pe", lambda e, i=i: e.transpose(out=bv[:C, i * 128:i * 128 + R], in_=parts[i].ap[:R, :C], identity=c["ident"].ap[:R, :R]),
             reads=[parts[i], c["ident"]], writes=[b])
    s.op("act", lambda e: e.copy(out=dst_ap, in_=bv[:C, 256:256 + R]), reads=[b], writes=[dst_tl])
    s.op("dve", lambda e: e.tensor_tensor(out=dst_ap, in0=bv[:C, 128:128 + R], in1=dst_ap, op=ALU.add), reads=[b, dst_tl], writes=[dst_tl])
    s.op("dve", lambda e: e.tensor_tensor(out=dst_ap, in0=bv[:C, 0:R], in1=dst_ap, op=ALU.add), reads=[b, dst_tl], writes=[dst_tl])


def ssd_consts(cx, es, c):
    s = cx.s
    di = cx.sb(es, "c_di", [8, 8, 128], I32)
    s.op("pool", lambda e: e.iota(di.ap, pattern=[[1, 8], [0, 128]], base=0, channel_multiplier=-1), writes=[di])
    df = cx.sb(es, "c_df", [8, 8, 128], F32)
    s.op("dve", lambda e: e.tensor_copy(out=df.ap, in_=di.ap), reads=[di], writes=[df])
    c["Delta"] = cx.sb(es, "c_Delta", [8, 8, 128], F32)
    s.op("dve", lambda e: e.tensor_single_scalar(out=c["Delta"].ap, in_=df.ap, scalar=0.0, op=ALU.is_equal), reads=[df], writes=[c["Delta"]])
    c["Deltab"] = cx.sb(es, "c_Deltab", [8, 8, 128], BF16)
    s.op("dve", lambda e: e.tensor_copy(out=c["Deltab"].ap, in_=c["Delta"].ap), reads=[c["Delta"]], writes=[c["Deltab"]])
    c["Ub"] = cx.sb(es, "c_Ub", [128, 128], BF16)
    s.op("dve", lambda e: e.tensor_copy(out=c["Ub"].ap, in_=c["U"].ap), reads=[c["U"]], writes=[c["Ub"]])
    for T in (128, 16):
        t = cx.sb(es, "c_sel%d" % T, [128, 128], BF16)
        s.op("dve", lambda e, t=t, T=T: e.tensor_scalar(out=t.ap, in0=c["onesf"].ap, scalar1=c["pf"].ap, scalar2=float(T - 1), op0=ALU.mult, op1=ALU.is_equal),
             reads=[c["onesf"], c["pf"]], writes=[t])
        c["sel%d" % T] = t


def ssd_tile(cx, c, W, T, x_tm, B_tm, BT_ap, CT_ap, bc_tl, dt_sb, zs, S, S_bf, yn_dram):
    s = cx.s
    Ab, Dsk = W["Ab"], W["Dsk"]
    dtA, acs_sb, eacs, acsT_sb = W["dtA"], W["acs_sb"], W["eacs"], W["acsT_sb"]
    s.op("dve", lambda e: e.tensor_tensor(out=dtA.ap[:T, :], in0=dt_sb.ap[:T, :], in1=Ab.ap[:T, :], op=ALU.mult), reads=[dt_sb, Ab], writes=[dtA])
    dth, dtl = W["dtA_h"], W["dtA_l"]
    split_bf(cx, dtA.ap[:T, :], dtA, [dth, dtl], lambda t: t.ap[:T, :])
    b_acs = cx.bank()
    s.op("pe", lambda e: e.matmul(b_acs.ap[:T, :8], lhsT=c["Ub"].ap[:T, :T], rhs=dth.ap[:T, :], start=True, stop=False), reads=[c["Ub"], dth], writes=[b_acs])
    s.op("pe", lambda e: e.matmul(b_acs.ap[:T, :8], lhsT=c["Ub"].ap[:T, :T], rhs=dtl.ap[:T, :], start=False, stop=True), reads=[c["Ub"], dtl], writes=[b_acs])
    s.op("dve", lambda e: e.tensor_copy(out=acs_sb.ap[:T, :], in_=b_acs.ap[:T, :8]), reads=[b_acs], writes=[acs_sb])
    s.op("act", lambda e: e.activation(out=eacs.ap[:T, :], in_=b_acs.ap[:T, :8], func=AF.Exp), reads=[b_acs], writes=[eacs])
    if SUB < 1:
        return
    b_at = cx.bank()
    s.op("pe", lambda e: e.matmul(b_at.ap[:8, :T], lhsT=dth.ap[:T, :], rhs=c["Ub"].ap[:T, :T], start=True, stop=False), reads=[c["Ub"], dth], writes=[b_at])
    s.op("pe", lambda e: e.matmul(b_at.ap[:8, :T], lhsT=dtl.ap[:T, :], rhs=c["Ub"].ap[:T, :T], start=False, stop=True), reads=[c["Ub"], dtl], writes=[b_at])
    s.op("dve", lambda e: e.tensor_copy(out=acsT_sb.ap[:, :T], in_=b_at.ap[:8, :T]), reads=[b_at], writes=[acsT_sb])
    ath, atl, nth, ntl = W["acsT_h"], W["acsT_l"], W["nacsT_h"], W["nacsT_l"]
    split_bf(cx, acsT_sb.ap[:, :T], acsT_sb, [ath, atl], lambda t: t.ap[:, :T])
    s.op("dve", lambda e: e.tensor_scalar(out=nth.ap[:, :T], in0=ath.ap[:, :T], scalar1=-1.0, scalar2=None, op0=ALU.mult), reads=[ath], writes=[nth])
    s.op("dve", lambda e: e.tensor_scalar(out=ntl.ap[:, :T], in0=atl.ap[:, :T], scalar1=-1.0, scalar2=None, op0=ALU.mult), reads=[atl], writes=[ntl])
    if SUB < 2:
        return
    Dmh, Dml = W["Dm_h"], W["Dm_l"]
    for (dm, at_) in ((Dmh, ath), (Dml, atl)):
        s.op("dve", lambda e, dm=dm, at_=at_: e.tensor_tensor(out=dm.ap[:, :8 * T].rearrange("h (g i) -> h g i", g=8), in0=c["Deltab"].ap[:, :, :T],
                                                              in1=at_.ap[:, :T].unsqueeze(1).to_broadcast([8, 8, T]), op=ALU.mult),
             reads=[c["Deltab"], at_], writes=[dm])
    if SUB < 3:
        return
    b_cb = cx.bank()
    s.op("pe", lambda e: e.matmul(b_cb.ap[:T, :T], lhsT=BT_ap, rhs=CT_ap, start=True, stop=True), reads=[bc_tl], writes=[b_cb])
    cbU = W["cbU"]
    s.op("dve", lambda e: e.tensor_tensor(out=cbU.ap[:T, :T], in0=b_cb.ap[:T, :T], in1=c["U"].ap[:T, :T], op=ALU.mult), reads=[b_cb, c["U"]], writes=[cbU])
    if SUB < 4:
        return
    Em, Ee, wts = W["Em"], W["Ee"], W["wts"]
    hp = 4 if 8 * T > 512 else 8
    for h0 in range(0, 8, hp):
        b_sg = cx.bank()
        ncol = hp * T
        s.op("pe", lambda e: e.matmul(b_sg.ap[:T, :ncol], lhsT=c["onesb"].ap[:8, :T], rhs=Dmh.ap[:, h0 * T:(h0 + hp) * T], start=True, stop=False),
             reads=[c["onesb"], Dmh], writes=[b_sg])
        s.op("pe", lambda e: e.matmul(b_sg.ap[:T, :ncol], lhsT=c["onesb"].ap[:8, :T], rhs=Dml.ap[:, h0 * T:(h0 + hp) * T], start=False, stop=False),
             reads=[c["onesb"], Dml], writes=[b_sg])
        s.op("pe", lambda e: e.matmul(b_sg.ap[:T, :ncol], lhsT=nth.ap[:, :T], rhs=c["Deltab"].ap[:, h0:h0 + hp, :T], start=False, stop=False),
             reads=[nth, c["Deltab"]], writes=[b_sg])
        s.op("pe", lambda e: e.matmul(b_sg.ap[:T, :ncol], lhsT=ntl.ap[:, :T], rhs=c["Deltab"].ap[:, h0:h0 + hp, :T], start=False, stop=True),
             reads=[ntl, c["Deltab"]], writes=[b_sg])
        s.op("dve", lambda e: e.tensor_scalar(out=Em.ap[:T, :ncol], in0=b_sg.ap[:T, :ncol], scalar1=0.0, scalar2=None, op0=ALU.min), reads=[b_sg], writes=[Em])
        s.op("act", lambda e: e.activation(out=Ee.ap[:T, :ncol], in_=Em.ap[:T, :ncol], func=AF.Exp), reads=[Em], writes=[Ee])
        s.op("dve", lambda e: e.tensor_tensor(out=wts.ap[:T, h0 * T:(h0 + hp) * T].rearrange("p (g i) -> p g i", g=hp),
                                              in0=Ee.ap[:T, :ncol].rearrange("p (g i) -> p g i", g=hp),
                                              in1=cbU.ap[:T, :T].unsqueeze(1).to_broadcast([T, hp, T]), op=ALU.mult), reads=[Ee, cbU], writes=[wts])
    if SUB < 5:
        return
    xdt, xdtt = W["xdt"], W["xdtt"]
    s.op("dve", lambda e: e.tensor_tensor(out=xdt.ap[:T, :].rearrange("p (h d) -> p h d", h=8), in0=x_tm.ap[:T, :].rearrange("p (h d) -> p h d", h=8),
                                          in1=dt_sb.ap[:T, :].unsqueeze(2).to_broadcast([T, 8, 64]), op=ALU.mult), reads=[x_tm, dt_sb], writes=[xdt])
    b_yd = cx.bank()
    for h in range(8):
        s.op("pe", lambda e, h=h: e.matmul(b_yd.ap[:T, h * 64:(h + 1) * 64], lhsT=wts.ap[:T, h * T:(h + 1) * T], rhs=xdt.ap[:T, h * 64:(h + 1) * 64],
                                           start=True, stop=True), reads=[wts, xdt], writes=[b_yd])
    if SUB < 6:
        return
    b_yo = cx.bank()
    s.op("pe", lambda e: e.matmul(b_yo.ap[:T, :], lhsT=CT_ap, rhs=S_bf.ap, start=True, stop=True), reads=[bc_tl, S_bf], writes=[b_yo])
    if SUB < 7:
        return
    t1, t2 = W["t1"], W["t2"]
    v3 = lambda ap: ap.rearrange("p (h d) -> p h d", h=8)
    s.op("dve", lambda e: e.tensor_tensor(out=v3(t1.ap[:T, :]), in0=v3(b_yo.ap[:T, :]), in1=eacs.ap[:T, :].unsqueeze(2).to_broadcast([T, 8, 64]), op=ALU.mult),
         reads=[b_yo, eacs], writes=[t1])
    s.op("dve", lambda e: e.tensor_tensor(out=t2.ap[:T, :], in0=b_yd.ap[:T, :], in1=t1.ap[:T, :], op=ALU.add), reads=[b_yd, t1], writes=[t2])
    s.op("dve", lambda e: e.tensor_tensor(out=v3(t1.ap[:T, :]), in0=v3(x_tm.ap[:T, :]), in1=Dsk.ap[:T, :].unsqueeze(2).to_broadcast([T, 8, 64]), op=ALU.mult),
         reads=[x_tm, Dsk], writes=[t1])
    s.op("dve", lambda e: e.tensor_tensor(out=t2.ap[:T, :], in0=t2.ap[:T, :], in1=t1.ap[:T, :], op=ALU.add), reads=[t2, t1], writes=[t2])
    s.op("dve", lambda e: e.tensor_tensor(out=t2.ap[:T, :], in0=t2.ap[:T, :], in1=zs.ap[:T, :], op=ALU.mult), reads=[t2, zs], writes=[t2])
    if SUB < 8:
        return
    rms_rstd(cx, c, t2.ap[:T, :], t2, T, 512, W["junkf"], W["ssq"], W["rstd"])
    yn = W["yn"][W["ynrot"][0] % 2]
    W["ynrot"][0] += 1
    s.op("dve", lambda e: e.tensor_scalar(out=yn.ap[:T, :], in0=t2.ap[:T, :], scalar1=W["rstd"].ap[:T, :], scalar2=None, op0=ALU.mult), reads=[t2, W["rstd"]], writes=[yn])
    Yd, NBv, tok0 = yn_dram
    bt_ = cx.bank()
    btv = bt_.ap.bitcast(BF16)
    for j in range(4):
        s.op("pe", lambda e, j=j: e.transpose(out=btv[:, j * 128:j * 128 + T], in_=yn.ap[:T, j * 128:(j + 1) * 128], identity=c["ident"].ap[:T, :T]),
             reads=[yn, c["ident"]], writes=[bt_])
    ynT = W["ynT"][W["ynrot"][0] % 2]
    evac(cx, ynT.ap[:, :, :T], ynT, btv[:, :512].rearrange("p (j t) -> p j t", j=4)[:, :, :T], bt_)
    tb, off = tok0 // 256, tok0 % 256
    s.dma("pool", Yd.rearrange("(k n) p t -> p k n t", k=4)[:, :, tb, off:off + T], ynT.ap[:, :, :T], reads=[ynT])
    if SUB < 9:
        return
    b_al = cx.bank()
    ach, acl = W["acs_h"], W["acs_l"]
    split_bf(cx, acs_sb.ap[:T, :], acs_sb, [ach, acl], lambda t: t.ap[:T, :])
    s.op("pe", lambda e: e.matmul(b_al.ap[:, :8], lhsT=c["sel%d" % T].ap[:T, :], rhs=ach.ap[:T, :], start=True, stop=False), reads=[c["sel%d" % T], ach], writes=[b_al])
    s.op("pe", lambda e: e.matmul(b_al.ap[:, :8], lhsT=c["sel%d" % T].ap[:T, :], rhs=acl.ap[:T, :], start=False, stop=True), reads=[c["sel%d" % T], acl], writes=[b_al])
    cd, tl = W["cd"], W["tl"]
    s.op("act", lambda e: e.activation(out=cd.ap, in_=b_al.ap[:, :8], func=AF.Exp), reads=[b_al], writes=[cd])
    s.op("dve", lambda e: e.tensor_tensor(out=tl.ap[:T, :], in0=b_al.ap[:T, :8], in1=acs_sb.ap[:T, :], op=ALU.subtract), reads=[b_al, acs_sb], writes=[tl])
    s.op("act", lambda e: e.activation(out=tl.ap[:T, :], in_=tl.ap[:T, :], func=AF.Exp), reads=[tl], writes=[tl])
    s.op("dve", lambda e: e.tensor_tensor(out=v3(xdtt.ap[:T, :]), in0=v3(xdt.ap[:T, :]), in1=tl.ap[:T, :].unsqueeze(2).to_broadcast([T, 8, 64]), op=ALU.mult),
         reads=[xdt, tl], writes=[xdtt])
    if SUB < 10:
        return
    b_st = cx.bank()
    s.op("pe", lambda e: e.matmul(b_st.ap, lhsT=B_tm.ap[:T, :], rhs=xdtt.ap[:T, :], start=True, stop=True), reads=[B_tm, xdtt], writes=[b_st])
    s.op("dve", lambda e: e.tensor_tensor(out=v3(S.ap), in0=v3(S.ap), in1=cd.ap.unsqueeze(2).to_broadcast([128, 8, 64]), op=ALU.mult), reads=[S, cd], writes=[S])
    s.op("dve", lambda e: e.tensor_tensor(out=S.ap, in0=S.ap, in1=b_st.ap, op=ALU.add), reads=[S, b_st], writes=[S])
    s.op("act", lambda e: e.copy(out=S_bf.ap, in_=S.ap), reads=[S], writes=[S_bf])


def pass_s(cx, c, SEQ, DB, x_all, w_s, nrm, cw_d, cb_d, dtb_d, alog_d, dsk_d, sconv_d, sssm_d, Yn, conv_o, ssm_o):
    s = cx.s
    NTOK = SEQ + DB * 16
    with ExitStack() as es:
        ssd_consts(cx, es, c)
        nrm_t = cx.sb(es, "s_nrm", [128, KC], F32)
        s.dma("sp", nrm_t.ap, nrm, writes=[nrm_t])
        ws = cx.sb(es, "ws_pre", [128, KC, 1288], BF16)
        with ExitStack() as es_st:
            stage = cx.sb(es_st, "s_stage", [128, 1288], F32)
            for kc in range(KC):
                s.dma("sp", stage.ap, w_s[:, kc, :], writes=[stage])
                s.op("act", lambda e, kc=kc: e.activation(out=ws.ap[:, kc, :], in_=stage.ap, func=AF.Copy, scale=nrm_t.ap[:, kc:kc + 1]),
                     reads=[stage, nrm_t], writes=[ws])
        s.barrier()
        cw = cx.sb(es, "s_cw", [128, 6, 4], F32)
        s.dma("sp", cw.ap, cw_d, writes=[cw])
        cb = cx.sb(es, "s_cb", [128, 6], F32)
        s.dma("sp", cb.ap, cb_d, writes=[cb])
        W = {}
        dtb = cx.sb(es, "s_dtb", [128, 8], F32)
        s.dma("sp", dtb.ap, dtb_d[0:1, :].to_broadcast([128, 8]), writes=[dtb])
        W["Ab"] = cx.sb(es, "s_Ab", [128, 8], F32)
        s.dma("sp", W["Ab"].ap, alog_d[0:1, :].to_broadcast([128, 8]), writes=[W["Ab"]])
        s.op("act", lambda e: e.activation(out=W["Ab"].ap, in_=W["Ab"].ap, func=AF.Exp), reads=[W["Ab"]], writes=[W["Ab"]])
        s.op("dve", lambda e: e.tensor_scalar(out=W["Ab"].ap, in0=W["Ab"].ap, scalar1=-1.0, scalar2=None, op0=ALU.mult), reads=[W["Ab"]], writes=[W["Ab"]])
        W["Dsk"] = cx.sb(es, "s_Dsk", [128, 8], F32)
        s.dma("sp", W["Dsk"].ap, dsk_d[0:1, :].to_broadcast([128, 8]), writes=[W["Dsk"]])
        W2 = [dict(W), dict(W)]
        for nm, shp, dt_ in (("dtA", [128, 8], F32), ("acs_sb", [128, 8], F32), ("eacs", [128, 8], F32), ("acsT_sb", [8, 128], F32),
                             ("cbU", [128, 128], F32), ("Em", [128, 512], F32),
                             ("Ee", [128, 512], F32), ("wts", [128, 1024], BF16), ("xdt", [128, 512], BF16), ("xdtt", [128, 512], BF16),
                             ("t1", [128, 512], F32), ("t2", [128, 512], F32), ("junkf", [128, 512], F32), ("ssq", [128, 1], F32),
                             ("rstd", [128, 1], F32), ("cd", [128, 8], F32), ("tl", [128, 8], F32),
                             ("dtA_h", [128, 8], BF16), ("dtA_l", [128, 8], BF16), ("acs_h", [128, 8], BF16), ("acs_l", [128, 8], BF16),
                             ("acsT_h", [8, 128], BF16), ("acsT_l", [8, 128], BF16), ("nacsT_h", [8, 128], BF16), ("nacsT_l", [8, 128], BF16),
                             ("Dm_h", [8, 1024], BF16), ("Dm_l", [8, 1024], BF16)):
            for wi in range(2):
                W2[wi][nm] = cx.sb(es, "w%d_" % wi + nm, shp, dt_)
        yn_l = [cx.sb(es, "w_yn%d" % i, [128, 512], BF16) for i in range(2)]
        ynT_l = [cx.sb(es, "w_ynT%d" % i, [128, 4, 128], BF16) for i in range(2)]
        rot = [0]
        for wi in range(2):
            W2[wi]["yn"] = yn_l
            W2[wi]["ynT"] = ynT_l
            W2[wi]["ynrot"] = rot
        tsrot = [0]
        TW = {"parts": [cx.sb(es, "tw_p%d" % i, [128, 128], BF16) for i in range(3)] + [cx.sb(es, "tw_r", [128, 128], F32)]}
        xt = [cx.sb(es, "s_xt%d" % i, [128, D], F32) for i in range(2)]
        xs4 = [cx.sb(es, "s_xs%d" % i, [128, D], BF16) for i in range(2)] * 2
        junk = cx.sb(es, "s_junk", [128, D], BF16)
        hT2 = [cx.sb(es, "s_hT0", [128, KC, 512], BF16)] * 2
        hTc = [hT2[0]]
        ssq = cx.sb(es, "s_ssq", [128, 1], F32)
        rstd = cx.sb(es, "s_rstd", [128, 1], F32)
        xp = [cx.sb(es, "s_xp%d" % i, [128, 6, 515], F32) for i in range(2)]
        acc = cx.sb(es, "s_acc", [128, 512], F32)
        xcT = cx.sb(es, "s_xcT", [128, 6, 512], BF16)
        x_tm2 = [cx.sb(es, "s_xtm%d" % i, [128, 512], BF16) for i in range(2)]
        B_tm2 = [cx.sb(es, "s_Btm%d" % i, [128, 128], BF16) for i in range(2)]
        zs2 = [cx.sb(es, "s_zs%d" % i, [128, 512], F32) for i in range(2)]
        dt_sb2 = [cx.sb(es, "s_dt%d" % i, [128, 8], F32) for i in range(2)]
        S = cx.sb(es, "s_S", [128, 512], F32)
        S_bf = cx.sb(es, "s_Sbf", [128, 512], BF16)
        so = cx.sb(es, "s_so", [128, 4, 128], F32)
        sc = cx.sb(es, "s_sc", [128, 768], F32)
        cso = cx.sb(es, "s_cso", [128, 768], F32)
        csi = cx.sb(es, "s_csi", [128, 6, 48], F32)

        def proj_fm(ST):
            pass

        def conv_chunk(xv, ch, ncols_view, out_ap_fn):
            a = out_ap_fn("acc")
            s.op("dve", lambda e: e.tensor_scalar(out=a, in0=xv(0), scalar1=cw.ap[:, ch, 0:1], scalar2=None, op0=ALU.mult), reads=[xv.tl, cw], writes=[acc])
            for k in range(1, 4):
                s.op("dve", lambda e, k=k: e.scalar_tensor_tensor(out=a, in0=xv(k), scalar=cw.ap[:, ch, k:k + 1], in1=a, op0=ALU.mult, op1=ALU.add),
                     reads=[xv.tl, cw, acc], writes=[acc])
            s.op("act", lambda e: e.activation(out=out_ap_fn("out"), in_=a, func=AF.Silu, bias=cb.ap[:, ch:ch + 1]), reads=[acc, cb], writes=[xcT])

        def token_side(col0, T, yn_rows):
            pi_ = tsrot[0] % 2
            tsrot[0] += 1
            W, x_tm, B_tm, zs, dt_sb, hT = W2[pi_], x_tm2[pi_], B_tm2[pi_], zs2[pi_], dt_sb2[pi_], hTc[0]
            hs = lambda kc: hT.ap[:, kc, col0:col0 + T]
            bz = cx.bank()
            mm_acc(cx, bz.ap[:T, :], [(hs(kc), ws.ap[:, kc, 0:512]) for kc in range(KC)], [hT, ws], bz)
            s.op("act", lambda e: e.activation(out=zs.ap[:T, :], in_=bz.ap[:T, :], func=AF.Silu), reads=[bz], writes=[zs])
            bd = cx.bank()
            mm_acc(cx, bd.ap[:T, :8], [(hs(kc), ws.ap[:, kc, 1280:1288]) for kc in range(KC)], [hT, ws], bd)
            s.op("dve", lambda e: e.tensor_tensor(out=dt_sb.ap[:T, :], in0=bd.ap[:T, :8], in1=dtb.ap[:T, :], op=ALU.add), reads=[bd, dtb], writes=[dt_sb])
            s.op("act", lambda e: e.activation(out=dt_sb.ap[:T, :], in_=dt_sb.ap[:T, :], func=AF.Exp), reads=[dt_sb], writes=[dt_sb])
            s.op("act", lambda e: e.activation(out=dt_sb.ap[:T, :], in_=dt_sb.ap[:T, :], func=AF.Ln, bias=1.0), reads=[dt_sb], writes=[dt_sb])
            b = cx.bank()
            bv = b.ap.bitcast(BF16)
            for j in range(5):
                s.op("pe", lambda e, j=j: e.transpose(out=bv[:T, j * 128:(j + 1) * 128], in_=xcT.ap[:, j, col0:col0 + T], identity=c["ident"].ap),
                     reads=[xcT, c["ident"]], writes=[b])
            evac(cx, x_tm.ap[:T, :], x_tm, bv[:T, 0:512], b)
            evac(cx, B_tm.ap[:T, :], B_tm, bv[:T, 512:640], b)
            if STOP < 4:
                return
            ssd_tile(cx, c, W, T, x_tm, B_tm, xcT.ap[:, 4, col0:col0 + T], xcT.ap[:, 5, col0:col0 + T], xcT, dt_sb, zs, S, S_bf, yn_rows)

        def state_out(idx):
            for j in range(4):
                transpose_f32(cx, c, TW, S.ap[:, j * 128:(j + 1) * 128], S, 128, 128, so.ap[:, j, :], so)
            s.dma("pool", ssm_o[idx].rearrange("(j p) n -> p j n", p=128), so.ap, reads=[so])

        if STOP < 2:
            return
        s.op("dve", lambda e: e.memset(S.ap, 0.0), writes=[S])
        s.op("dve", lambda e: e.memset(S_bf.ap, 0.0), writes=[S_bf])
        s.op("dve", lambda e: e.memset(xp[1].ap[:, :, 0:515], 0.0), writes=[xp[1]])
        nstp = SEQ // 512
        for st in range(nstp):
            T0 = st * 512
            cur, prv = xp[st % 2], xp[(st + 1) % 2]
            hT = hT2[st % 2]
            hTc[0] = hT
            for sub in range(4):
                load_norm_T(cx, c, x_all[T0 + sub * 128:T0 + (sub + 1) * 128, :], 128, xt[sub % 2], xs4[sub], junk, hT, sub * 128, (ssq, rstd))
            s.op("dve", lambda e: e.tensor_copy(out=cur.ap[:, :, 0:3], in_=prv.ap[:, :, 512:515]), reads=[prv], writes=[cur])
            for ch in range(6):
                b = cx.bank()
                mm_acc(cx, b.ap, [(ws.ap[:, kc, 512 + ch * 128:512 + (ch + 1) * 128], hT.ap[:, kc, :]) for kc in range(KC)], [ws, hT], b)
                evac(cx, cur.ap[:, ch, 3:515], cur, b.ap, b)
            for ch in range(6):
                xv = lambda k, ch=ch: cur.ap[:, ch, k:k + 512]
                xv.tl = cur
                conv_chunk(xv, ch, 512, lambda w, ch=ch: acc.ap if w == "acc" else xcT.ap[:, ch, :])
            if STOP < 3:
                continue
            for sub in range(4):
                token_side(sub * 128, 128, (Yn, NTOK // 256, T0 + sub * 128))
        last = xp[(nstp - 1) % 2]
        for ch in range(6):
            s.dma("pool", conv_o[0][:, ch * 128:(ch + 1) * 128].rearrange("k p -> p k"), last.ap[:, ch, 512:515], reads=[last], allow_slow_non_contiguous=True)
        state_out(0)
        if STOP < 6:
            return
        NS = DB * 16
        nsub = (NS + 127) // 128
        hT = hT2[0]
        hTc[0] = hT
        for sub in range(nsub):
            nt = min(128, NS - sub * 128)
            load_norm_T(cx, c, x_all[SEQ + sub * 128:SEQ + sub * 128 + nt, :], nt, xt[sub % 2], xs4[sub], junk, hT, sub * 128, (ssq, rstd))
        xq = xp[0]
        xq4 = xq.ap[:, :, 0:DB * 19].rearrange("p c (b t) -> p c b t", t=19)
        s.dma("sp", sc.ap[:DB * 3, :], sconv_d.rearrange("b k f -> (b k) f"), writes=[sc])
        for ch in range(6):
            transpose_f32(cx, c, TW, sc.ap[:DB * 3, ch * 128:(ch + 1) * 128], sc, DB * 3, 128, csi.ap[:, ch, :DB * 3], csi)
            evac(cx, xq4[:, ch, :, 0:3], xq, csi.ap[:, ch, :DB * 3].rearrange("p (b k) -> p b k", k=3), csi)
        for ch in range(6):
            b = cx.bank()
            mm_acc(cx, b.ap[:, :NS], [(ws.ap[:, kc, 512 + ch * 128:512 + (ch + 1) * 128], hT.ap[:, kc, :NS]) for kc in range(KC)], [ws, hT], b)
            evac(cx, xq4[:, ch, :, 3:19], xq, b.ap[:, :NS].rearrange("p (b t) -> p b t", t=16), b)
        for ch in range(6):
            xv = lambda k, ch=ch: xq4[:, ch, :, k:k + 16]
            xv.tl = xq
            conv_chunk(xv, ch, NS, lambda w, ch=ch: (acc.ap[:, :NS].rearrange("p (b t) -> p b t", t=16) if w == "acc"
                                                      else xcT.ap[:, ch, :NS].rearrange("p (b t) -> p b t", t=16)))
        for ch in range(6):
            evac(cx, csi.ap[:, ch, :DB * 3].rearrange("p (b k) -> p b k", k=3), csi, xq4[:, ch, :, 16:19], xq)
            transpose_f32(cx, c, TW, csi.ap[:, ch, :DB * 3], csi, 128, DB * 3, cso.ap[:DB * 3, ch * 128:(ch + 1) * 128], cso)
        s.dma("pool", conv_o[1:1 + DB].rearrange("b k f -> (b k) f"), cso.ap[:DB * 3, :], reads=[cso])
        for b_ in range(DB):
            s.dma("sp", so.ap, sssm_d[b_].rearrange("(j p) n -> p j n", p=128), writes=[so])
            for j in range(4):
                transpose_f32(cx, c, TW, so.ap[:, j, :], so, 128, 128, S.ap[:, j * 128:(j + 1) * 128], S)
            s.op("act", lambda e: e.copy(out=S_bf.ap, in_=S.ap), reads=[S], writes=[S_bf])
            token_side(b_ * 16, 16, (Yn, NTOK // 256, SEQ + b_ * 16))
            state_out(1 + b_)


def build_l1(SEQ, DB, PAST, fused=False):
    NTOK = SEQ + DB * 16
    nc = bass.Bass("TRN2", target_bir_lowering=False)
    x_all = nc.dram_tensor("x_all", [NTOK, D], F32, kind="ExternalInput").ap()
    w_m = nc.dram_tensor("w_m", [128, KC, 1344], F32, kind="ExternalInput").ap()
    w_q = nc.dram_tensor("w_q", [128, 4, 384], F32, kind="ExternalInput").ap()
    w_kv = nc.dram_tensor("w_kv", [128, 4, 512], F32, kind="ExternalInput").ap()
    nrm = nc.dram_tensor("nrm", [128, KC], F32, kind="ExternalInput").ap()
    qnw = nc.dram_tensor("qnw", [128, 4], F32, kind="ExternalInput").ap()
    kvw = nc.dram_tensor("kvw", [1, 512], F32, kind="ExternalInput").ap()
    cache_kv = nc.dram_tensor("cache_kv", [DB, PAST, 512], F32, kind="ExternalInput").ap()
    cache_kr = nc.dram_tensor("cache_kr", [DB, PAST, 64], F32, kind="ExternalInput").ap()
    kvlat = nc.dram_tensor("kvlat", [NTOK, 512], F32, kind="ExternalOutput").ap()
    krope = nc.dram_tensor("krope", [NTOK, 64], F32, kind="ExternalOutput").ap()
    NB = NTOK // 256
    AT = (nc.dram_tensor("ATc", [2 * NB, 128, 256], BF16).ap() if fused
          else nc.dram_tensor("ATc", [2 * NB, 128, 256], BF16, kind="ExternalOutput").ap())
    w_s = nc.dram_tensor("w_s", [128, KC, 1288], F32, kind="ExternalInput").ap()
    cw_d = nc.dram_tensor("cw", [128, 6, 4], F32, kind="ExternalInput").ap()
    cb_d = nc.dram_tensor("cb", [128, 6], F32, kind="ExternalInput").ap()
    dtb_d = nc.dram_tensor("dtb", [1, 8], F32, kind="ExternalInput").ap()
    alog_d = nc.dram_tensor("alog", [1, 8], F32, kind="ExternalInput").ap()
    dsk_d = nc.dram_tensor("dsk", [1, 8], F32, kind="ExternalInput").ap()
    sconv_d = nc.dram_tensor("sconv", [DB, 3, 768], F32, kind="ExternalInput").ap()
    sssm_d = nc.dram_tensor("sssm", [DB, 512, 128], F32, kind="ExternalInput").ap()
    Yn = (nc.dram_tensor("YTc", [4 * NB, 128, 256], BF16).ap() if fused
          else nc.dram_tensor("YTc", [4 * NB, 128, 256], BF16, kind="ExternalOutput").ap())
    if fused:
        GY = nc.dram_tensor("GY", [4 * NB, 1024, 256], BF16).ap()
        GA = nc.dram_tensor("GA", [2 * NB, 1024, 256], BF16).ap()
        y2 = nc.dram_tensor("y2", [SEQ // NCORES + DB * 16 // NCORES, D], F32, kind="ExternalOutput").ap()
        l2w = declare_l2_weights(nc)
        l2s = declare_l2_scratch(nc)
    conv_o = nc.dram_tensor("conv_o", [1 + DB, 3, 768], F32, kind="ExternalOutput").ap()
    ssm_o = nc.dram_tensor("ssm_o", [1 + DB, 512, 128], F32, kind="ExternalOutput").ap()
    QT = nc.dram_tensor("QT", [128, 2, NTOK], BF16).ap()
    QRT = nc.dram_tensor("QRT", [64, 2, NTOK], BF16).ap()
    KT = nc.dram_tensor("KT", [128, 2, NTOK], BF16).ap()
    KRT = nc.dram_tensor("KRT", [64, NTOK], BF16).ap()
    V = nc.dram_tensor("V", [NTOK, 256], BF16).ap()
    GT = nc.dram_tensor("GT", [128, 2, NTOK], BF16).ap()
    KTc = nc.dram_tensor("KTc", [128, 2, DB * PAST], BF16).ap()
    KRTc = nc.dram_tensor("KRTc", [64, DB * PAST], BF16).ap()
    Vc = nc.dram_tensor("Vc", [DB * PAST, 256], BF16).ap()
    cx = Ctx(nc)
    s = cx.s
    SCALE = 192.0 ** -0.5
    with ExitStack() as es0:
        c = make_consts(cx, es0)
        setup_small_consts(cx, es0, c)
        with ExitStack() as es:
            nrm_t = cx.sb(es, "nrm_t", [128, KC], F32)
            s.dma("sp", nrm_t.ap, nrm, writes=[nrm_t])
            qnw_t = cx.sb(es, "qnw_t", [128, 4], F32)
            s.dma("sp", qnw_t.ap, qnw, writes=[qnw_t])
            s.op("dve", lambda e: e.tensor_scalar(out=qnw_t.ap, in0=qnw_t.ap, scalar1=SCALE, scalar2=None, op0=ALU.mult), reads=[qnw_t], writes=[qnw_t])
            kvw_b = cx.sb(es, "kvw_b", [128, 512], F32)
            s.dma("sp", kvw_b.ap, kvw[0:1, :].to_broadcast([128, 512]), writes=[kvw_b])
            stage = cx.sb(es, "stage", [128, 1344], F32)
            wm = load_weights_bf16(cx, es, "wm", w_m, KC, 1344, scale_tl=nrm_t, stage=stage)
            wq = load_weights_bf16(cx, es, "wq", w_q, 4, 384, scale_tl=qnw_t, stage=stage)
            wkv = load_weights_bf16(cx, es, "wkv", w_kv, 4, 512, stage=stage)
            xt = [cx.sb(es, "xt%d" % i, [128, D], F32) for i in range(2)]
            xs4 = [cx.sb(es, "xs%d" % i, [128, D], BF16) for i in range(4)]
            junk = cx.sb(es, "junk", [128, D], BF16)
            junkf = cx.sb(es, "junkf", [128, 512], F32)
            hT2 = [cx.sb(es, "hT%d" % i, [128, KC, 512], BF16) for i in range(2)]
            ssq = cx.sb(es, "ssq", [128, 1], F32)
            rstd = cx.sb(es, "rstd", [128, 1], F32)
            ssq2 = cx.sb(es, "ssq2", [128, 1], F32)
            rstd2 = cx.sb(es, "rstd2", [128, 1], F32)
            pos = cx.sb(es, "pos", [128, 4], F32)
            tabs = [cx.sb(es, "tab%d" % i, [128, 4, 32], F32 if i != 2 else I32) for i in range(6)]
            cos, sin = tabs[3], tabs[4]
            ckvn = [cx.sb(es, "ckvn%d" % i, [128, 512], F32) for i in range(2)]
            ckvb = cx.sb(es, "ckvb", [128, 512], BF16)
            cqnb = cx.sb(es, "cqnb", [128, 512], BF16)
            kro = [cx.sb(es, "kro%d" % i, [128, 64], F32) for i in range(2)]
            krt = cx.sb(es, "krt", [128, 128], F32)
            krb = cx.sb(es, "krb", [128, 64], BF16)
            qro = cx.sb(es, "qro", [128, 128], F32)
            qrb = cx.sb(es, "qrb", [128, 128], BF16)
            cqnT = cx.sb(es, "cqnT", [128, 4, 512], BF16)
            ckvT = cx.sb(es, "ckvT", [128, 4, 512], BF16)
            krT = cx.sb(es, "krT", [64, 512], BF16)
            qrT = cx.sb(es, "qrT", [64, 2, 512], BF16)
            qst = cx.sb(es, "qst", [128, 2, 512], BF16)
            kst = cx.sb(es, "kst", [128, 2, 512], BF16)
            gst = cx.sb(es, "gst", [128, 2, 512], BF16)
            vst = cx.sb(es, "vst", [128, 4, 256], BF16)
            cst = [cx.sb(es, "cst%d" % i, [128, 4, 512], F32) for i in range(2)]
            cstb = cx.sb(es, "cstb", [128, 4, 512], BF16)
            ckr = cx.sb(es, "ckr", [128, 4, 64], F32)
            ckrb = cx.sb(es, "ckrb", [128, 4, 64], BF16)
            nst = (NTOK + 511) // 512
            for st in range(nst if "M" in PH else 0):
                T0 = st * 512
                ST = min(512, NTOK - T0)
                nsub = ST // 128
                is_prompt = T0 < SEQ
                hT = hT2[st % 2]
                for sub in range(nsub):
                    load_norm_T(cx, c, x_all[T0 + sub * 128:T0 + (sub + 1) * 128, :], 128, xt[sub % 2], xs4[sub], junk, hT, sub * 128, (ssq, rstd))
                if is_prompt:
                    for sub in range(nsub):
                        s.op("dve", lambda e, sub=sub: e.tensor_scalar(out=pos.ap[:, sub:sub + 1], in0=c["pf"].ap, scalar1=float(T0 + sub * 128), scalar2=None, op0=ALU.add),
                             reads=[c["pf"]], writes=[pos])
                else:
                    for sub in range(nsub):
                        s.op("dve", lambda e, sub=sub: e.tensor_scalar(out=pos.ap[:, sub:sub + 1], in0=c["pm16"].ap, scalar1=float(PAST), scalar2=None, op0=ALU.add),
                             reads=[c["pm16"]], writes=[pos])
                rope_tables(cx, c, tabs, pos, nsub)
                for sub in range(nsub):
                    t0 = T0 + sub * 128
                    hs = lambda kc: hT.ap[:, kc, sub * 128:(sub + 1) * 128]
                    bq = cx.bank()
                    mm_acc(cx, bq.ap, [(hs(kc), wm.ap[:, kc, 0:512]) for kc in range(KC)], [hT, wm], bq)
                    rms_rstd(cx, c, bq.ap, bq, 128, 512, junkf, ssq2, rstd2)
                    s.op("dve", lambda e: e.tensor_scalar(out=cqnb.ap, in0=bq.ap, scalar1=rstd2.ap, scalar2=None, op0=ALU.mult),
                         reads=[bq, rstd2], writes=[cqnb])
                    transpose_to(cx, c, cqnb, lambda j: cqnb.ap[:, j * 128:(j + 1) * 128], 4, 128,
                                 lambda j: cqnT.ap[:, j, sub * 128:(sub + 1) * 128], cqnT)
                    b1 = cx.bank()
                    mm_acc(cx, b1.ap, [(hs(kc), wm.ap[:, kc, 512:1024]) for kc in range(KC)], [hT, wm], b1)
                    rms_rstd(cx, c, b1.ap, b1, 128, 512, junkf, ssq2, rstd2)
                    ck = ckvn[sub % 2]
                    s.op("dve", lambda e: e.scalar_tensor_tensor(out=ck.ap, in0=b1.ap, scalar=rstd2.ap, in1=kvw_b.ap, op0=ALU.mult, op1=ALU.mult),
                         reads=[b1, rstd2, kvw_b], writes=[ck])
                    s.dma("pool", kvlat[t0:t0 + 128, :], ck.ap, reads=[ck])
                    s.op("act", lambda e: e.copy(out=ckvb.ap, in_=ck.ap), reads=[ck], writes=[ckvb])
                    transpose_to(cx, c, ckvb, lambda j: ckvb.ap[:, j * 128:(j + 1) * 128], 4, 128,
                                 lambda j: ckvT.ap[:, j, sub * 128:(sub + 1) * 128], ckvT)
                    b2 = cx.bank()
                    mm_acc(cx, b2.ap[:, :64], [(hs(kc), wm.ap[:, kc, 1024:1088]) for kc in range(KC)], [hT, wm], b2)
                    ko = kro[sub % 2]
                    apply_rope(cx, b2.ap[:, :64].rearrange("p (h d) -> p h d", h=1), b2, 1, cos.ap[:, sub, :], sin.ap[:, sub, :], [cos, sin], ko, krt, 128)
                    s.dma("pool", krope[t0:t0 + 128, :], ko.ap, reads=[ko])
                    s.op("act", lambda e: e.copy(out=krb.ap, in_=ko.ap), reads=[ko], writes=[krb])
                    transpose_to(cx, c, krb, lambda j: krb.ap[:, :], 1, 128, lambda j: krT.ap[:, sub * 128:(sub + 1) * 128], krT, rows=64)
                    b3 = cx.bank()
                    mm_acc(cx, b3.ap[:, :128], [(cqnT.ap[:, kc, sub * 128:(sub + 1) * 128], wq.ap[:, kc, 256:384]) for kc in range(4)], [cqnT, wq], b3)
                    apply_rope(cx, b3.ap[:, :128].rearrange("p (h d) -> p h d", h=2), b3, 2, cos.ap[:, sub, :], sin.ap[:, sub, :], [cos, sin], qro, krt, 128)
                    s.op("act", lambda e: e.copy(out=qrb.ap, in_=qro.ap), reads=[qro], writes=[qrb])
                    transpose_to(cx, c, qrb, lambda j: qrb.ap[:, j * 64:(j + 1) * 64], 2, 128,
                                 lambda j: qrT.ap[:, j, sub * 128:(sub + 1) * 128], qrT, rows=64)
                for h in range(2):
                    b = cx.bank()
                    mm_acc(cx, b.ap[:, :ST], [(wq.ap[:, kc, h * 128:(h + 1) * 128], cqnT.ap[:, kc, :ST]) for kc in range(4)], [wq, cqnT], b)
                    evac(cx, qst.ap[:, h, :ST], qst, b.ap[:, :ST], b)
                s.dma("pool", QT[:, :, T0:T0 + ST], qst.ap[:, :, :ST], reads=[qst])
                s.dma("pool", QRT[:, :, T0:T0 + ST], qrT.ap[:, :, :ST], reads=[qrT])
                s.dma("pool", KRT[:, T0:T0 + ST], krT.ap[:, :ST], reads=[krT])
                for h in range(2):
                    b = cx.bank()
                    mm_acc(cx, b.ap[:, :ST], [(wm.ap[:, kc, 1088 + h * 128:1088 + (h + 1) * 128], hT.ap[:, kc, :ST]) for kc in range(KC)], [wm, hT], b)
                    s.op("act", lambda e, h=h, b=b: e.activation(out=gst.ap[:, h, :ST], in_=b.ap[:, :ST], func=AF.Silu), reads=[b], writes=[gst])
                s.dma("pool", GT[:, :, T0:T0 + ST], gst.ap[:, :, :ST], reads=[gst])
                kv_from_latent(cx, c, ckvT, None, wkv, ST, nsub, kst, vst, KT, V, T0)
            for b_ in range(DB if "C" in PH else 0):
                for st in range(PAST // 512):
                    r0 = st * 512
                    cs = cst[st % 2]
                    s.dma("sp", cs.ap, cache_kv[b_, r0:r0 + 512, :].rearrange("(n p) f -> p n f", p=128), writes=[cs])
                    s.op("act", lambda e, cs=cs: e.copy(out=cstb.ap, in_=cs.ap), reads=[cs], writes=[cstb])
                    for sub in range(4):
                        transpose_to(cx, c, cstb, lambda j, sub=sub: cstb.ap[:, sub, j * 128:(j + 1) * 128], 4, 128,
                                     lambda j, sub=sub: ckvT.ap[:, j, sub * 128:(sub + 1) * 128], ckvT)
                    kv_from_latent(cx, c, ckvT, None, wkv, 512, 4, kst, vst, KTc, Vc, b_ * PAST + r0)
                    s.dma("sp", ckr.ap, cache_kr[b_, r0:r0 + 512, :].rearrange("(n p) f -> p n f", p=128), writes=[ckr])
                    s.op("act", lambda e: e.copy(out=ckrb.ap, in_=ckr.ap), reads=[ckr], writes=[ckrb])
                    transpose_to(cx, c, ckrb, lambda j: ckrb.ap[:, j, :], 4, 128, lambda j: krT.ap[:, j * 128:(j + 1) * 128], krT, rows=64)
                    s.dma("pool", KRTc[:, b_ * PAST + r0:b_ * PAST + r0 + 512], krT.ap[:, :512], reads=[krT])
        s.barrier()
        with ExitStack() as es:
            if "A" in PH:
                attention_phase(cx, c, es, SEQ, DB, PAST, QT, QRT, KT, KRT, V, GT, KTc, KRTc, Vc, AT)
        s.barrier()
        if "S" in PH:
            pass_s(cx, c, SEQ, DB, x_all, w_s, nrm, cw_d, cb_d, dtb_d, alog_d, dsk_d, sconv_d, sssm_d, Yn, conv_o, ssm_o)
        s.barrier()
        if fused:
            for i in range(2 * NB):
                s.coll(AT[i], GA[i])
            for i in range(4 * NB):
                s.coll(Yn[i], GY[i])
            s.barrier()
            l2_phase(cx, c, es0, SEQ, DB, x_all, GY, GA, *l2w, y2, *l2s)
    s.emit()
    return nc


def l2_phase(cx, c, es0, SEQ, DB, x_all, GY, GA, wg, wssm, wmla, wout, nrm, snw, fnw, y2, WGS, WGM, WS, WM):
    nc, s = cx.nc, cx.s
    NS = DB * 16
    NTOK = SEQ + NS
    NB = NTOK // 256
    PB = SEQ // 256 // NCORES
    SPC = NS // NCORES
    tbs = SEQ // 256
    nrm_t = cx.sb(es0, "l2_nrm", [128, KC], F32)
    s.dma("sp", nrm_t.ap, nrm, writes=[nrm_t])
    snw_t = cx.sb(es0, "l2_snw", [128, 32], F32)
    s.dma("sp", snw_t.ap, snw, writes=[snw_t])
    wo = cx.sb(es0, "l2_wo", [128, 16, 2048], BF16)
    with ExitStack() as es:
        stg = [cx.sb(es, "stg%d" % i, [128, 32, 128], F32) for i in range(2)]
        stb = [cx.sb(es, "stb%d" % i, [128, 32, 128], BF16) for i in range(2)]
        k = 0
        for jj in range(16):
            for (src, c0, nk, scl, dst) in ((wg, jj * 128, 16, nrm_t, WGS), (wg, 2048 + jj * 128, 16, nrm_t, WGM),
                                           (wssm, jj * 128, 32, snw_t, WS), (wmla, jj * 128, 16, None, WM)):
                sg, sbb = stg[k % 2], stb[k % 2]
                k += 1
                s.dma("sp", sg.ap[:, :nk, :], src[:, :, c0:c0 + 128], writes=[sg])
                if scl is not None:
                    s.op("dve", lambda e, sg=sg, sbb=sbb, nk=nk, scl=scl: e.tensor_tensor(out=sbb.ap[:, :nk, :], in0=sg.ap[:, :nk, :],
                                                                                            in1=scl.ap[:, :nk].unsqueeze(2).to_broadcast([128, nk, 128]), op=ALU.mult),
                         reads=[sg, scl], writes=[sbb])
                else:
                    s.op("act", lambda e, sg=sg, sbb=sbb, nk=nk: e.copy(out=sbb.ap[:, :nk, :], in_=sg.ap[:, :nk, :]), reads=[sg], writes=[sbb])
                s.dma("pool", dst[jj], sbb.ap[:, :nk, :], reads=[sbb])
        for kc in range(16):
            sg = stg[kc % 2]
            s.dma("sp", sg.ap.rearrange("p a b -> p (a b)")[:, :2048], wout[:, kc, :], writes=[sg])
            s.op("act", lambda e, sg=sg, kc=kc: e.copy(out=wo.ap[:, kc, :], in_=sg.ap.rearrange("p a b -> p (a b)")[:, :2048]), reads=[sg], writes=[wo])
    s.barrier()
    with ExitStack() as es:
        fnw_b = cx.sb(es, "fnw_b", [128, D], F32)
        s.dma("sp", fnw_b.ap, fnw[0:1, :].to_broadcast([128, D]), writes=[fnw_b])
        xt = [cx.sb(es, "l2_xt0", [128, D], F32)]
        xs = cx.sb(es, "l2_xs", [128, D], BF16)
        junk = cx.sb(es, "l2_junk", [128, D], BF16)
        xr = cx.sb(es, "l2_xr", [128, D], F32)
        xo = cx.sb(es, "l2_xo", [128, D], F32)
        hT = cx.sb(es, "l2_hT", [128, KC, 256], BF16)
        yT = cx.sb(es, "l2_yT", [128, 32, 256], BF16)
        aT = cx.sb(es, "l2_aT", [128, 16, 256], BF16)
        mixT = cx.sb(es, "l2_mixT", [128, 16, 256], BF16)
        wgs = [cx.sb(es, "wgs%d" % i, [128, 16, 128], BF16) for i in range(2)]
        wgm = [cx.sb(es, "wgm%d" % i, [128, 16, 128], BF16) for i in range(2)]
        wsb = [cx.sb(es, "wsb%d" % i, [128, 32, 128], BF16) for i in range(2)]
        wmb = [cx.sb(es, "wmb%d" % i, [128, 16, 128], BF16) for i in range(2)]
        sg1 = cx.sb(es, "sg1", [128, 256], F32)
        sg2 = cx.sb(es, "sg2", [128, 256], F32)
        ssq = cx.sb(es, "l2_ssq", [128, 1], F32)
        rstd = cx.sb(es, "l2_rstd", [128, 1], F32)
        ssq2 = cx.sb(es, "l2_ssq2", [128, 1], F32)
        rstd2 = cx.sb(es, "l2_rstd2", [128, 1], F32)
        def rv(rk, key, mul, add=0):
            if key not in rk:
                rk[key] = rk["r"] * mul + add if add else rk["r"] * mul
            return rk[key]
        GYr = nc.dram_tensor("GYr", [4, PB, 1024 * 256], BF16).ap()
        GAr = nc.dram_tensor("GAr", [2, PB, 1024 * 256], BF16).ap()
        GYs = nc.dram_tensor("GYs", [4, 1024, SPC], BF16).ap()
        GAs = nc.dram_tensor("GAs", [2, 1024, SPC], BF16).ap()
        X2 = nc.dram_tensor("X2", [PB * 256 + SPC, D], F32).ap()
        s.dma("sp", GYr, lambda rk: GY.rearrange("(k n) r t -> k n (r t)", k=4)[:, bass.ds(rv(rk, "pb", PB), PB), :])
        s.dma("sp", GAr, lambda rk: GA.rearrange("(k n) r t -> k n (r t)", k=2)[:, bass.ds(rv(rk, "pb", PB), PB), :])
        s.dma("sp", GYs, lambda rk: GY.rearrange("(k n) r t -> k n r t", k=4)[:, tbs, :, bass.ds(rv(rk, "sp", SPC), SPC)])
        s.dma("sp", GAs, lambda rk: GA.rearrange("(k n) r t -> k n r t", k=2)[:, tbs, :, bass.ds(rv(rk, "sp", SPC), SPC)])
        s.dma("sp", X2[0:PB * 256, :], lambda rk: x_all[bass.ds(rv(rk, "xr", PB * 256), PB * 256), :])
        s.dma("sp", X2[PB * 256:PB * 256 + SPC, :], lambda rk: x_all[bass.ds(rv(rk, "xs", SPC, SEQ), SPC), :])
        s.barrier()
        it = 0
        items = [("p", i) for i in range(PB)] + [("s", 0)]
        for (kind, i) in items:
            if kind == "p":
                ST, subs = 256, [(0, 128), (128, 128)]
                xrow = lambda o, n, i=i: X2[i * 256 + o:i * 256 + o + n, :]
                ysrc = lambda kc, i=i: GYr[kc, i].rearrange("(r p t) -> p r t", p=128, t=256)
                asrc = lambda h, i=i: GAr[h, i].rearrange("(r p t) -> p r t", p=128, t=256)
                yrow0 = i * 256
            else:
                ST, subs = SPC, [(0, SPC)]
                xrow = lambda o, n: X2[PB * 256 + o:PB * 256 + o + n, :]
                ysrc = lambda kc: GYs[kc].rearrange("(r p) t -> p r t", p=128)
                asrc = lambda h: GAs[h].rearrange("(r p) t -> p r t", p=128)
                yrow0 = PB * 256
            for (o, n) in subs:
                load_norm_T(cx, c, xrow(o, n), n, xt[0], xs, junk, hT, o, (ssq, rstd))
            yT4 = yT.ap.rearrange("p (r k) t -> p r k t", k=4)
            for kc in range(4):
                s.dma("sp", yT4[:, :, kc, :ST], ysrc(kc), writes=[yT])
            aT4 = aT.ap.rearrange("p (r h) t -> p r h t", h=2)
            for h in range(2):
                s.dma("sp", aT4[:, :, h, :ST], asrc(h), writes=[aT])
            for jj in range(16):
                ib = it % 2
                it += 1
                s.dma("sp", wgs[ib].ap, WGS[jj], writes=[wgs[ib]])
                s.dma("sp", wgm[ib].ap, WGM[jj], writes=[wgm[ib]])
                s.dma("sp", wsb[ib].ap, WS[jj], writes=[wsb[ib]])
                s.dma("sp", wmb[ib].ap, WM[jj], writes=[wmb[ib]])
                bgs = cx.bank()
                mm_acc(cx, bgs.ap[:, :ST], [(wgs[ib].ap[:, kc, :], hT.ap[:, kc, :ST]) for kc in range(16)], [wgs[ib], hT], bgs)
                bys = cx.bank()
                mm_acc(cx, bys.ap[:, :ST], [(wsb[ib].ap[:, kc, :], yT.ap[:, kc, :ST]) for kc in range(32)], [wsb[ib], yT], bys)
                bgm = cx.bank()
                mm_acc(cx, bgm.ap[:, :ST], [(wgm[ib].ap[:, kc, :], hT.ap[:, kc, :ST]) for kc in range(16)], [wgm[ib], hT], bgm)
                bym = cx.bank()
                mm_acc(cx, bym.ap[:, :ST], [(wmb[ib].ap[:, kc, :], aT.ap[:, kc, :ST]) for kc in range(16)], [wmb[ib], aT], bym)
                s.op("act", lambda e: e.activation(out=sg1.ap[:, :ST], in_=bgs.ap[:, :ST], func=AF.Sigmoid), reads=[bgs], writes=[sg1])
                s.op("act", lambda e: e.activation(out=sg2.ap[:, :ST], in_=bgm.ap[:, :ST], func=AF.Sigmoid), reads=[bgm], writes=[sg2])
                s.op("dve", lambda e: e.tensor_tensor(out=sg1.ap[:, :ST], in0=bys.ap[:, :ST], in1=sg1.ap[:, :ST], op=ALU.mult), reads=[bys, sg1], writes=[sg1])
                s.op("dve", lambda e: e.tensor_tensor(out=sg2.ap[:, :ST], in0=bym.ap[:, :ST], in1=sg2.ap[:, :ST], op=ALU.mult), reads=[bym, sg2], writes=[sg2])
                s.op("dve", lambda e, jj=jj: e.tensor_tensor(out=mixT.ap[:, jj, :ST], in0=sg1.ap[:, :ST], in1=sg2.ap[:, :ST], op=ALU.add), reads=[sg1, sg2], writes=[mixT])
            for (o, n) in subs:
                s.dma("sp", xr.ap[:n, :], xrow(o, n), writes=[xr])
                for cg in range(4):
                    b = cx.bank()
                    mm_acc(cx, b.ap[:n, :], [(mixT.ap[:, kc, o:o + n], wo.ap[:, kc, cg * 512:(cg + 1) * 512]) for kc in range(16)], [mixT, wo], b)
                    s.op("dve", lambda e, cg=cg, b=b: e.tensor_tensor(out=xo.ap[:n, cg * 512:(cg + 1) * 512], in0=b.ap[:n, :], in1=xr.ap[:n, cg * 512:(cg + 1) * 512], op=ALU.add),
                         reads=[b, xr], writes=[xo])
                s.op("act", lambda e: e.activation(out=junk.ap[:n, :], in_=xo.ap[:n, :], func=AF.Square, accum_out=ssq2.ap[:n, :]), reads=[xo], writes=[junk, ssq2])
                s.op("act", lambda e: e.activation(out=rstd2.ap[:n, :], in_=ssq2.ap[:n, :], func=AF.Sqrt, scale=1.0 / D, bias=c["eps"].ap[:n, :]), reads=[ssq2, c["eps"]], writes=[rstd2])
                s.op("dve", lambda e: e.reciprocal(out=rstd2.ap[:n, :], in_=rstd2.ap[:n, :]), reads=[rstd2], writes=[rstd2])
                s.op("dve", lambda e: e.scalar_tensor_tensor(out=xo.ap[:n, :], in0=xo.ap[:n, :], scalar=rstd2.ap[:n, :], in1=fnw_b.ap[:n, :], op0=ALU.mult, op1=ALU.mult),
                     reads=[xo, rstd2, fnw_b], writes=[xo])
                s.dma("pool", y2[yrow0 + o:yrow0 + o + n, :], xo.ap[:n, :], reads=[xo])
    s.barrier()


def build_l2(SEQ, DB):
    NTOK = SEQ + DB * 16
    NB = NTOK // 256
    NT2 = SEQ // NCORES + DB * 16 // NCORES
    nc = bass.Bass("TRN2", target_bir_lowering=False)
    x_all = nc.dram_tensor("x_all", [NTOK, D], F32, kind="ExternalInput").ap()
    GY = nc.dram_tensor("GY", [4 * NB, 1024, 256], BF16, kind="ExternalInput").ap()
    GA = nc.dram_tensor("GA", [2 * NB, 1024, 256], BF16, kind="ExternalInput").ap()
    y2 = nc.dram_tensor("y2", [NT2, D], F32, kind="ExternalOutput").ap()
    l2w = declare_l2_weights(nc)
    cx = Ctx(nc)
    with ExitStack() as es0:
        c = make_consts(cx, es0)
        setup_small_consts(cx, es0, c)
        l2_phase(cx, c, es0, SEQ, DB, x_all, GY, GA, *l2w, y2, *declare_l2_scratch(nc))
    cx.s.emit()
    return nc


def declare_l2_weights(nc):
    wg = nc.dram_tensor("wg", [128, 16, 4096], F32, kind="ExternalInput").ap()
    wssm = nc.dram_tensor("wssm", [128, 32, 2048], F32, kind="ExternalInput").ap()
    wmla = nc.dram_tensor("wmla", [128, 16, 2048], F32, kind="ExternalInput").ap()
    wout = nc.dram_tensor("wout", [128, 16, 2048], F32, kind="ExternalInput").ap()
    nrm = nc.dram_tensor("nrm2", [128, KC], F32, kind="ExternalInput").ap()
    snw = nc.dram_tensor("snw", [128, 32], F32, kind="ExternalInput").ap()
    fnw = nc.dram_tensor("fnw", [1, D], F32, kind="ExternalInput").ap()
    return wg, wssm, wmla, wout, nrm, snw, fnw


def declare_l2_scratch(nc):
    WGS = nc.dram_tensor("WGS", [16, 128, 16, 128], BF16).ap()
    WGM = nc.dram_tensor("WGM", [16, 128, 16, 128], BF16).ap()
    WS = nc.dram_tensor("WS", [16, 128, 32, 128], BF16).ap()
    WM = nc.dram_tensor("WM", [16, 128, 16, 128], BF16).ap()
    return WGS, WGM, WS, WM


def _arr_k(w):
    K, C = w.shape
    return np.ascontiguousarray(w.reshape(K // 128, 128, C).transpose(1, 0, 2))


def _conv_idx(core):
    return np.concatenate([np.arange(core * 512, (core + 1) * 512), 4096 + np.arange(core * 128, (core + 1) * 128),
                           5120 + np.arange(core * 128, (core + 1) * 128)])


def _prep_l1_inputs(x_all, win, norm_in_w, q_norm_w, kv_norm_w, w_q_up, w_kv_up, cache_kv_latent, cache_k_rope, core, extra):
    f = np.float32
    offs = np.cumsum([0, 4096, 6144, 64, 512, 512, 64, 2048, 2048, 2048])
    h0 = 2 * core
    g_cols = win[:, offs[6] + h0 * 128: offs[6] + (h0 + 2) * 128]
    w_m = _arr_k(np.concatenate([win[:, offs[3]:offs[4]], win[:, offs[4]:offs[5]], win[:, offs[5]:offs[6]], g_cols], 1))
    wq = np.asarray(w_q_up, f)[0].reshape(512, 16, 192)
    w_q = np.concatenate([wq[:, h0, :128], wq[:, h0 + 1, :128], wq[:, h0, 128:], wq[:, h0 + 1, 128:]], 1)
    wkv = np.asarray(w_kv_up, f)[0].reshape(512, 16, 256)
    w_kv = np.concatenate([wkv[:, h0, :128], wkv[:, h0 + 1, :128], wkv[:, h0, 128:], wkv[:, h0 + 1, 128:]], 1)
    conv_w, conv_b, dt_bias, a_log, d_skip, state_conv, state_ssm = extra
    idx = _conv_idx(core)
    w_s = _arr_k(np.concatenate([win[:, core * 512:(core + 1) * 512], win[:, offs[1] + idx], win[:, offs[2] + core * 8: offs[2] + (core + 1) * 8]], 1))
    cwv = np.asarray(conv_w, f)[0][:, idx]
    DBn = state_conv.shape[1]
    return {
        "w_s": w_s, "cw": np.ascontiguousarray(cwv.reshape(4, 6, 128).transpose(2, 1, 0)),
        "cb": np.ascontiguousarray(np.asarray(conv_b, f)[0][idx].reshape(6, 128).T),
        "dtb": np.asarray(dt_bias, f)[0][core * 8:(core + 1) * 8].reshape(1, 8).copy(),
        "alog": np.asarray(a_log, f)[0][core * 8:(core + 1) * 8].reshape(1, 8).copy(),
        "dsk": np.asarray(d_skip, f)[0][core * 8:(core + 1) * 8].reshape(1, 8).copy(),
        "sconv": np.ascontiguousarray(np.asarray(state_conv, f)[0][:, :, idx]),
        "sssm": np.ascontiguousarray(np.asarray(state_ssm, f)[0][:, core * 8:(core + 1) * 8].reshape(DBn, 512, 128)),
        "x_all": x_all, "w_m": w_m, "w_q": _arr_k(w_q), "w_kv": _arr_k(w_kv),
        "nrm": np.ascontiguousarray(np.asarray(norm_in_w, f)[0].reshape(16, 128).T),
        "qnw": np.ascontiguousarray(np.asarray(q_norm_w, f)[0].reshape(4, 128).T),
        "kvw": np.asarray(kv_norm_w, f).reshape(1, 512),
        "cache_kv": np.ascontiguousarray(np.asarray(cache_kv_latent, f)[0]),
        "cache_kr": np.ascontiguousarray(np.asarray(cache_k_rope, f)[0]),
    }


def _prep_l2_weights(win, w_ssm_out, w_mla_out, w_out, norm_in_w, ssm_norm_w, final_norm_w):
    f = np.float32
    offs = np.cumsum([0, 4096, 6144, 64, 512, 512, 64, 2048, 2048, 2048])
    return {
        "wg": _arr_k(np.ascontiguousarray(win[:, offs[7]:offs[9]])),
        "wssm": _arr_k(np.asarray(w_ssm_out, f)[0]), "wmla": _arr_k(np.asarray(w_mla_out, f)[0]), "wout": _arr_k(np.asarray(w_out, f)[0]),
        "nrm2": np.ascontiguousarray(np.asarray(norm_in_w, f)[0].reshape(16, 128).T),
        "snw": np.ascontiguousarray(np.asarray(ssm_norm_w, f)[0].reshape(32, 128).T),
        "fnw": np.asarray(final_norm_w, f).reshape(1, D),
    }


def _assemble_y(results, key, SEQ, DB, DS, B):
    f = np.float32
    pp, sp_ = SEQ // NCORES, DB * DS // NCORES
    yp = np.concatenate([np.asarray(r[key])[:pp] for r in results], 0).reshape(B, SEQ, D).astype(f)
    ys = np.concatenate([np.asarray(r[key])[pp:pp + sp_] for r in results], 0).reshape(DB, DS, D).astype(f)
    return yp, ys


def kernel(x_prompt, x_sample, cache_kv_latent, cache_k_rope, state_ssm, state_conv, norm_in_w, w_in,
           conv_w, conv_b, dt_bias, a_log, d_skip, ssm_norm_w, w_ssm_out, q_norm_w, w_q_up, kv_norm_w,
           w_kv_up, w_mla_out, w_out, final_norm_w):
    f = np.float32
    B, SEQ, _ = x_prompt.shape
    DB, DS, _ = x_sample.shape
    PAST = cache_kv_latent.shape[2]
    NTOK = SEQ + DB * DS
    x_all = np.concatenate([np.asarray(x_prompt, f).reshape(SEQ, D), np.asarray(x_sample, f).reshape(DB * DS, D)], 0)
    win = np.asarray(w_in, f)[0]
    FUSED = bool(int(os.environ.get("K_FUSED", "0")))
    nc = build_l1(SEQ, DB, PAST, fused=FUSED)
    extra = (conv_w, conv_b, dt_bias, a_log, d_skip, state_conv, state_ssm)
    ims = [_prep_l1_inputs(x_all, win, norm_in_w, q_norm_w, kv_norm_w, w_q_up, w_kv_up, cache_kv_latent, cache_k_rope, cidx, extra)
           for cidx in range(NCORES)]
    if FUSED:
        l2in = _prep_l2_weights(win, w_ssm_out, w_mla_out, w_out, norm_in_w, ssm_norm_w, final_norm_w)
        for im in ims:
            im.update(l2in)
    res = run_bass_kernel_spmd(nc, ims, core_ids=list(range(NCORES)))
    r0 = res.results[0]
    global _DBG
    _DBG = {}
    kvl = r0["kvlat"]
    krp = r0["krope"]
    conv_p = np.zeros((1, B, 3, 6144), f)
    conv_s = np.zeros((1, DB, 3, 6144), f)
    ssm_p = np.zeros((1, B, 64, 64, 128), f)
    ssm_s = np.zeros((1, DB, 64, 64, 128), f)
    for cidx, r in enumerate(res.results):
        idx = _conv_idx(cidx)
        conv_p[0, 0][:, idx] = r["conv_o"][0]
        conv_s[0][:, :, idx] = r["conv_o"][1:]
        ssm_p[0, 0, cidx * 8:(cidx + 1) * 8] = r["ssm_o"][0].reshape(8, 64, 128)
        ssm_s[0, :, cidx * 8:(cidx + 1) * 8] = r["ssm_o"][1:].reshape(DB, 8, 64, 128)
    NS = DB * DS
    if FUSED:
        y_prompt, y_sample = _assemble_y(res.results, "y2", SEQ, DB, DS, B)
        return (y_prompt, y_sample,
                kvl[:SEQ].reshape(1, B, SEQ, 512), krp[:SEQ].reshape(1, B, SEQ, 64), ssm_p, conv_p,
                kvl[SEQ:].reshape(1, DB, DS, 512), krp[SEQ:].reshape(1, DB, DS, 64), ssm_s, conv_s)
    GYh = np.concatenate([np.asarray(r["YTc"]) for r in res.results], 1)
    GAh = np.concatenate([np.asarray(r["ATc"]) for r in res.results], 1)
    l2in = _prep_l2_weights(win, w_ssm_out, w_mla_out, w_out, norm_in_w, ssm_norm_w, final_norm_w)
    l2in.update({"x_all": x_all, "GY": GYh, "GA": GAh})
    if os.environ.get("K_ONLY_L1"):
        y_prompt, y_sample = np.zeros((B, SEQ, D), f), np.zeros((DB, DS, D), f)
    else:
        print("L1 done", flush=True)
        nc2 = build_l2(SEQ, DB)
        res2 = run_bass_kernel_spmd(nc2, [l2in for _ in range(NCORES)], core_ids=list(range(NCORES)))
        y_prompt, y_sample = _assemble_y(res2.results, "y2", SEQ, DB, DS, B)
    outs = (y_prompt, y_sample,
            kvl[:SEQ].reshape(1, B, SEQ, 512), krp[:SEQ].reshape(1, B, SEQ, 64),
            ssm_p, conv_p,
            kvl[SEQ:].reshape(1, DB, DS, 512), krp[SEQ:].reshape(1, DB, DS, 64),
            ssm_s, conv_s)
    return outs
```

```python
import numpy as np
import ml_dtypes
from contextlib import ExitStack
import concourse.bass as bass
import concourse.mybir as mybir
from concourse.bass_utils import run_bass_kernel_spmd

F32 = mybir.dt.float32
BF16 = mybir.dt.bfloat16
I32 = mybir.dt.int32
AF = mybir.ActivationFunctionType
ALU = mybir.AluOpType
AX = mybir.AxisListType

import os
STOP = int(os.environ.get("K_STOP", "99"))
PH = os.environ.get("K_PH", "MCAS")
SAMEWAIT = bool(int(os.environ.get("K_SAMEWAIT", "1")))
SUB = int(os.environ.get("K_SUB", "99"))
NCORES = 8
D = 2048
KC = 16
EPS = 1e-6
TWO_PI = 6.283185307179586
PI = 3.141592653589793


class Tl:
    __slots__ = ("ap", "w", "r")

    def __init__(self, ap):
        self.ap = ap
        self.w = None
        self.r = {}

    def __getitem__(self, k):
        return self.ap[k]


class _Rec:
    def __init__(self):
        self.calls = []

    def __getattr__(self, name):
        def f(*a, **k):
            self.calls.append((name, a, k))
            return self
        return f


class Sch:
    def __init__(self, nc, ndma=40):
        self.nc = nc
        self.names = ("pe", "act", "dve", "pool", "sp")
        self.q = {k: [] for k in self.names}
        self.sem = {k: nc.alloc_semaphore("sm_" + k) for k in self.names}
        self.cnt = {k: 0 for k in self.names}
        self.seen = {k: {} for k in self.names}
        self.dsem = [nc.alloc_semaphore("sd%d" % i) for i in range(ndma)]
        self.dcnt = [0] * ndma
        self.rr = 0
        self.ccsem = nc.alloc_semaphore("sm_cc")
        self.cccnt = 0
        self.rank = {}

    def _wait(self, en, key, val):
        if val is None or val <= 0:
            return
        if key == en and (en == "pe" or not SAMEWAIT) and en in ("pe", "act", "dve"):
            return
        if self.seen[en].get(key, 0) >= val:
            return
        self.seen[en][key] = val
        sem = self.sem[key] if isinstance(key, str) else self.dsem[key]
        self.q[en].append(lambda e, sem=sem, val=val: e.wait_ge(sem, val))

    def _deps(self, en, reads, writes):
        for t in reads:
            if t.w is not None:
                self._wait(en, *t.w)
        for t in writes:
            if t.w is not None:
                self._wait(en, *t.w)
            for k, v in t.r.items():
                self._wait(en, k, v)

    def _mark(self, tk, reads, writes):
        for t in reads:
            if t.r.get(tk[0], 0) < tk[1]:
                t.r[tk[0]] = tk[1]
        for t in writes:
            t.w = tk
            t.r = {}

    def op(self, en, fn, reads=(), writes=()):
        self._deps(en, reads, writes)
        self.cnt[en] += 1
        sem = self.sem[en]
        rec = _Rec()
        fn(rec)
        assert len(rec.calls) == 1
        name, a, k = rec.calls[0]
        self.q[en].append(lambda e, name=name, a=a, k=k: getattr(e, name)(*a, **k).then_inc(sem, 1))
        tk = (en, self.cnt[en])
        self._mark(tk, reads, writes)
        return tk

    def dma(self, en, out, in_, reads=(), writes=(), **kw):
        self._deps(en, reads, writes)
        i = self.rr
        self.rr = (self.rr + 1) % len(self.dsem)
        self._wait(en, i, self.dcnt[i])
        self.dcnt[i] += 16
        dsem = self.dsem[i]

        def emit(e):
            o, n = out, in_
            if callable(o) or callable(n):
                if en not in self.rank:
                    self.rank[en] = {"r": e.partition_id()}
                if callable(o):
                    o = o(self.rank[en])
                if callable(n):
                    n = n(self.rank[en])
            try:
                e.dma_start(out=o, in_=n, **kw).then_inc(dsem, 16)
            except Exception:
                print("DMA FAIL", en, o, n, flush=True)
                raise
        self.q[en].append(emit)
        tk = (i, self.dcnt[i])
        self._mark(tk, reads, writes)
        return tk

    def coll(self, in_ap, out_ap, reads=(), writes=()):
        self._deps("pool", reads, writes)
        self.cccnt += 1
        n = self.cccnt
        sem = self.ccsem

        def emit(e):
            e.collective_compute("AllGather", ALU.bypass, replica_groups=[list(range(NCORES))],
                                 ins=[in_ap.opt()], outs=[out_ap.opt()]).then_inc(sem, 1)
            e.wait_ge(sem, n)
        self.q["pool"].append(emit)

    def barrier(self):
        for en in self.names:
            for k in self.names:
                self._wait(en, k, self.cnt[k])
            for i in range(len(self.dsem)):
                self._wait(en, i, self.dcnt[i])
            if self.cccnt and en != "pool" and self.seen[en].get("cc", 0) < self.cccnt:
                self.seen[en]["cc"] = self.cccnt
                self.q[en].append(lambda e, sem=self.ccsem, val=self.cccnt: e.wait_ge(sem, val))

    def emit(self, use_block=None):
        nc = self.nc
        if use_block is None:
            use_block = bool(int(os.environ.get("K_BLOCK", "1")))
        if not use_block:
            eng = dict(pe=nc.tensor, act=nc.scalar, dve=nc.vector, pool=nc.gpsimd, sp=nc.sync)
            for en in self.names:
                for f in self.q[en]:
                    f(eng[en])
            return
        with nc.Block() as block:
            starters = dict(pe=block.tensor, act=block.scalar, dve=block.vector, pool=block.gpsimd, sp=block.sync)
            for en in self.names:
                fl = self.q[en]
                if not fl:
                    continue

                def body(e, fl=fl):
                    for f in fl:
                        f(e)
                starters[en](body)


class Ctx:
    ARENA = 206 * 1024

    def __init__(self, nc):
        self.nc = nc
        self.s = Sch(nc)
        self.banks = [Tl(nc.alloc_psum_tensor("pb%d" % i, [128, 512], F32).ap()) for i in range(8)]
        self.bi = 0
        self.flip = 0
        self.arena = nc.alloc_sbuf_tensor("arena", [128, self.ARENA], mybir.dt.uint8).ap()
        self.ptr = 0

    pool = list(range(8))

    def bank(self):
        self.bi = (self.bi + 1) % len(self.pool)
        return self.banks[self.pool[self.bi]]

    def _restore(self, p):
        self.ptr = p

    def sb(self, es, name, shape, dt):
        shape = list(shape)
        esz = {F32: 4, BF16: 2, I32: 4}[dt]
        n = 1
        for d in shape[1:]:
            n *= d
        nbytes = ((n * esz + 31) // 32) * 32
        old = self.ptr
        assert old + nbytes <= self.ARENA, "SBUF arena overflow at %s: %d + %d" % (name, old, nbytes)
        self.ptr = old + nbytes
        es.callback(self._restore, old)
        ap = self.arena[:shape[0], old:old + n * esz].bitcast(dt)
        if len(shape) == 3:
            ap = ap.rearrange("p (a b) -> p a b", a=shape[1])
        elif len(shape) == 4:
            ap = ap.rearrange("p (a b c) -> p a b c", a=shape[1], b=shape[2])
        return Tl(ap)

    def evac_eng(self):
        self.flip ^= 1
        return "act" if self.flip else "dve"


def make_consts(cx, es):
    nc, s = cx.nc, cx.s
    c = {}
    io = cx.sb(es, "c_io", [128, 128], I32)
    s.op("pool", lambda e: e.iota(io.ap, pattern=[[1, 128]], base=0, channel_multiplier=-1), writes=[io])
    iof = cx.sb(es, "c_iof", [128, 128], F32)
    s.op("dve", lambda e: e.tensor_copy(out=iof.ap, in_=io.ap), reads=[io], writes=[iof])
    c["identf"] = cx.sb(es, "c_identf", [128, 128], F32)
    s.op("dve", lambda e: e.tensor_single_scalar(out=c["identf"].ap, in_=iof.ap, scalar=0.0, op=ALU.is_equal),
         reads=[iof], writes=[c["identf"]])
    c["ident"] = cx.sb(es, "c_ident", [128, 128], BF16)
    s.op("dve", lambda e: e.tensor_copy(out=c["ident"].ap, in_=c["identf"].ap), reads=[c["identf"]], writes=[c["ident"]])
    c["U"] = cx.sb(es, "c_U", [128, 128], F32)
    s.op("dve", lambda e: e.tensor_single_scalar(out=c["U"].ap, in_=iof.ap, scalar=0.0, op=ALU.is_ge),
         reads=[iof], writes=[c["U"]])
    c["onesb"] = cx.sb(es, "c_onesb", [128, 128], BF16)
    s.op("dve", lambda e: e.memset(c["onesb"].ap, 1.0), writes=[c["onesb"]])
    c["onesf"] = cx.sb(es, "c_onesf", [128, 128], F32)
    s.op("dve", lambda e: e.memset(c["onesf"].ap, 1.0), writes=[c["onesf"]])
    c["iof"] = iof
    return c


def load_norm_T(cx, c, x_rows_ap, ntok, xt, xs, junk, hT, col0, small):
    s = cx.s
    s.dma("sp", xt.ap[:ntok, :], x_rows_ap, writes=[xt])
    ssq, rstd = small
    s.op("act", lambda e: e.activation(out=junk.ap[:ntok, :], in_=xt.ap[:ntok, :], func=AF.Square, accum_out=ssq.ap[:ntok, :]),
         reads=[xt], writes=[junk, ssq])
    s.op("act", lambda e: e.activation(out=rstd.ap[:ntok, :], in_=ssq.ap[:ntok, :], func=AF.Sqrt, scale=1.0 / D, bias=c["eps"].ap[:ntok, :]),
         reads=[ssq, c["eps"]], writes=[rstd])
    s.op("dve", lambda e: e.reciprocal(out=rstd.ap[:ntok, :], in_=rstd.ap[:ntok, :]), reads=[rstd], writes=[rstd])
    s.op("dve", lambda e: e.tensor_scalar(out=xs.ap[:ntok, :], in0=xt.ap[:ntok, :], scalar1=rstd.ap[:ntok, :], scalar2=None, op0=ALU.mult),
         reads=[xt, rstd], writes=[xs])
    for g in range(2):
        b = cx.bank()
        bv = b.ap.bitcast(BF16)
        for j in range(8):
            kc = g * 8 + j
            s.op("pe", lambda e, j=j, kc=kc: e.transpose(out=bv[:, j * 128:j * 128 + ntok], in_=xs.ap[:ntok, kc * 128:(kc + 1) * 128],
                                                         identity=c["ident"].ap[:ntok, :ntok]),
                 reads=[xs, c["ident"]], writes=[b])
        en = cx.evac_eng()
        src = bv.rearrange("p (j t) -> p j t", j=8)[:, :, :ntok]
        dst = hT.ap[:, g * 8:(g + 1) * 8, col0:col0 + ntok]
        if en == "act":
            s.op("act", lambda e: e.copy(out=dst, in_=src), reads=[b], writes=[hT])
        else:
            s.op("dve", lambda e: e.tensor_copy(out=dst, in_=src), reads=[b], writes=[hT])


def load_weights_bf16(cx, es, name, w_dram, nkc, ncols, scale_tl=None, const_scale=1.0, stage=None):
    s = cx.s
    wt = cx.sb(es, name, [128, nkc, ncols], BF16)
    CH = stage.ap.shape[1]
    for kc in range(nkc):
        for c0 in range(0, ncols, CH):
            cw = min(CH, ncols - c0)
            s.dma("sp", stage.ap[:, :cw], w_dram[:, kc, c0:c0 + cw], writes=[stage])
            if scale_tl is not None:
                s.op("act", lambda e, kc=kc, c0=c0, cw=cw: e.activation(out=wt.ap[:, kc, c0:c0 + cw], in_=stage.ap[:, :cw], func=AF.Copy,
                                                                        scale=scale_tl.ap[:, kc:kc + 1]),
                     reads=[stage, scale_tl], writes=[wt])
            else:
                s.op("act", lambda e, kc=kc, c0=c0, cw=cw: e.activation(out=wt.ap[:, kc, c0:c0 + cw], in_=stage.ap[:, :cw], func=AF.Copy,
                                                                        scale=float(const_scale)),
                     reads=[stage], writes=[wt])
    return wt


def mm_acc(cx, bank_ap, pairs, reads, bank):
    n = len(pairs)
    for i, (l, r) in enumerate(pairs):
        cx.s.op("pe", lambda e, l=l, r=r, i=i: e.matmul(bank_ap, lhsT=l, rhs=r, start=(i == 0), stop=(i == n - 1)),
                reads=reads, writes=[bank])


def rope_tables(cx, c, tabs, pos_tl, nsub):
    s = cx.s
    ang, kf, ki, cos, sin, m1 = tabs
    for (dst, shift) in ((sin, 0.0), (cos, PI / 2)):
        s.op("dve", lambda e: e.tensor_tensor(out=ang.ap[:, :nsub, :], in0=pos_tl.ap[:, :nsub].unsqueeze(2).to_broadcast([128, nsub, 32]),
                                              in1=c["inv"].ap.unsqueeze(1).to_broadcast([128, nsub, 32]), op=ALU.mult),
             reads=[pos_tl, c["inv"]], writes=[ang])
        if shift:
            s.op("dve", lambda e: e.tensor_scalar(out=ang.ap[:, :nsub, :], in0=ang.ap[:, :nsub, :], scalar1=float(shift), scalar2=None, op0=ALU.add),
                 reads=[ang], writes=[ang])
        s.op("dve", lambda e: e.tensor_scalar(out=kf.ap[:, :nsub, :], in0=ang.ap[:, :nsub, :], scalar1=1.0 / TWO_PI, scalar2=None, op0=ALU.mult),
             reads=[ang], writes=[kf])
        s.op("dve", lambda e: e.tensor_copy(out=ki.ap[:, :nsub, :], in_=kf.ap[:, :nsub, :]), reads=[kf], writes=[ki])
        s.op("dve", lambda e: e.tensor_copy(out=kf.ap[:, :nsub, :], in_=ki.ap[:, :nsub, :]), reads=[ki], writes=[kf])
        s.op("dve", lambda e: e.scalar_tensor_tensor(out=ang.ap[:, :nsub, :], in0=kf.ap[:, :nsub, :], scalar=-TWO_PI, in1=ang.ap[:, :nsub, :],
                                                     op0=ALU.mult, op1=ALU.add), reads=[kf, ang], writes=[ang])
        s.op("dve", lambda e: e.tensor_scalar(out=m1.ap[:, :nsub, :], in0=ang.ap[:, :nsub, :], scalar1=PI, scalar2=-TWO_PI, op0=ALU.is_gt, op1=ALU.mult),
             reads=[ang], writes=[m1])
        s.op("dve", lambda e: e.tensor_tensor(out=ang.ap[:, :nsub, :], in0=ang.ap[:, :nsub, :], in1=m1.ap[:, :nsub, :], op=ALU.add),
             reads=[ang, m1], writes=[ang])
        s.op("dve", lambda e: e.tensor_scalar(out=m1.ap[:, :nsub, :], in0=ang.ap[:, :nsub, :], scalar1=-PI, scalar2=TWO_PI, op0=ALU.is_lt, op1=ALU.mult),
             reads=[ang], writes=[m1])
        s.op("dve", lambda e: e.tensor_tensor(out=ang.ap[:, :nsub, :], in0=ang.ap[:, :nsub, :], in1=m1.ap[:, :nsub, :], op=ALU.add),
             reads=[ang, m1], writes=[ang])
        s.op("dve", lambda e: e.tensor_scalar(out=ang.ap[:, :nsub, :], in0=ang.ap[:, :nsub, :], scalar1=PI, scalar2=-PI, op0=ALU.min, op1=ALU.max),
             reads=[ang], writes=[ang])
        s.op("act", lambda e, dst=dst: e.activation(out=dst.ap[:, :nsub, :], in_=ang.ap[:, :nsub, :], func=AF.Sin), reads=[ang], writes=[dst])


def apply_rope(cx, src_ap, src_tl, nh, cos_ap, sin_ap, tabs_tl, out_tl, tmp_tl, ntok):
    s = cx.s
    x1 = src_ap[:, :, 0:32]
    x2 = src_ap[:, :, 32:64]
    cb = cos_ap.unsqueeze(1).to_broadcast([ntok, nh, 32])
    sb_ = sin_ap.unsqueeze(1).to_broadcast([ntok, nh, 32])
    o = out_tl.ap[:ntok].rearrange("p (h d) -> p h d", h=nh)
    t = tmp_tl.ap[:ntok].rearrange("p (h d) -> p h d", h=nh)
    rd = [src_tl] + list(tabs_tl)
    s.op("dve", lambda e: e.tensor_tensor(out=o[:, :, 0:32], in0=x1, in1=cb, op=ALU.mult), reads=rd, writes=[out_tl])
    s.op("dve", lambda e: e.tensor_tensor(out=t[:, :, 0:32], in0=x2, in1=sb_, op=ALU.mult), reads=rd, writes=[tmp_tl])
    s.op("dve", lambda e: e.tensor_tensor(out=o[:, :, 32:64], in0=x1, in1=sb_, op=ALU.mult), reads=rd, writes=[out_tl])
    s.op("dve", lambda e: e.tensor_tensor(out=t[:, :, 32:64], in0=x2, in1=cb, op=ALU.mult), reads=rd, writes=[tmp_tl])
    s.op("dve", lambda e: e.tensor_tensor(out=o[:, :, 0:32], in0=o[:, :, 0:32], in1=t[:, :, 0:32], op=ALU.subtract),
         reads=[out_tl, tmp_tl], writes=[out_tl])
    s.op("dve", lambda e: e.tensor_tensor(out=o[:, :, 32:64], in0=o[:, :, 32:64], in1=t[:, :, 32:64], op=ALU.add),
         reads=[out_tl, tmp_tl], writes=[out_tl])


def setup_small_consts(cx, es, c):
    s = cx.s
    c["eps"] = cx.sb(es, "c_eps", [128, 1], F32)
    s.op("dve", lambda e: e.memset(c["eps"].ap, EPS), writes=[c["eps"]])
    ji = cx.sb(es, "c_ji", [128, 32], I32)
    s.op("pool", lambda e: e.iota(ji.ap, pattern=[[1, 32]], base=0, channel_multiplier=0), writes=[ji])
    jf = cx.sb(es, "c_jf", [128, 32], F32)
    s.op("dve", lambda e: e.tensor_copy(out=jf.ap, in_=ji.ap), reads=[ji], writes=[jf])
    c["inv"] = cx.sb(es, "c_inv", [128, 32], F32)
    s.op("act", lambda e: e.activation(out=c["inv"].ap, in_=jf.ap, func=AF.Exp, scale=-float(np.log(10000.0)) / 32.0),
         reads=[jf], writes=[c["inv"]])
    pi_ = cx.sb(es, "c_pi", [128, 1], I32)
    s.op("pool", lambda e: e.iota(pi_.ap, pattern=[[0, 1]], base=0, channel_multiplier=1), writes=[pi_])
    c["pf"] = cx.sb(es, "c_pf", [128, 1], F32)
    s.op("dve", lambda e: e.tensor_copy(out=c["pf"].ap, in_=pi_.ap), reads=[pi_], writes=[c["pf"]])
    pm = cx.sb(es, "c_pm", [128, 1], I32)
    s.op("dve", lambda e: e.tensor_single_scalar(out=pm.ap, in_=pi_.ap, scalar=15, op=ALU.bitwise_and), reads=[pi_], writes=[pm])
    c["pm16"] = cx.sb(es, "c_pm16", [128, 1], F32)
    s.op("dve", lambda e: e.tensor_copy(out=c["pm16"].ap, in_=pm.ap), reads=[pm], writes=[c["pm16"]])


def evac(cx, dst_ap, dst_tl, src_ap, src_tl, en=None, extra_reads=()):
    en = en or cx.evac_eng()
    if en == "act":
        cx.s.op("act", lambda e: e.copy(out=dst_ap, in_=src_ap), reads=[src_tl] + list(extra_reads), writes=[dst_tl])
    else:
        cx.s.op("dve", lambda e: e.tensor_copy(out=dst_ap, in_=src_ap), reads=[src_tl] + list(extra_reads), writes=[dst_tl])


def rms_rstd(cx, c, src_ap, src_tl, ntok, n, junkf, ssq, rstd):
    s = cx.s
    s.op("act", lambda e: e.activation(out=junkf.ap[:ntok, :n], in_=src_ap, func=AF.Square, accum_out=ssq.ap[:ntok, :]),
         reads=[src_tl], writes=[junkf, ssq])
    s.op("act", lambda e: e.activation(out=rstd.ap[:ntok, :], in_=ssq.ap[:ntok, :], func=AF.Sqrt, scale=1.0 / n, bias=c["eps"].ap[:ntok, :]),
         reads=[ssq, c["eps"]], writes=[rstd])
    s.op("dve", lambda e: e.reciprocal(out=rstd.ap[:ntok, :], in_=rstd.ap[:ntok, :]), reads=[rstd], writes=[rstd])


def transpose_to(cx, c, src_tl, src_ap_fn, nblk, ntok, dst_fn, dst_tl, rows=128):
    s = cx.s
    b = cx.bank()
    bv = b.ap.bitcast(BF16)
    for j in range(nblk):
        s.op("pe", lambda e, j=j: e.transpose(out=bv[:rows, j * 128:j * 128 + ntok], in_=src_ap_fn(j), identity=c["ident"].ap[:ntok, :ntok]),
             reads=[src_tl, c["ident"]], writes=[b])
    for j in range(nblk):
        evac(cx, dst_fn(j), dst_tl, bv[:rows, j * 128:j * 128 + ntok], b)


def kv_from_latent(cx, c, ckvT, krT_unused, wkv, ST, nsub, kst, vst, KT_dram, V_dram, tok0):
    s = cx.s
    for h in range(2):
        b = cx.bank()
        mm_acc(cx, b.ap[:, :ST], [(wkv.ap[:, kc, h * 128:(h + 1) * 128], ckvT.ap[:, kc, :ST]) for kc in range(4)], [wkv, ckvT], b)
        evac(cx, kst.ap[:, h, :ST], kst, b.ap[:, :ST], b)
    s.dma("pool", KT_dram[:, :, tok0:tok0 + ST], kst.ap[:, :, :ST], reads=[kst])
    for sub in range(nsub):
        b = cx.bank()
        mm_acc(cx, b.ap[:, :256], [(ckvT.ap[:, kc, sub * 128:(sub + 1) * 128], wkv.ap[:, kc, 256:512]) for kc in range(4)], [wkv, ckvT], b)
        evac(cx, vst.ap[:, sub, :], vst, b.ap[:, :256], b)
    s.dma("pool", V_dram[tok0:tok0 + ST, :].rearrange("(n p) f -> p n f", p=128), vst.ap[:, :nsub, :], reads=[vst])


def attention_phase(cx, c, es, SEQ, DB, PAST, QT, QRT, KT, KRT, V, GT, KTc, KRTc, Vc, AT):
    s = cx.s
    qn = [cx.sb(es, "a_qn%d" % i, [128, 2, 512], BF16) for i in range(2)]
    qr = [cx.sb(es, "a_qr%d" % i, [64, 2, 512], BF16) for i in range(2)]
    gg = [cx.sb(es, "a_g%d" % i, [128, 2, 512], BF16) for i in range(2)]
    kt = [cx.sb(es, "a_kt%d" % i, [128, 2, 512], BF16) for i in range(3)]
    kr = [cx.sb(es, "a_kr%d" % i, [64, 512], BF16) for i in range(3)]
    vv = [cx.sb(es, "a_v%d" % i, [128, 4, 256], BF16) for i in range(3)]
    pT = [cx.sb(es, "a_p%d" % i, [128, 512], BF16) for i in range(4)]
    msk = [cx.sb(es, "a_m%d" % i, [128, 512], BF16) for i in range(4)]
    rec = cx.sb(es, "a_rec", [128, 512], F32)
    ot = cx.sb(es, "a_o", [128, 512], F32)
    ao = [cx.sb(es, "a_ao%d" % i, [128, 2, 512], BF16) for i in range(2)]
    for j in range(4):
        s.op("dve", lambda e, j=j: e.memset(msk[j].ap, 0.0), writes=[msk[j]])
        if 128 * j < 512:
            s.op("dve", lambda e, j=j: e.memset(msk[j].ap[0:64, 128 * j:], 1.0), writes=[msk[j]])
        if 128 * j + 64 < 512:
            s.op("dve", lambda e, j=j: e.memset(msk[j].ap[64:128, 128 * j + 64:], 1.0), writes=[msk[j]])
    O = [cx.banks[0], cx.banks[1]]
    Dn = [cx.banks[2], cx.banks[3]]
    st = {"srot": 0, "prot": 0, "kld": 0, "qld": 0}

    pend = []

    def flush():
        while pend:
            pend.pop(0)()

    def block(ktl, krl, vl, kt_ap, kr_ap, v_ap, nk, qtl, qrl, qn_ap, qr_ap, nq, first, last, mask):
        for h in range(2):
            S = cx.banks[4 + st["srot"] % 4]
            st["srot"] += 1
            s.op("pe", lambda e: e.matmul(S.ap[:nk, :nq], lhsT=kt_ap(h), rhs=qn_ap(h), start=True, stop=False), reads=[ktl, qtl], writes=[S])
            s.op("pe", lambda e: e.matmul(S.ap[:nk, :nq], lhsT=kr_ap, rhs=qr_ap(h), start=False, stop=True), reads=[krl, qrl], writes=[S])
            p = pT[st["prot"] % 4]
            st["prot"] += 1
            s.op("act", lambda e: e.activation(out=p.ap[:nk, :nq], in_=S.ap[:nk, :nq], func=AF.Exp), reads=[S], writes=[p])
            if mask is not None:
                s.op("dve", lambda e: e.tensor_tensor(out=p.ap[:nk, :nq], in0=p.ap[:nk, :nq], in1=mask.ap[:nk, :nq], op=ALU.mult), reads=[p, mask], writes=[p])

            def pv(h=h, p=p, vap=v_ap(h)):
                s.op("pe", lambda e: e.matmul(O[h].ap[:, :nq], lhsT=vap, rhs=p.ap[:nk, :nq], start=first, stop=last), reads=[vl, p], writes=[O[h]])
                s.op("pe", lambda e: e.matmul(Dn[h].ap[:, :nq], lhsT=c["onesb"].ap[:nk, :], rhs=p.ap[:nk, :nq], start=first, stop=last),
                     reads=[c["onesb"], p], writes=[Dn[h]])
            if len(pend) >= 2:
                pend.pop(0)()
            pend.append(pv)

    NB = (SEQ + DB * 16) // 256

    def finalize(gtl, g_ap, nq, out_tl, q0):
        flush()
        for h in range(2):
            s.op("dve", lambda e: e.reciprocal(out=rec.ap[:, :nq], in_=Dn[h].ap[:, :nq]), reads=[Dn[h]], writes=[rec])
            s.op("dve", lambda e: e.tensor_tensor(out=ot.ap[:, :nq], in0=O[h].ap[:, :nq], in1=rec.ap[:, :nq], op=ALU.mult), reads=[O[h], rec], writes=[ot])
            s.op("dve", lambda e: e.tensor_tensor(out=out_tl.ap[:, h, :nq], in0=ot.ap[:, :nq], in1=g_ap(h), op=ALU.mult), reads=[ot, gtl], writes=[out_tl])
        tb, off = q0 // 256, q0 % 256
        for h in range(2):
            if nq >= 256:
                s.dma("pool", AT[h * NB + tb:h * NB + tb + nq // 256].rearrange("n p t -> p n t"),
                      out_tl.ap[:, h, :nq].rearrange("p (n t) -> p n t", t=256), reads=[out_tl])
            else:
                s.dma("pool", AT[h * NB + tb][:, off:off + nq], out_tl.ap[:, h, :nq], reads=[out_tl])

    def load_keys(KTd, KRTd, Vd, k0, nk):
        i = st["kld"] % 3
        st["kld"] += 1
        s.dma("sp", kt[i].ap[:, :, :nk], KTd[:, :, k0:k0 + nk], writes=[kt[i]])
        s.dma("sp", kr[i].ap[:, :nk], KRTd[:, k0:k0 + nk], writes=[kr[i]])
        if nk >= 128:
            s.dma("sp", vv[i].ap[:, :nk // 128, :], Vd[k0:k0 + nk, :].rearrange("(n p) f -> p n f", p=128), writes=[vv[i]])
        else:
            s.dma("sp", vv[i].ap[:nk, 0, :], Vd[k0:k0 + nk, :], writes=[vv[i]])
        return kt[i], kr[i], vv[i]

    def load_q(q0, nq):
        i = st["qld"] % 2
        st["qld"] += 1
        s.dma("sp", qn[i].ap[:, :, :nq], QT[:, :, q0:q0 + nq], writes=[qn[i]])
        s.dma("sp", qr[i].ap[:, :, :nq], QRT[:, :, q0:q0 + nq], writes=[qr[i]])
        s.dma("sp", gg[i].ap[:, :, :nq], GT[:, :, q0:q0 + nq], writes=[gg[i]])
        return qn[i], qr[i], gg[i], ao[i]

    for qb in range(SEQ // 512):
        q_n, q_r, g_, a_ = load_q(qb * 512, 512)
        for ksb in range(qb + 1):
            k_, r_, v_ = load_keys(KT, KRT, V, ksb * 512, 512)
            for j in range(4):
                block(k_, r_, v_, lambda h, j=j: k_.ap[:, h, j * 128:(j + 1) * 128], r_.ap[:, j * 128:(j + 1) * 128],
                      lambda h, j=j: v_.ap[:, j, h * 128:(h + 1) * 128], 128,
                      q_n, q_r, lambda h: q_n.ap[:, h, :], lambda h: q_r.ap[:, h, :], 512,
                      first=(ksb == 0 and j == 0), last=(ksb == qb and j == 3), mask=(msk[j] if ksb == qb else None))
        finalize(g_, lambda h: g_.ap[:, h, :], 512, a_, qb * 512)
    for b_ in range(DB):
        q0 = SEQ + b_ * 16
        q_n, q_r, g_, a_ = load_q(q0, 16)
        nsb = PAST // 512
        for ksb in range(nsb):
            k_, r_, v_ = load_keys(KTc, KRTc, Vc, b_ * PAST + ksb * 512, 512)
            for j in range(4):
                block(k_, r_, v_, lambda h, j=j: k_.ap[:, h, j * 128:(j + 1) * 128], r_.ap[:, j * 128:(j + 1) * 128],
                      lambda h, j=j: v_.ap[:, j, h * 128:(h + 1) * 128], 128,
                      q_n, q_r, lambda h: q_n.ap[:, h, :16], lambda h: q_r.ap[:, h, :16], 16,
                      first=(ksb == 0 and j == 0), last=False, mask=None)
        k_, r_, v_ = load_keys(KT, KRT, V, q0, 16)
        block(k_, r_, v_, lambda h: k_.ap[:, h, :16], r_.ap[:, :16], lambda h: v_.ap[:16, 0, h * 128:(h + 1) * 128], 16,
              q_n, q_r, lambda h: q_n.ap[:, h, :16], lambda h: q_r.ap[:, h, :16], 16, first=(nsb == 0), last=True, mask=None)
        finalize(g_, lambda h: g_.ap[:, h, :16], 16, a_, q0)


def split_bf(cx, src_ap, src_tl, parts, shape_fn, nterms=2):
    s = cx.s
    s.op("dve", lambda e: e.tensor_copy(out=shape_fn(parts[0]), in_=src_ap), reads=[src_tl], writes=[parts[0]])
    if nterms == 2:
        s.op("dve", lambda e: e.tensor_tensor(out=shape_fn(parts[1]), in0=src_ap, in1=shape_fn(parts[0]), op=ALU.subtract),
             reads=[src_tl, parts[0]], writes=[parts[1]])
    else:
        r = parts[-1]
        s.op("dve", lambda e: e.tensor_tensor(out=shape_fn(r), in0=src_ap, in1=shape_fn(parts[0]), op=ALU.subtract),
             reads=[src_tl, parts[0]], writes=[r])
        s.op("dve", lambda e: e.tensor_copy(out=shape_fn(parts[1]), in_=shape_fn(r)), reads=[r], writes=[parts[1]])
        s.op("dve", lambda e: e.tensor_tensor(out=shape_fn(r), in0=shape_fn(r), in1=shape_fn(parts[1]), op=ALU.subtract),
             reads=[r, parts[1]], writes=[r])
        s.op("dve", lambda e: e.tensor_copy(out=shape_fn(parts[2]), in_=shape_fn(r)), reads=[r], writes=[parts[2]])


def transpose_f32(cx, c, TW, src_ap, src_tl, R, C, dst_ap, dst_tl):
    s = cx.s
    parts = TW["parts"]
    sf = lambda t: t.ap[:R, :C]
    split_bf(cx, src_ap, src_tl, parts, sf, nterms=3)
    b = cx.bank()
    bv = b.ap.bitcast(BF16)
    for i in range(3):
        s.op("pe", lambda e, i=i: e.transpose(out=bv[:C, i * 128:i * 128 + R], in_=parts[i].ap[:R, :C], identity=c["ident"].ap[:R, :R]),
             reads=[parts[i], c["ident"]], writes=[b])
    s.op("act", lambda e: e.copy(out=dst_ap, in_=bv[:C, 256:256 + R]), reads=[b], writes=[dst_tl])
    s.op("dve", lambda e: e.tensor_tensor(out=dst_ap, in0=bv[:C, 128:128 + R], in1=dst_ap, op=ALU.add), reads=[b, dst_tl], writes=[dst_tl])
    s.op("dve", lambda e: e.tensor_tensor(out=dst_ap, in0=bv[:C, 0:R], in1=dst_ap, op=ALU.add), reads=[b, dst_tl], writes=[dst_tl])


def ssd_consts(cx, es, c):
    s = cx.s
    di = cx.sb(es, "c_di", [8, 8, 128], I32)
    s.op("pool", lambda e: e.iota(di.ap, pattern=[[1, 8], [0, 128]], base=0, channel_multiplier=-1), writes=[di])
    df = cx.sb(es, "c_df", [8, 8, 128], F32)
    s.op("dve", lambda e: e.tensor_copy(out=df.ap, in_=di.ap), reads=[di], writes=[df])
    c["Delta"] = cx.sb(es, "c_Delta", [8, 8, 128], F32)
    s.op("dve", lambda e: e.tensor_single_scalar(out=c["Delta"].ap, in_=df.ap, scalar=0.0, op=ALU.is_equal), reads=[df], writes=[c["Delta"]])
    c["Deltab"] = cx.sb(es, "c_Deltab", [8, 8, 128], BF16)
    s.op("dve", lambda e: e.tensor_copy(out=c["Deltab"].ap, in_=c["Delta"].ap), reads=[c["Delta"]], writes=[c["Deltab"]])
    c["Ub"] = cx.sb(es, "c_Ub", [128, 128], BF16)
    s.op("dve", lambda e: e.tensor_copy(out=c["Ub"].ap, in_=c["U"].ap), reads=[c["U"]], writes=[c["Ub"]])
    for T in (128, 16):
        t = cx.sb(es, "c_sel%d" % T, [128, 128], BF16)
        s.op("dve", lambda e, t=t, T=T: e.tensor_scalar(out=t.ap, in0=c["onesf"].ap, scalar1=c["pf"].ap, scalar2=float(T - 1), op0=ALU.mult, op1=ALU.is_equal),
             reads=[c["onesf"], c["pf"]], writes=[t])
        c["sel%d" % T] = t


def ssd_tile(cx, c, W, T, x_tm, B_tm, BT_ap, CT_ap, bc_tl, dt_sb, zs, S, S_bf, yn_dram):
    s = cx.s
    Ab, Dsk = W["Ab"], W["Dsk"]
    dtA, acs_sb, eacs, acsT_sb = W["dtA"], W["acs_sb"], W["eacs"], W["acsT_sb"]
    s.op("dve", lambda e: e.tensor_tensor(out=dtA.ap[:T, :], in0=dt_sb.ap[:T, :], in1=Ab.ap[:T, :], op=ALU.mult), reads=[dt_sb, Ab], writes=[dtA])
    dth, dtl = W["dtA_h"], W["dtA_l"]
    split_bf(cx, dtA.ap[:T, :], dtA, [dth, dtl], lambda t: t.ap[:T, :])
    b_acs = cx.bank()
    s.op("pe", lambda e: e.matmul(b_acs.ap[:T, :8], lhsT=c["Ub"].ap[:T, :T], rhs=dth.ap[:T, :], start=True, stop=False), reads=[c["Ub"], dth], writes=[b_acs])
    s.op("pe", lambda e: e.matmul(b_acs.ap[:T, :8], lhsT=c["Ub"].ap[:T, :T], rhs=dtl.ap[:T, :], start=False, stop=True), reads=[c["Ub"], dtl], writes=[b_acs])
    s.op("dve", lambda e: e.tensor_copy(out=acs_sb.ap[:T, :], in_=b_acs.ap[:T, :8]), reads=[b_acs], writes=[acs_sb])
    s.op("act", lambda e: e.activation(out=eacs.ap[:T, :], in_=b_acs.ap[:T, :8], func=AF.Exp), reads=[b_acs], writes=[eacs])
    if SUB < 1:
        return
    b_at = cx.bank()
    s.op("pe", lambda e: e.matmul(b_at.ap[:8, :T], lhsT=dth.ap[:T, :], rhs=c["Ub"].ap[:T, :T], start=True, stop=False), reads=[c["Ub"], dth], writes=[b_at])
    s.op("pe", lambda e: e.matmul(b_at.ap[:8, :T], lhsT=dtl.ap[:T, :], rhs=c["Ub"].ap[:T, :T], start=False, stop=True), reads=[c["Ub"], dtl], writes=[b_at])
    s.op("dve", lambda e: e.tensor_copy(out=acsT_sb.ap[:, :T], in_=b_at.ap[:8, :T]), reads=[b_at], writes=[acsT_sb])
    ath, atl, nth, ntl = W["acsT_h"], W["acsT_l"], W["nacsT_h"], W["nacsT_l"]
    split_bf(cx, acsT_sb.ap[:, :T], acsT_sb, [ath, atl], lambda t: t.ap[:, :T])
    s.op("dve", lambda e: e.tensor_scalar(out=nth.ap[:, :T], in0=ath.ap[:, :T], scalar1=-1.0, scalar2=None, op0=ALU.mult), reads=[ath], writes=[nth])
    s.op("dve", lambda e: e.tensor_scalar(out=ntl.ap[:, :T], in0=atl.ap[:, :T], scalar1=-1.0, scalar2=None, op0=ALU.mult), reads=[atl], writes=[ntl])
    if SUB < 2:
        return
    Dmh, Dml = W["Dm_h"], W["Dm_l"]
    for (dm, at_) in ((Dmh, ath), (Dml, atl)):
        s.op("dve", lambda e, dm=dm, at_=at_: e.tensor_tensor(out=dm.ap[:, :8 * T].rearrange("h (g i) -> h g i", g=8), in0=c["Deltab"].ap[:, :, :T],
                                                              in1=at_.ap[:, :T].unsqueeze(1).to_broadcast([8, 8, T]), op=ALU.mult),
             reads=[c["Deltab"], at_], writes=[dm])
    if SUB < 3:
        return
    b_cb = cx.bank()
    s.op("pe", lambda e: e.matmul(b_cb.ap[:T, :T], lhsT=BT_ap, rhs=CT_ap, start=True, stop=True), reads=[bc_tl], writes=[b_cb])
    cbU = W["cbU"]
    s.op("dve", lambda e: e.tensor_tensor(out=cbU.ap[:T, :T], in0=b_cb.ap[:T, :T], in1=c["U"].ap[:T, :T], op=ALU.mult), reads=[b_cb, c["U"]], writes=[cbU])
    if SUB < 4:
        return
    Em, Ee, wts = W["Em"], W["Ee"], W["wts"]
    hp = 4 if 8 * T > 512 else 8
    for h0 in range(0, 8, hp):
        b_sg = cx.bank()
        ncol = hp * T
        s.op("pe", lambda e: e.matmul(b_sg.ap[:T, :ncol], lhsT=c["onesb"].ap[:8, :T], rhs=Dmh.ap[:, h0 * T:(h0 + hp) * T], start=True, stop=False),
             reads=[c["onesb"], Dmh], writes=[b_sg])
        s.op("pe", lambda e: e.matmul(b_sg.ap[:T, :ncol], lhsT=c["onesb"].ap[:8, :T], rhs=Dml.ap[:, h0 * T:(h0 + hp) * T], start=False, stop=False),
             reads=[c["onesb"], Dml], writes=[b_sg])
        s.op("pe", lambda e: e.matmul(b_sg.ap[:T, :ncol], lhsT=nth.ap[:, :T], rhs=c["Deltab"].ap[:, h0:h0 + hp, :T], start=False, stop=False),
             reads=[nth, c["Deltab"]], writes=[b_sg])
        s.op("pe", lambda e: e.matmul(b_sg.ap[:T, :ncol], lhsT=ntl.ap[:, :T], rhs=c["Deltab"].ap[:, h0:h0 + hp, :T], start=False, stop=True),
             reads=[ntl, c["Deltab"]], writes=[b_sg])
        s.op("dve", lambda e: e.tensor_scalar(out=Em.ap[:T, :ncol], in0=b_sg.ap[:T, :ncol], scalar1=0.0, scalar2=None, op0=ALU.min), reads=[b_sg], writes=[Em])
        s.op("act", lambda e: e.activation(out=Ee.ap[:T, :ncol], in_=Em.ap[:T, :ncol], func=AF.Exp), reads=[Em], writes=[Ee])
        s.op("dve", lambda e: e.tensor_tensor(out=wts.ap[:T, h0 * T:(h0 + hp) * T].rearrange("p (g i) -> p g i", g=hp),
                                              in0=Ee.ap[:T, :ncol].rearrange("p (g i) -> p g i", g=hp),
                                              in1=cbU.ap[:T, :T].unsqueeze(1).to_broadcast([T, hp, T]), op=ALU.mult), reads=[Ee, cbU], writes=[wts])
    if SUB < 5:
        return
    xdt, xdtt = W["xdt"], W["xdtt"]
    s.op("dve", lambda e: e.tensor_tensor(out=xdt.ap[:T, :].rearrange("p (h d) -> p h d", h=8), in0=x_tm.ap[:T, :].rearrange("p (h d) -> p h d", h=8),
                                          in1=dt_sb.ap[:T, :].unsqueeze(2).to_broadcast([T, 8, 64]), op=ALU.mult), reads=[x_tm, dt_sb], writes=[xdt])
    b_yd = cx.bank()
    for h in range(8):
        s.op("pe", lambda e, h=h: e.matmul(b_yd.ap[:T, h * 64:(h + 1) * 64], lhsT=wts.ap[:T, h * T:(h + 1) * T], rhs=xdt.ap[:T, h * 64:(h + 1) * 64],
                                           start=True, stop=True), reads=[wts, xdt], writes=[b_yd])
    if SUB < 6:
        return
    b_yo = cx.bank()
    s.op("pe", lambda e: e.matmul(b_yo.ap[:T, :], lhsT=CT_ap, rhs=S_bf.ap, start=True, stop=True), reads=[bc_tl, S_bf], writes=[b_yo])
    if SUB < 7:
        return
    t1, t2 = W["t1"], W["t2"]
    v3 = lambda ap: ap.rearrange("p (h d) -> p h d", h=8)
    s.op("dve", lambda e: e.tensor_tensor(out=v3(t1.ap[:T, :]), in0=v3(b_yo.ap[:T, :]), in1=eacs.ap[:T, :].unsqueeze(2).to_broadcast([T, 8, 64]), op=ALU.mult),
         reads=[b_yo, eacs], writes=[t1])
    s.op("dve", lambda e: e.tensor_tensor(out=t2.ap[:T, :], in0=b_yd.ap[:T, :], in1=t1.ap[:T, :], op=ALU.add), reads=[b_yd, t1], writes=[t2])
    s.op("dve", lambda e: e.tensor_tensor(out=v3(t1.ap[:T, :]), in0=v3(x_tm.ap[:T, :]), in1=Dsk.ap[:T, :].unsqueeze(2).to_broadcast([T, 8, 64]), op=ALU.mult),
         reads=[x_tm, Dsk], writes=[t1])
    s.op("dve", lambda e: e.tensor_tensor(out=t2.ap[:T, :], in0=t2.ap[:T, :], in1=t1.ap[:T, :], op=ALU.add), reads=[t2, t1], writes=[t2])
    s.op("dve", lambda e: e.tensor_tensor(out=t2.ap[:T, :], in0=t2.ap[:T, :], in1=zs.ap[:T, :], op=ALU.mult), reads=[t2, zs], writes=[t2])
    if SUB < 8:
        return
    rms_rstd(cx, c, t2.ap[:T, :], t2, T, 512, W["junkf"], W["ssq"], W["rstd"])
    yn = W["yn"][W["ynrot"][0] % 2]
    W["ynrot"][0] += 1
    s.op("dve", lambda e: e.tensor_scalar(out=yn.ap[:T, :], in0=t2.ap[:T, :], scalar1=W["rstd"].ap[:T, :], scalar2=None, op0=ALU.mult), reads=[t2, W["rstd"]], writes=[yn])
    Yd, NBv, tok0 = yn_dram
    bt_ = cx.bank()
    btv = bt_.ap.bitcast(BF16)
    for j in range(4):
        s.op("pe", lambda e, j=j: e.transpose(out=btv[:, j * 128:j * 128 + T], in_=yn.ap[:T, j * 128:(j + 1) * 128], identity=c["ident"].ap[:T, :T]),
             reads=[yn, c["ident"]], writes=[bt_])
    ynT = W["ynT"][W["ynrot"][0] % 2]
    evac(cx, ynT.ap[:, :, :T], ynT, btv[:, :512].rearrange("p (j t) -> p j t", j=4)[:, :, :T], bt_)
    tb, off = tok0 // 256, tok0 % 256
    s.dma("pool", Yd.rearrange("(k n) p t -> p k n t", k=4)[:, :, tb, off:off + T], ynT.ap[:, :, :T], reads=[ynT])
    if SUB < 9:
        return
    b_al = cx.bank()
    ach, acl = W["acs_h"], W["acs_l"]
    split_bf(cx, acs_sb.ap[:T, :], acs_sb, [ach, acl], lambda t: t.ap[:T, :])
    s.op("pe", lambda e: e.matmul(b_al.ap[:, :8], lhsT=c["sel%d" % T].ap[:T, :], rhs=ach.ap[:T, :], start=True, stop=False), reads=[c["sel%d" % T], ach], writes=[b_al])
    s.op("pe", lambda e: e.matmul(b_al.ap[:, :8], lhsT=c["sel%d" % T].ap[:T, :], rhs=acl.ap[:T, :], start=False, stop=True), reads=[c["sel%d" % T], acl], writes=[b_al])
    cd, tl = W["cd"], W["tl"]
    s.op("act", lambda e: e.activation(out=cd.ap, in_=b_al.ap[:, :8], func=AF.Exp), reads=[b_al], writes=[cd])
    s.op("dve", lambda e: e.tensor_tensor(out=tl.ap[:T, :], in0=b_al.ap[:T, :8], in1=acs_sb.ap[:T, :], op=ALU.subtract), reads=[b_al, acs_sb], writes=[tl])
    s.op("act", lambda e: e.activation(out=tl.ap[:T, :], in_=tl.ap[:T, :], func=AF.Exp), reads=[tl], writes=[tl])
    s.op("dve", lambda e: e.tensor_tensor(out=v3(xdtt.ap[:T, :]), in0=v3(xdt.ap[:T, :]), in1=tl.ap[:T, :].unsqueeze(2).to_broadcast([T, 8, 64]), op=ALU.mult),
         reads=[xdt, tl], writes=[xdtt])
    if SUB < 10:
        return
    b_st = cx.bank()
    s.op("pe", lambda e: e.matmul(b_st.ap, lhsT=B_tm.ap[:T, :], rhs=xdtt.ap[:T, :], start=True, stop=True), reads=[B_tm, xdtt], writes=[b_st])
    s.op("dve", lambda e: e.tensor_tensor(out=v3(S.ap), in0=v3(S.ap), in1=cd.ap.unsqueeze(2).to_broadcast([128, 8, 64]), op=ALU.mult), reads=[S, cd], writes=[S])
    s.op("dve", lambda e: e.tensor_tensor(out=S.ap, in0=S.ap, in1=b_st.ap, op=ALU.add), reads=[S, b_st], writes=[S])
    s.op("act", lambda e: e.copy(out=S_bf.ap, in_=S.ap), reads=[S], writes=[S_bf])


def pass_s(cx, c, SEQ, DB, x_all, w_s, nrm, cw_d, cb_d, dtb_d, alog_d, dsk_d, sconv_d, sssm_d, Yn, conv_o, ssm_o):
    s = cx.s
    NTOK = SEQ + DB * 16
    with ExitStack() as es:
        ssd_consts(cx, es, c)
        nrm_t = cx.sb(es, "s_nrm", [128, KC], F32)
        s.dma("sp", nrm_t.ap, nrm, writes=[nrm_t])
        ws = cx.sb(es, "ws_pre", [128, KC, 1288], BF16)
        with ExitStack() as es_st:
            stage = cx.sb(es_st, "s_stage", [128, 1288], F32)
            for kc in range(KC):
                s.dma("sp", stage.ap, w_s[:, kc, :], writes=[stage])
                s.op("act", lambda e, kc=kc: e.activation(out=ws.ap[:, kc, :], in_=stage.ap, func=AF.Copy, scale=nrm_t.ap[:, kc:kc + 1]),
                     reads=[stage, nrm_t], writes=[ws])
        s.barrier()
        cw = cx.sb(es, "s_cw", [128, 6, 4], F32)
        s.dma("sp", cw.ap, cw_d, writes=[cw])
        cb = cx.sb(es, "s_cb", [128, 6], F32)
        s.dma("sp", cb.ap, cb_d, writes=[cb])
        W = {}
        dtb = cx.sb(es, "s_dtb", [128, 8], F32)
        s.dma("sp", dtb.ap, dtb_d[0:1, :].to_broadcast([128, 8]), writes=[dtb])
        W["Ab"] = cx.sb(es, "s_Ab", [128, 8], F32)
        s.dma("sp", W["Ab"].ap, alog_d[0:1, :].to_broadcast([128, 8]), writes=[W["Ab"]])
        s.op("act", lambda e: e.activation(out=W["Ab"].ap, in_=W["Ab"].ap, func=AF.Exp), reads=[W["Ab"]], writes=[W["Ab"]])
        s.op("dve", lambda e: e.tensor_scalar(out=W["Ab"].ap, in0=W["Ab"].ap, scalar1=-1.0, scalar2=None, op0=ALU.mult), reads=[W["Ab"]], writes=[W["Ab"]])
        W["Dsk"] = cx.sb(es, "s_Dsk", [128, 8], F32)
        s.dma("sp", W["Dsk"].ap, dsk_d[0:1, :].to_broadcast([128, 8]), writes=[W["Dsk"]])
        W2 = [dict(W), dict(W)]
        for nm, shp, dt_ in (("dtA", [128, 8], F32), ("acs_sb", [128, 8], F32), ("eacs", [128, 8], F32), ("acsT_sb", [8, 128], F32),
                             ("cbU", [128, 128], F32), ("Em", [128, 512], F32),
                             ("Ee", [128, 512], F32), ("wts", [128, 1024], BF16), ("xdt", [128, 512], BF16), ("xdtt", [128, 512], BF16),
                             ("t1", [128, 512], F32), ("t2", [128, 512], F32), ("junkf", [128, 512], F32), ("ssq", [128, 1], F32),
                             ("rstd", [128, 1], F32), ("cd", [128, 8], F32), ("tl", [128, 8], F32),
                             ("dtA_h", [128, 8], BF16), ("dtA_l", [128, 8], BF16), ("acs_h", [128, 8], BF16), ("acs_l", [128, 8], BF16),
                             ("acsT_h", [8, 128], BF16), ("acsT_l", [8, 128], BF16), ("nacsT_h", [8, 128], BF16), ("nacsT_l", [8, 128], BF16),
                             ("Dm_h", [8, 1024], BF16), ("Dm_l", [8, 1024], BF16)):
            for wi in range(2):
                W2[wi][nm] = cx.sb(es, "w%d_" % wi + nm, shp, dt_)
        yn_l = [cx.sb(es, "w_yn%d" % i, [128, 512], BF16) for i in range(2)]
        ynT_l = [cx.sb(es, "w_ynT%d" % i, [128, 4, 128], BF16) for i in range(2)]
        rot = [0]
        for wi in range(2):
            W2[wi]["yn"] = yn_l
            W2[wi]["ynT"] = ynT_l
            W2[wi]["ynrot"] = rot
        tsrot = [0]
        TW = {"parts": [cx.sb(es, "tw_p%d" % i, [128, 128], BF16) for i in range(3)] + [cx.sb(es, "tw_r", [128, 128], F32)]}
        xt = [cx.sb(es, "s_xt%d" % i, [128, D], F32) for i in range(2)]
        xs4 = [cx.sb(es, "s_xs%d" % i, [128, D], BF16) for i in range(2)] * 2
        junk = cx.sb(es, "s_junk", [128, D], BF16)
        hT2 = [cx.sb(es, "s_hT0", [128, KC, 512], BF16)] * 2
        hTc = [hT2[0]]
        ssq = cx.sb(es, "s_ssq", [128, 1], F32)
        rstd = cx.sb(es, "s_rstd", [128, 1], F32)
        xp = [cx.sb(es, "s_xp%d" % i, [128, 6, 515], F32) for i in range(2)]
        acc = cx.sb(es, "s_acc", [128, 512], F32)
        xcT = cx.sb(es, "s_xcT", [128, 6, 512], BF16)
        x_tm2 = [cx.sb(es, "s_xtm%d" % i, [128, 512], BF16) for i in range(2)]
        B_tm2 = [cx.sb(es, "s_Btm%d" % i, [128, 128], BF16) for i in range(2)]
        zs2 = [cx.sb(es, "s_zs%d" % i, [128, 512], F32) for i in range(2)]
        dt_sb2 = [cx.sb(es, "s_dt%d" % i, [128, 8], F32) for i in range(2)]
        S = cx.sb(es, "s_S", [128, 512], F32)
        S_bf = cx.sb(es, "s_Sbf", [128, 512], BF16)
        so = cx.sb(es, "s_so", [128, 4, 128], F32)
        sc = cx.sb(es, "s_sc", [128, 768], F32)
        cso = cx.sb(es, "s_cso", [128, 768], F32)
        csi = cx.sb(es, "s_csi", [128, 6, 48], F32)

        def proj_fm(ST):
            pass

        def conv_chunk(xv, ch, ncols_view, out_ap_fn):
            a = out_ap_fn("acc")
            s.op("dve", lambda e: e.tensor_scalar(out=a, in0=xv(0), scalar1=cw.ap[:, ch, 0:1], scalar2=None, op0=ALU.mult), reads=[xv.tl, cw], writes=[acc])
            for k in range(1, 4):
                s.op("dve", lambda e, k=k: e.scalar_tensor_tensor(out=a, in0=xv(k), scalar=cw.ap[:, ch, k:k + 1], in1=a, op0=ALU.mult, op1=ALU.add),
                     reads=[xv.tl, cw, acc], writes=[acc])
            s.op("act", lambda e: e.activation(out=out_ap_fn("out"), in_=a, func=AF.Silu, bias=cb.ap[:, ch:ch + 1]), reads=[acc, cb], writes=[xcT])

        def token_front(col0, T):
            pi_ = tsrot[0] % 2
            tsrot[0] += 1
            W, x_tm, B_tm, zs, dt_sb, hT = W2[pi_], x_tm2[pi_], B_tm2[pi_], zs2[pi_], dt_sb2[pi_], hTc[0]
            hs = lambda kc: hT.ap[:, kc, col0:col0 + T]
            bz = cx.bank()
            mm_acc(cx, bz.ap[:T, :], [(hs(kc), ws.ap[:, kc, 0:512]) for kc in range(KC)], [hT, ws], bz)
            bd = cx.bank()
            mm_acc(cx, bd.ap[:T, :8], [(hs(kc), ws.ap[:, kc, 1280:1288]) for kc in range(KC)], [hT, ws], bd)
            b = cx.bank()
            bv = b.ap.bitcast(BF16)
            for j in range(5):
                s.op("pe", lambda e, j=j: e.transpose(out=bv[:T, j * 128:(j + 1) * 128], in_=xcT.ap[:, j, col0:col0 + T], identity=c["ident"].ap),
                     reads=[xcT, c["ident"]], writes=[b])
            s.op("act", lambda e: e.activation(out=zs.ap[:T, :], in_=bz.ap[:T, :], func=AF.Silu), reads=[bz], writes=[zs])
            s.op("dve", lambda e: e.tensor_tensor(out=dt_sb.ap[:T, :], in0=bd.ap[:T, :8], in1=dtb.ap[:T, :], op=ALU.add), reads=[bd, dtb], writes=[dt_sb])
            s.op("act", lambda e: e.activation(out=dt_sb.ap[:T, :], in_=dt_sb.ap[:T, :], func=AF.Exp), reads=[dt_sb], writes=[dt_sb])
            s.op("act", lambda e: e.activation(out=dt_sb.ap[:T, :], in_=dt_sb.ap[:T, :], func=AF.Ln, bias=1.0), reads=[dt_sb], writes=[dt_sb])
            evac(cx, x_tm.ap[:T, :], x_tm, bv[:T, 0:512], b)
            evac(cx, B_tm.ap[:T, :], B_tm, bv[:T, 512:640], b)
            return (W, x_tm, B_tm, zs, dt_sb, col0, T)

        def token_back(fr, yn_rows):
            W, x_tm, B_tm, zs, dt_sb, col0, T = fr
            if STOP < 4:
                return
            ssd_tile(cx, c, W, T, x_tm, B_tm, xcT.ap[:, 4, col0:col0 + T], xcT.ap[:, 5, col0:col0 + T], xcT, dt_sb, zs, S, S_bf, yn_rows)

        def token_side(col0, T, yn_rows):
            token_back(token_front(col0, T), yn_rows)

        def state_out(idx):
            for j in range(4):
                transpose_f32(cx, c, TW, S.ap[:, j * 128:(j + 1) * 128], S, 128, 128, so.ap[:, j, :], so)
            s.dma("pool", ssm_o[idx].rearrange("(j p) n -> p j n", p=128), so.ap, reads=[so])

        if STOP < 2:
            return
        s.op("dve", lambda e: e.memset(S.ap, 0.0), writes=[S])
        s.op("dve", lambda e: e.memset(S_bf.ap, 0.0), writes=[S_bf])
        s.op("dve", lambda e: e.memset(xp[1].ap[:, :, 0:515], 0.0), writes=[xp[1]])
        nstp = SEQ // 512
        for st in range(nstp):
            T0 = st * 512
            cur, prv = xp[st % 2], xp[(st + 1) % 2]
            hT = hT2[st % 2]
            hTc[0] = hT
            for sub in range(4):
                load_norm_T(cx, c, x_all[T0 + sub * 128:T0 + (sub + 1) * 128, :], 128, xt[sub % 2], xs4[sub], junk, hT, sub * 128, (ssq, rstd))
            s.op("dve", lambda e: e.tensor_copy(out=cur.ap[:, :, 0:3], in_=prv.ap[:, :, 512:515]), reads=[prv], writes=[cur])
            for ch in range(6):
                b = cx.bank()
                mm_acc(cx, b.ap, [(ws.ap[:, kc, 512 + ch * 128:512 + (ch + 1) * 128], hT.ap[:, kc, :]) for kc in range(KC)], [ws, hT], b)
                evac(cx, cur.ap[:, ch, 3:515], cur, b.ap, b)
            for ch in range(6):
                xv = lambda k, ch=ch: cur.ap[:, ch, k:k + 512]
                xv.tl = cur
                conv_chunk(xv, ch, 512, lambda w, ch=ch: acc.ap if w == "acc" else xcT.ap[:, ch, :])
            if STOP < 3:
                continue
            fr = token_front(0, 128)
            for sub in range(4):
                nxt = token_front((sub + 1) * 128, 128) if sub < 3 else None
                token_back(fr, (Yn, NTOK // 256, T0 + sub * 128))
                fr = nxt
        last = xp[(nstp - 1) % 2]
        for ch in range(6):
            s.dma("pool", conv_o[0][:, ch * 128:(ch + 1) * 128].rearrange("k p -> p k"), last.ap[:, ch, 512:515], reads=[last], allow_slow_non_contiguous=True)
        state_out(0)
        if STOP < 6:
            return
        NS = DB * 16
        nsub = (NS + 127) // 128
        hT = hT2[0]
        hTc[0] = hT
        for sub in range(nsub):
            nt = min(128, NS - sub * 128)
            load_norm_T(cx, c, x_all[SEQ + sub * 128:SEQ + sub * 128 + nt, :], nt, xt[sub % 2], xs4[sub], junk, hT, sub * 128, (ssq, rstd))
        xq = xp[0]
        xq4 = xq.ap[:, :, 0:DB * 19].rearrange("p c (b t) -> p c b t", t=19)
        s.dma("sp", sc.ap[:DB * 3, :], sconv_d.rearrange("b k f -> (b k) f"), writes=[sc])
        for ch in range(6):
            transpose_f32(cx, c, TW, sc.ap[:DB * 3, ch * 128:(ch + 1) * 128], sc, DB * 3, 128, csi.ap[:, ch, :DB * 3], csi)
            evac(cx, xq4[:, ch, :, 0:3], xq, csi.ap[:, ch, :DB * 3].rearrange("p (b k) -> p b k", k=3), csi)
        for ch in range(6):
            b = cx.bank()
            mm_acc(cx, b.ap[:, :NS], [(ws.ap[:, kc, 512 + ch * 128:512 + (ch + 1) * 128], hT.ap[:, kc, :NS]) for kc in range(KC)], [ws, hT], b)
            evac(cx, xq4[:, ch, :, 3:19], xq, b.ap[:, :NS].rearrange("p (b t) -> p b t", t=16), b)
        for ch in range(6):
            xv = lambda k, ch=ch: xq4[:, ch, :, k:k + 16]
            xv.tl = xq
            conv_chunk(xv, ch, NS, lambda w, ch=ch: (acc.ap[:, :NS].rearrange("p (b t) -> p b t", t=16) if w == "acc"
                                                      else xcT.ap[:, ch, :NS].rearrange("p (b t) -> p b t", t=16)))
        for ch in range(6):
            evac(cx, csi.ap[:, ch, :DB * 3].rearrange("p (b k) -> p b k", k=3), csi, xq4[:, ch, :, 16:19], xq)
            transpose_f32(cx, c, TW, csi.ap[:, ch, :DB * 3], csi, 128, DB * 3, cso.ap[:DB * 3, ch * 128:(ch + 1) * 128], cso)
        s.dma("pool", conv_o[1:1 + DB].rearrange("b k f -> (b k) f"), cso.ap[:DB * 3, :], reads=[cso])
        for b_ in range(DB):
            s.dma("sp", so.ap, sssm_d[b_].rearrange("(j p) n -> p j n", p=128), writes=[so])
            for j in range(4):
                transpose_f32(cx, c, TW, so.ap[:, j, :], so, 128, 128, S.ap[:, j * 128:(j + 1) * 128], S)
            s.op("act", lambda e: e.copy(out=S_bf.ap, in_=S.ap), reads=[S], writes=[S_bf])
            token_side(b_ * 16, 16, (Yn, NTOK // 256, SEQ + b_ * 16))
            state_out(1 + b_)


def build_l1(SEQ, DB, PAST, fused=False):
    NTOK = SEQ + DB * 16
    nc = bass.Bass("TRN2", target_bir_lowering=False)
    x_all = nc.dram_tensor("x_all", [NTOK, D], F32, kind="ExternalInput").ap()
    w_m = nc.dram_tensor("w_m", [128, KC, 1344], F32, kind="ExternalInput").ap()
    w_q = nc.dram_tensor("w_q", [128, 4, 384], F32, kind="ExternalInput").ap()
    w_kv = nc.dram_tensor("w_kv", [128, 4, 512], F32, kind="ExternalInput").ap()
    nrm = nc.dram_tensor("nrm", [128, KC], F32, kind="ExternalInput").ap()
    qnw = nc.dram_tensor("qnw", [128, 4], F32, kind="ExternalInput").ap()
    kvw = nc.dram_tensor("kvw", [1, 512], F32, kind="ExternalInput").ap()
    cache_kv = nc.dram_tensor("cache_kv", [DB, PAST, 512], F32, kind="ExternalInput").ap()
    cache_kr = nc.dram_tensor("cache_kr", [DB, PAST, 64], F32, kind="ExternalInput").ap()
    kvlat = nc.dram_tensor("kvlat", [NTOK, 512], F32, kind="ExternalOutput").ap()
    krope = nc.dram_tensor("krope", [NTOK, 64], F32, kind="ExternalOutput").ap()
    NB = NTOK // 256
    AT = (nc.dram_tensor("ATc", [2 * NB, 128, 256], BF16).ap() if fused
          else nc.dram_tensor("ATc", [2 * NB, 128, 256], BF16, kind="ExternalOutput").ap())
    w_s = nc.dram_tensor("w_s", [128, KC, 1288], F32, kind="ExternalInput").ap()
    cw_d = nc.dram_tensor("cw", [128, 6, 4], F32, kind="ExternalInput").ap()
    cb_d = nc.dram_tensor("cb", [128, 6], F32, kind="ExternalInput").ap()
    dtb_d = nc.dram_tensor("dtb", [1, 8], F32, kind="ExternalInput").ap()
    alog_d = nc.dram_tensor("alog", [1, 8], F32, kind="ExternalInput").ap()
    dsk_d = nc.dram_tensor("dsk", [1, 8], F32, kind="ExternalInput").ap()
    sconv_d = nc.dram_tensor("sconv", [DB, 3, 768], F32, kind="ExternalInput").ap()
    sssm_d = nc.dram_tensor("sssm", [DB, 512, 128], F32, kind="ExternalInput").ap()
    Yn = (nc.dram_tensor("YTc", [4 * NB, 128, 256], BF16).ap() if fused
          else nc.dram_tensor("YTc", [4 * NB, 128, 256], BF16, kind="ExternalOutput").ap())
    if fused:
        GY = nc.dram_tensor("GY", [4 * NB, 1024, 256], BF16).ap()
        GA = nc.dram_tensor("GA", [2 * NB, 1024, 256], BF16).ap()
        y2 = nc.dram_tensor("y2", [SEQ // NCORES + DB * 16 // NCORES, D], F32, kind="ExternalOutput").ap()
        l2w = declare_l2_weights(nc)
        l2s = declare_l2_scratch(nc)
    conv_o = nc.dram_tensor("conv_o", [1 + DB, 3, 768], F32, kind="ExternalOutput").ap()
    ssm_o = nc.dram_tensor("ssm_o", [1 + DB, 512, 128], F32, kind="ExternalOutput").ap()
    QT = nc.dram_tensor("QT", [128, 2, NTOK], BF16).ap()
    QRT = nc.dram_tensor("QRT", [64, 2, NTOK], BF16).ap()
    KT = nc.dram_tensor("KT", [128, 2, NTOK], BF16).ap()
    KRT = nc.dram_tensor("KRT", [64, NTOK], BF16).ap()
    V = nc.dram_tensor("V", [NTOK, 256], BF16).ap()
    GT = nc.dram_tensor("GT", [128, 2, NTOK], BF16).ap()
    KTc = nc.dram_tensor("KTc", [128, 2, DB * PAST], BF16).ap()
    KRTc = nc.dram_tensor("KRTc", [64, DB * PAST], BF16).ap()
    Vc = nc.dram_tensor("Vc", [DB * PAST, 256], BF16).ap()
    cx = Ctx(nc)
    s = cx.s
    SCALE = 192.0 ** -0.5
    with ExitStack() as es0:
        c = make_consts(cx, es0)
        setup_small_consts(cx, es0, c)
        with ExitStack() as es:
            nrm_t = cx.sb(es, "nrm_t", [128, KC], F32)
            s.dma("sp", nrm_t.ap, nrm, writes=[nrm_t])
            qnw_t = cx.sb(es, "qnw_t", [128, 4], F32)
            s.dma("sp", qnw_t.ap, qnw, writes=[qnw_t])
            s.op("dve", lambda e: e.tensor_scalar(out=qnw_t.ap, in0=qnw_t.ap, scalar1=SCALE, scalar2=None, op0=ALU.mult), reads=[qnw_t], writes=[qnw_t])
            kvw_b = cx.sb(es, "kvw_b", [128, 512], F32)
            s.dma("sp", kvw_b.ap, kvw[0:1, :].to_broadcast([128, 512]), writes=[kvw_b])
            stage = cx.sb(es, "stage", [128, 1344], F32)
            wm = load_weights_bf16(cx, es, "wm", w_m, KC, 1344, scale_tl=nrm_t, stage=stage)
            wq = load_weights_bf16(cx, es, "wq", w_q, 4, 384, scale_tl=qnw_t, stage=stage)
            wkv = load_weights_bf16(cx, es, "wkv", w_kv, 4, 512, stage=stage)
            xt = [cx.sb(es, "xt%d" % i, [128, D], F32) for i in range(2)]
            xs4 = [cx.sb(es, "xs%d" % i, [128, D], BF16) for i in range(4)]
            junk = cx.sb(es, "junk", [128, D], BF16)
            junkf = cx.sb(es, "junkf", [128, 512], F32)
            hT2 = [cx.sb(es, "hT%d" % i, [128, KC, 512], BF16) for i in range(2)]
            ssq = cx.sb(es, "ssq", [128, 1], F32)
            rstd = cx.sb(es, "rstd", [128, 1], F32)
            ssq2 = cx.sb(es, "ssq2", [128, 1], F32)
            rstd2 = cx.sb(es, "rstd2", [128, 1], F32)
            pos = cx.sb(es, "pos", [128, 4], F32)
            tabs = [cx.sb(es, "tab%d" % i, [128, 4, 32], F32 if i != 2 else I32) for i in range(6)]
            cos, sin = tabs[3], tabs[4]
            ckvn = [cx.sb(es, "ckvn%d" % i, [128, 512], F32) for i in range(2)]
            ckvb2 = [cx.sb(es, "ckvb%d" % i, [128, 512], BF16) for i in range(2)]
            cqnb2 = [cx.sb(es, "cqnb%d" % i, [128, 512], BF16) for i in range(2)]
            krb2 = [cx.sb(es, "krb2_%d" % i, [128, 64], BF16) for i in range(2)]
            krt2 = cx.sb(es, "krt2", [128, 128], F32)
            kro = [cx.sb(es, "kro%d" % i, [128, 64], F32) for i in range(2)]
            krt = cx.sb(es, "krt", [128, 128], F32)
            krb = cx.sb(es, "krb", [128, 64], BF16)
            qro = cx.sb(es, "qro", [128, 128], F32)
            qrb = cx.sb(es, "qrb", [128, 128], BF16)
            cqnT = cx.sb(es, "cqnT", [128, 4, 512], BF16)
            ckvT = cx.sb(es, "ckvT", [128, 4, 512], BF16)
            krT = cx.sb(es, "krT", [64, 512], BF16)
            qrT = cx.sb(es, "qrT", [64, 2, 512], BF16)
            qst = cx.sb(es, "qst", [128, 2, 512], BF16)
            kst = cx.sb(es, "kst", [128, 2, 512], BF16)
            gst = cx.sb(es, "gst", [128, 2, 512], BF16)
            vst = cx.sb(es, "vst", [128, 4, 256], BF16)
            cst = [cx.sb(es, "cst%d" % i, [128, 4, 512], F32) for i in range(2)]
            cstb = cx.sb(es, "cstb", [128, 4, 512], BF16)
            ckr = cx.sb(es, "ckr", [128, 4, 64], F32)
            ckrb = cx.sb(es, "ckrb", [128, 4, 64], BF16)
            nst = (NTOK + 511) // 512
            for st in range(nst if "M" in PH else 0):
                T0 = st * 512
                ST = min(512, NTOK - T0)
                nsub = ST // 128
                is_prompt = T0 < SEQ
                hT = hT2[st % 2]
                for sub in range(nsub):
                    load_norm_T(cx, c, x_all[T0 + sub * 128:T0 + (sub + 1) * 128, :], 128, xt[sub % 2], xs4[sub], junk, hT, sub * 128, (ssq, rstd))
                if is_prompt:
                    for sub in range(nsub):
                        s.op("dve", lambda e, sub=sub: e.tensor_scalar(out=pos.ap[:, sub:sub + 1], in0=c["pf"].ap, scalar1=float(T0 + sub * 128), scalar2=None, op0=ALU.add),
                             reads=[c["pf"]], writes=[pos])
                else:
                    for sub in range(nsub):
                        s.op("dve", lambda e, sub=sub: e.tensor_scalar(out=pos.ap[:, sub:sub + 1], in0=c["pm16"].ap, scalar1=float(PAST), scalar2=None, op0=ALU.add),
                             reads=[c["pm16"]], writes=[pos])
                rope_tables(cx, c, tabs, pos, nsub)
                def stA(sub):
                    hs = lambda kc: hT.ap[:, kc, sub * 128:(sub + 1) * 128]
                    o3 = 3 * (sub % 2)
                    bq = cx.banks[o3]
                    mm_acc(cx, bq.ap, [(hs(kc), wm.ap[:, kc, 0:512]) for kc in range(KC)], [hT, wm], bq)
                    b1 = cx.banks[o3 + 1]
                    mm_acc(cx, b1.ap, [(hs(kc), wm.ap[:, kc, 512:1024]) for kc in range(KC)], [hT, wm], b1)
                    b2 = cx.banks[o3 + 2]
                    mm_acc(cx, b2.ap[:, :64], [(hs(kc), wm.ap[:, kc, 1024:1088]) for kc in range(KC)], [hT, wm], b2)
                    return bq, b1, b2

                def stB(sub, banks):
                    bq, b1, b2 = banks
                    t0 = T0 + sub * 128
                    cqb, ckb, krb_ = cqnb2[sub % 2], ckvb2[sub % 2], krb2[sub % 2]
                    rms_rstd(cx, c, bq.ap, bq, 128, 512, junkf, ssq2, rstd2)
                    s.op("dve", lambda e: e.tensor_scalar(out=cqb.ap, in0=bq.ap, scalar1=rstd2.ap, scalar2=None, op0=ALU.mult),
                         reads=[bq, rstd2], writes=[cqb])
                    rms_rstd(cx, c, b1.ap, b1, 128, 512, junkf, ssq2, rstd2)
                    ck = ckvn[sub % 2]
                    s.op("dve", lambda e: e.scalar_tensor_tensor(out=ck.ap, in0=b1.ap, scalar=rstd2.ap, in1=kvw_b.ap, op0=ALU.mult, op1=ALU.mult),
                         reads=[b1, rstd2, kvw_b], writes=[ck])
                    s.dma("pool", kvlat[t0:t0 + 128, :], ck.ap, reads=[ck])
                    s.op("act", lambda e: e.copy(out=ckb.ap, in_=ck.ap), reads=[ck], writes=[ckb])
                    ko = kro[sub % 2]
                    apply_rope(cx, b2.ap[:, :64].rearrange("p (h d) -> p h d", h=1), b2, 1, cos.ap[:, sub, :], sin.ap[:, sub, :], [cos, sin], ko, krt, 128)
                    s.dma("pool", krope[t0:t0 + 128, :], ko.ap, reads=[ko])
                    s.op("act", lambda e: e.copy(out=krb_.ap, in_=ko.ap), reads=[ko], writes=[krb_])

                def stC(sub):
                    cqb, ckb, krb_ = cqnb2[sub % 2], ckvb2[sub % 2], krb2[sub % 2]
                    transpose_to(cx, c, cqb, lambda j: cqb.ap[:, j * 128:(j + 1) * 128], 4, 128,
                                 lambda j: cqnT.ap[:, j, sub * 128:(sub + 1) * 128], cqnT)
                    transpose_to(cx, c, ckb, lambda j: ckb.ap[:, j * 128:(j + 1) * 128], 4, 128,
                                 lambda j: ckvT.ap[:, j, sub * 128:(sub + 1) * 128], ckvT)
                    transpose_to(cx, c, krb_, lambda j: krb_.ap[:, :], 1, 128, lambda j: krT.ap[:, sub * 128:(sub + 1) * 128], krT, rows=64)
                    b3 = cx.bank()
                    mm_acc(cx, b3.ap[:, :128], [(cqnT.ap[:, kc, sub * 128:(sub + 1) * 128], wq.ap[:, kc, 256:384]) for kc in range(4)], [cqnT, wq], b3)
                    apply_rope(cx, b3.ap[:, :128].rearrange("p (h d) -> p h d", h=2), b3, 2, cos.ap[:, sub, :], sin.ap[:, sub, :], [cos, sin], qro, krt2, 128)
                    s.op("act", lambda e: e.copy(out=qrb.ap, in_=qro.ap), reads=[qro], writes=[qrb])
                    transpose_to(cx, c, qrb, lambda j: qrb.ap[:, j * 64:(j + 1) * 64], 2, 128,
                                 lambda j: qrT.ap[:, j, sub * 128:(sub + 1) * 128], qrT, rows=64)

                cx.pool = [6, 7]
                bk_ = stA(0)
                for sub in range(nsub):
                    nxt = stA(sub + 1) if sub + 1 < nsub else None
                    stB(sub, bk_)
                    stC(sub)
                    bk_ = nxt
                cx.pool = list(range(8))
                for h in range(2):
                    b = cx.bank()
                    mm_acc(cx, b.ap[:, :ST], [(wq.ap[:, kc, h * 128:(h + 1) * 128], cqnT.ap[:, kc, :ST]) for kc in range(4)], [wq, cqnT], b)
                    evac(cx, qst.ap[:, h, :ST], qst, b.ap[:, :ST], b)
                s.dma("pool", QT[:, :, T0:T0 + ST], qst.ap[:, :, :ST], reads=[qst])
                s.dma("pool", QRT[:, :, T0:T0 + ST], qrT.ap[:, :, :ST], reads=[qrT])
                s.dma("pool", KRT[:, T0:T0 + ST], krT.ap[:, :ST], reads=[krT])
                for h in range(2):
                    b = cx.bank()
                    mm_acc(cx, b.ap[:, :ST], [(wm.ap[:, kc, 1088 + h * 128:1088 + (h + 1) * 128], hT.ap[:, kc, :ST]) for kc in range(KC)], [wm, hT], b)
                    s.op("act", lambda e, h=h, b=b: e.activation(out=gst.ap[:, h, :ST], in_=b.ap[:, :ST], func=AF.Silu), reads=[b], writes=[gst])
                s.dma("pool", GT[:, :, T0:T0 + ST], gst.ap[:, :, :ST], reads=[gst])
                kv_from_latent(cx, c, ckvT, None, wkv, ST, nsub, kst, vst, KT, V, T0)
            for b_ in range(DB if "C" in PH else 0):
                for st in range(PAST // 512):
                    r0 = st * 512
                    cs = cst[st % 2]
                    s.dma("sp", cs.ap, cache_kv[b_, r0:r0 + 512, :].rearrange("(n p) f -> p n f", p=128), writes=[cs])
                    s.op("act", lambda e, cs=cs: e.copy(out=cstb.ap, in_=cs.ap), reads=[cs], writes=[cstb])
                    for sub in range(4):
                        transpose_to(cx, c, cstb, lambda j, sub=sub: cstb.ap[:, sub, j * 128:(j + 1) * 128], 4, 128,
                                     lambda j, sub=sub: ckvT.ap[:, j, sub * 128:(sub + 1) * 128], ckvT)
                    kv_from_latent(cx, c, ckvT, None, wkv, 512, 4, kst, vst, KTc, Vc, b_ * PAST + r0)
                    s.dma("sp", ckr.ap, cache_kr[b_, r0:r0 + 512, :].rearrange("(n p) f -> p n f", p=128), writes=[ckr])
                    s.op("act", lambda e: e.copy(out=ckrb.ap, in_=ckr.ap), reads=[ckr], writes=[ckrb])
                    transpose_to(cx, c, ckrb, lambda j: ckrb.ap[:, j, :], 4, 128, lambda j: krT.ap[:, j * 128:(j + 1) * 128], krT, rows=64)
                    s.dma("pool", KRTc[:, b_ * PAST + r0:b_ * PAST + r0 + 512], krT.ap[:, :512], reads=[krT])
        s.barrier()
        with ExitStack() as es:
            if "A" in PH:
                attention_phase(cx, c, es, SEQ, DB, PAST, QT, QRT, KT, KRT, V, GT, KTc, KRTc, Vc, AT)
        s.barrier()
        if "S" in PH:
            pass_s(cx, c, SEQ, DB, x_all, w_s, nrm, cw_d, cb_d, dtb_d, alog_d, dsk_d, sconv_d, sssm_d, Yn, conv_o, ssm_o)
        s.barrier()
        if fused:
            for i in range(2 * NB):
                s.coll(AT[i], GA[i])
            for i in range(4 * NB):
                s.coll(Yn[i], GY[i])
            s.barrier()
            l2_phase(cx, c, es0, SEQ, DB, x_all, GY, GA, *l2w, y2, *l2s)
    s.emit()
    return nc


def l2_phase(cx, c, es0, SEQ, DB, x_all, GY, GA, wg, wssm, wmla, wout, nrm, snw, fnw, y2, WGS, WGM, WS, WM):
    nc, s = cx.nc, cx.s
    NS = DB * 16
    NTOK = SEQ + NS
    NB = NTOK // 256
    PB = SEQ // 256 // NCORES
    SPC = NS // NCORES
    tbs = SEQ // 256
    nrm_t = cx.sb(es0, "l2_nrm", [128, KC], F32)
    s.dma("sp", nrm_t.ap, nrm, writes=[nrm_t])
    snw_t = cx.sb(es0, "l2_snw", [128, 32], F32)
    s.dma("sp", snw_t.ap, snw, writes=[snw_t])
    wo = cx.sb(es0, "l2_wo", [128, 16, 2048], BF16)
    with ExitStack() as es:
        stg = [cx.sb(es, "stg%d" % i, [128, 32, 128], F32) for i in range(2)]
        stb = [cx.sb(es, "stb%d" % i, [128, 32, 128], BF16) for i in range(2)]
        k = 0
        for jj in range(16):
            for (src, c0, nk, scl, dst) in ((wg, jj * 128, 16, nrm_t, WGS), (wg, 2048 + jj * 128, 16, nrm_t, WGM),
                                           (wssm, jj * 128, 32, snw_t, WS), (wmla, jj * 128, 16, None, WM)):
                sg, sbb = stg[k % 2], stb[k % 2]
                k += 1
                s.dma("sp", sg.ap[:, :nk, :], src[:, :, c0:c0 + 128], writes=[sg])
                if scl is not None:
                    s.op("dve", lambda e, sg=sg, sbb=sbb, nk=nk, scl=scl: e.tensor_tensor(out=sbb.ap[:, :nk, :], in0=sg.ap[:, :nk, :],
                                                                                            in1=scl.ap[:, :nk].unsqueeze(2).to_broadcast([128, nk, 128]), op=ALU.mult),
                         reads=[sg, scl], writes=[sbb])
                else:
                    s.op("act", lambda e, sg=sg, sbb=sbb, nk=nk: e.copy(out=sbb.ap[:, :nk, :], in_=sg.ap[:, :nk, :]), reads=[sg], writes=[sbb])
                s.dma("pool", dst[jj], sbb.ap[:, :nk, :], reads=[sbb])
        for kc in range(16):
            sg = stg[kc % 2]
            s.dma("sp", sg.ap.rearrange("p a b -> p (a b)")[:, :2048], wout[:, kc, :], writes=[sg])
            s.op("act", lambda e, sg=sg, kc=kc: e.copy(out=wo.ap[:, kc, :], in_=sg.ap.rearrange("p a b -> p (a b)")[:, :2048]), reads=[sg], writes=[wo])
    s.barrier()
    with ExitStack() as es:
        fnw_b = cx.sb(es, "fnw_b", [128, D], F32)
        s.dma("sp", fnw_b.ap, fnw[0:1, :].to_broadcast([128, D]), writes=[fnw_b])
        xt = [cx.sb(es, "l2_xt0", [128, D], F32)]
        xs = cx.sb(es, "l2_xs", [128, D], BF16)
        junk = cx.sb(es, "l2_junk", [128, D], BF16)
        xr = cx.sb(es, "l2_xr", [128, D], F32)
        xo = cx.sb(es, "l2_xo", [128, D], F32)
        hT = cx.sb(es, "l2_hT", [128, KC, 256], BF16)
        yT = cx.sb(es, "l2_yT", [128, 32, 256], BF16)
        aT = cx.sb(es, "l2_aT", [128, 16, 256], BF16)
        mixT = cx.sb(es, "l2_mixT", [128, 16, 256], BF16)
        wgs = [cx.sb(es, "wgs%d" % i, [128, 16, 128], BF16) for i in range(2)]
        wgm = [cx.sb(es, "wgm%d" % i, [128, 16, 128], BF16) for i in range(2)]
        wsb = [cx.sb(es, "wsb%d" % i, [128, 32, 128], BF16) for i in range(2)]
        wmb = [cx.sb(es, "wmb%d" % i, [128, 16, 128], BF16) for i in range(2)]
        sg1 = cx.sb(es, "sg1", [128, 256], F32)
        sg2 = cx.sb(es, "sg2", [128, 256], F32)
        ssq = cx.sb(es, "l2_ssq", [128, 1], F32)
        rstd = cx.sb(es, "l2_rstd", [128, 1], F32)
        ssq2 = cx.sb(es, "l2_ssq2", [128, 1], F32)
        rstd2 = cx.sb(es, "l2_rstd2", [128, 1], F32)
        def rv(rk, key, mul, add=0):
            if key not in rk:
                rk[key] = rk["r"] * mul + add if add else rk["r"] * mul
            return rk[key]
        GYr = nc.dram_tensor("GYr", [4, PB, 1024 * 256], BF16).ap()
        GAr = nc.dram_tensor("GAr", [2, PB, 1024 * 256], BF16).ap()
        GYs = nc.dram_tensor("GYs", [4, 1024, SPC], BF16).ap()
        GAs = nc.dram_tensor("GAs", [2, 1024, SPC], BF16).ap()
        X2 = nc.dram_tensor("X2", [PB * 256 + SPC, D], F32).ap()
        s.dma("sp", GYr, lambda rk: GY.rearrange("(k n) r t -> k n (r t)", k=4)[:, bass.ds(rv(rk, "pb", PB), PB), :])
        s.dma("sp", GAr, lambda rk: GA.rearrange("(k n) r t -> k n (r t)", k=2)[:, bass.ds(rv(rk, "pb", PB), PB), :])
        s.dma("sp", GYs, lambda rk: GY.rearrange("(k n) r t -> k n r t", k=4)[:, tbs, :, bass.ds(rv(rk, "sp", SPC), SPC)])
        s.dma("sp", GAs, lambda rk: GA.rearrange("(k n) r t -> k n r t", k=2)[:, tbs, :, bass.ds(rv(rk, "sp", SPC), SPC)])
        s.dma("sp", X2[0:PB * 256, :], lambda rk: x_all[bass.ds(rv(rk, "xr", PB * 256), PB * 256), :])
        s.dma("sp", X2[PB * 256:PB * 256 + SPC, :], lambda rk: x_all[bass.ds(rv(rk, "xs", SPC, SEQ), SPC), :])
        s.barrier()
        it = 0
        items = [("p", i) for i in range(PB)] + [("s", 0)]
        for (kind, i) in items:
            if kind == "p":
                ST, subs = 256, [(0, 128), (128, 128)]
                xrow = lambda o, n, i=i: X2[i * 256 + o:i * 256 + o + n, :]
                ysrc = lambda kc, i=i: GYr[kc, i].rearrange("(r p t) -> p r t", p=128, t=256)
                asrc = lambda h, i=i: GAr[h, i].rearrange("(r p t) -> p r t", p=128, t=256)
                yrow0 = i * 256
            else:
                ST, subs = SPC, [(0, SPC)]
                xrow = lambda o, n: X2[PB * 256 + o:PB * 256 + o + n, :]
                ysrc = lambda kc: GYs[kc].rearrange("(r p) t -> p r t", p=128)
                asrc = lambda h: GAs[h].rearrange("(r p) t -> p r t", p=128)
                yrow0 = PB * 256
            for (o, n) in subs:
                load_norm_T(cx, c, xrow(o, n), n, xt[0], xs, junk, hT, o, (ssq, rstd))
            yT4 = yT.ap.rearrange("p (r k) t -> p r k t", k=4)
            for kc in range(4):
                s.dma("sp", yT4[:, :, kc, :ST], ysrc(kc), writes=[yT])
            aT4 = aT.ap.rearrange("p (r h) t -> p r h t", h=2)
            for h in range(2):
                s.dma("sp", aT4[:, :, h, :ST], asrc(h), writes=[aT])
            for jj in range(16):
                ib = it % 2
                it += 1
                s.dma("sp", wgs[ib].ap, WGS[jj], writes=[wgs[ib]])
                s.dma("sp", wgm[ib].ap, WGM[jj], writes=[wgm[ib]])
                s.dma("sp", wsb[ib].ap, WS[jj], writes=[wsb[ib]])
                s.dma("sp", wmb[ib].ap, WM[jj], writes=[wmb[ib]])
                bgs = cx.bank()
                mm_acc(cx, bgs.ap[:, :ST], [(wgs[ib].ap[:, kc, :], hT.ap[:, kc, :ST]) for kc in range(16)], [wgs[ib], hT], bgs)
                bys = cx.bank()
                mm_acc(cx, bys.ap[:, :ST], [(wsb[ib].ap[:, kc, :], yT.ap[:, kc, :ST]) for kc in range(32)], [wsb[ib], yT], bys)
                bgm = cx.bank()
                mm_acc(cx, bgm.ap[:, :ST], [(wgm[ib].ap[:, kc, :], hT.ap[:, kc, :ST]) for kc in range(16)], [wgm[ib], hT], bgm)
                bym = cx.bank()
                mm_acc(cx, bym.ap[:, :ST], [(wmb[ib].ap[:, kc, :], aT.ap[:, kc, :ST]) for kc in range(16)], [wmb[ib], aT], bym)
                s.op("act", lambda e: e.activation(out=sg1.ap[:, :ST], in_=bgs.ap[:, :ST], func=AF.Sigmoid), reads=[bgs], writes=[sg1])
                s.op("act", lambda e: e.activation(out=sg2.ap[:, :ST], in_=bgm.ap[:, :ST], func=AF.Sigmoid), reads=[bgm], writes=[sg2])
                s.op("dve", lambda e: e.tensor_tensor(out=sg1.ap[:, :ST], in0=bys.ap[:, :ST], in1=sg1.ap[:, :ST], op=ALU.mult), reads=[bys, sg1], writes=[sg1])
                s.op("dve", lambda e: e.tensor_tensor(out=sg2.ap[:, :ST], in0=bym.ap[:, :ST], in1=sg2.ap[:, :ST], op=ALU.mult), reads=[bym, sg2], writes=[sg2])
                s.op("dve", lambda e, jj=jj: e.tensor_tensor(out=mixT.ap[:, jj, :ST], in0=sg1.ap[:, :ST], in1=sg2.ap[:, :ST], op=ALU.add), reads=[sg1, sg2], writes=[mixT])
            for (o, n) in subs:
                s.dma("sp", xr.ap[:n, :], xrow(o, n), writes=[xr])
                for cg in range(4):
                    b = cx.bank()
                    mm_acc(cx, b.ap[:n, :], [(mixT.ap[:, kc, o:o + n], wo.ap[:, kc, cg * 512:(cg + 1) * 512]) for kc in range(16)], [mixT, wo], b)
                    s.op("dve", lambda e, cg=cg, b=b: e.tensor_tensor(out=xo.ap[:n, cg * 512:(cg + 1) * 512], in0=b.ap[:n, :], in1=xr.ap[:n, cg * 512:(cg + 1) * 512], op=ALU.add),
                         reads=[b, xr], writes=[xo])
                s.op("act", lambda e: e.activation(out=junk.ap[:n, :], in_=xo.ap[:n, :], func=AF.Square, accum_out=ssq2.ap[:n, :]), reads=[xo], writes=[junk, ssq2])
                s.op("act", lambda e: e.activation(out=rstd2.ap[:n, :], in_=ssq2.ap[:n, :], func=AF.Sqrt, scale=1.0 / D, bias=c["eps"].ap[:n, :]), reads=[ssq2, c["eps"]], writes=[rstd2])
                s.op("dve", lambda e: e.reciprocal(out=rstd2.ap[:n, :], in_=rstd2.ap[:n, :]), reads=[rstd2], writes=[rstd2])
                s.op("dve", lambda e: e.scalar_tensor_tensor(out=xo.ap[:n, :], in0=xo.ap[:n, :], scalar=rstd2.ap[:n, :], in1=fnw_b.ap[:n, :], op0=ALU.mult, op1=ALU.mult),
                     reads=[xo, rstd2, fnw_b], writes=[xo])
                s.dma("pool", y2[yrow0 + o:yrow0 + o + n, :], xo.ap[:n, :], reads=[xo])
    s.barrier()


def build_l2(SEQ, DB):
    NTOK = SEQ + DB * 16
    NB = NTOK // 256
    NT2 = SEQ // NCORES + DB * 16 // NCORES
    nc = bass.Bass("TRN2", target_bir_lowering=False)
    x_all = nc.dram_tensor("x_all", [NTOK, D], F32, kind="ExternalInput").ap()
    GY = nc.dram_tensor("GY", [4 * NB, 1024, 256], BF16, kind="ExternalInput").ap()
    GA = nc.dram_tensor("GA", [2 * NB, 1024, 256], BF16, kind="ExternalInput").ap()
    y2 = nc.dram_tensor("y2", [NT2, D], F32, kind="ExternalOutput").ap()
    l2w = declare_l2_weights(nc)
    cx = Ctx(nc)
    with ExitStack() as es0:
        c = make_consts(cx, es0)
        setup_small_consts(cx, es0, c)
        l2_phase(cx, c, es0, SEQ, DB, x_all, GY, GA, *l2w, y2, *declare_l2_scratch(nc))
    cx.s.emit()
    return nc


def declare_l2_weights(nc):
    wg = nc.dram_tensor("wg", [128, 16, 4096], F32, kind="ExternalInput").ap()
    wssm = nc.dram_tensor("wssm", [128, 32, 2048], F32, kind="ExternalInput").ap()
    wmla = nc.dram_tensor("wmla", [128, 16, 2048], F32, kind="ExternalInput").ap()
    wout = nc.dram_tensor("wout", [128, 16, 2048], F32, kind="ExternalInput").ap()
    nrm = nc.dram_tensor("nrm2", [128, KC], F32, kind="ExternalInput").ap()
    snw = nc.dram_tensor("snw", [128, 32], F32, kind="ExternalInput").ap()
    fnw = nc.dram_tensor("fnw", [1, D], F32, kind="ExternalInput").ap()
    return wg, wssm, wmla, wout, nrm, snw, fnw


def declare_l2_scratch(nc):
    WGS = nc.dram_tensor("WGS", [16, 128, 16, 128], BF16).ap()
    WGM = nc.dram_tensor("WGM", [16, 128, 16, 128], BF16).ap()
    WS = nc.dram_tensor("WS", [16, 128, 32, 128], BF16).ap()
    WM = nc.dram_tensor("WM", [16, 128, 16, 128], BF16).ap()
    return WGS, WGM, WS, WM


def _arr_k(w):
    K, C = w.shape
    return np.ascontiguousarray(w.reshape(K // 128, 128, C).transpose(1, 0, 2))


def _conv_idx(core):
    return np.concatenate([np.arange(core * 512, (core + 1) * 512), 4096 + np.arange(core * 128, (core + 1) * 128),
                           5120 + np.arange(core * 128, (core + 1) * 128)])


def _prep_l1_inputs(x_all, win, norm_in_w, q_norm_w, kv_norm_w, w_q_up, w_kv_up, cache_kv_latent, cache_k_rope, core, extra):
    f = np.float32
    offs = np.cumsum([0, 4096, 6144, 64, 512, 512, 64, 2048, 2048, 2048])
    h0 = 2 * core
    g_cols = win[:, offs[6] + h0 * 128: offs[6] + (h0 + 2) * 128]
    w_m = _arr_k(np.concatenate([win[:, offs[3]:offs[4]], win[:, offs[4]:offs[5]], win[:, offs[5]:offs[6]], g_cols], 1))
    wq = np.asarray(w_q_up, f)[0].reshape(512, 16, 192)
    w_q = np.concatenate([wq[:, h0, :128], wq[:, h0 + 1, :128], wq[:, h0, 128:], wq[:, h0 + 1, 128:]], 1)
    wkv = np.asarray(w_kv_up, f)[0].reshape(512, 16, 256)
    w_kv = np.concatenate([wkv[:, h0, :128], wkv[:, h0 + 1, :128], wkv[:, h0, 128:], wkv[:, h0 + 1, 128:]], 1)
    conv_w, conv_b, dt_bias, a_log, d_skip, state_conv, state_ssm = extra
    idx = _conv_idx(core)
    w_s = _arr_k(np.concatenate([win[:, core * 512:(core + 1) * 512], win[:, offs[1] + idx], win[:, offs[2] + core * 8: offs[2] + (core + 1) * 8]], 1))
    cwv = np.asarray(conv_w, f)[0][:, idx]
    DBn = state_conv.shape[1]
    return {
        "w_s": w_s, "cw": np.ascontiguousarray(cwv.reshape(4, 6, 128).transpose(2, 1, 0)),
        "cb": np.ascontiguousarray(np.asarray(conv_b, f)[0][idx].reshape(6, 128).T),
        "dtb": np.asarray(dt_bias, f)[0][core * 8:(core + 1) * 8].reshape(1, 8).copy(),
        "alog": np.asarray(a_log, f)[0][core * 8:(core + 1) * 8].reshape(1, 8).copy(),
        "dsk": np.asarray(d_skip, f)[0][core * 8:(core + 1) * 8].reshape(1, 8).copy(),
        "sconv": np.ascontiguousarray(np.asarray(state_conv, f)[0][:, :, idx]),
        "sssm": np.ascontiguousarray(np.asarray(state_ssm, f)[0][:, core * 8:(core + 1) * 8].reshape(DBn, 512, 128)),
        "x_all": x_all, "w_m": w_m, "w_q": _arr_k(w_q), "w_kv": _arr_k(w_kv),
        "nrm": np.ascontiguousarray(np.asarray(norm_in_w, f)[0].reshape(16, 128).T),
        "qnw": np.ascontiguousarray(np.asarray(q_norm_w, f)[0].reshape(4, 128).T),
        "kvw": np.asarray(kv_norm_w, f).reshape(1, 512),
        "cache_kv": np.ascontiguousarray(np.asarray(cache_kv_latent, f)[0]),
        "cache_kr": np.ascontiguousarray(np.asarray(cache_k_rope, f)[0]),
    }


def _prep_l2_weights(win, w_ssm_out, w_mla_out, w_out, norm_in_w, ssm_norm_w, final_norm_w):
    f = np.float32
    offs = np.cumsum([0, 4096, 6144, 64, 512, 512, 64, 2048, 2048, 2048])
    return {
        "wg": _arr_k(np.ascontiguousarray(win[:, offs[7]:offs[9]])),
        "wssm": _arr_k(np.asarray(w_ssm_out, f)[0]), "wmla": _arr_k(np.asarray(w_mla_out, f)[0]), "wout": _arr_k(np.asarray(w_out, f)[0]),
        "nrm2": np.ascontiguousarray(np.asarray(norm_in_w, f)[0].reshape(16, 128).T),
        "snw": np.ascontiguousarray(np.asarray(ssm_norm_w, f)[0].reshape(32, 128).T),
        "fnw": np.asarray(final_norm_w, f).reshape(1, D),
    }


def _assemble_y(results, key, SEQ, DB, DS, B):
    f = np.float32
    pp, sp_ = SEQ // NCORES, DB * DS // NCORES
    yp = np.concatenate([np.asarray(r[key])[:pp] for r in results], 0).reshape(B, SEQ, D).astype(f)
    ys = np.concatenate([np.asarray(r[key])[pp:pp + sp_] for r in results], 0).reshape(DB, DS, D).astype(f)
    return yp, ys


def kernel(x_prompt, x_sample, cache_kv_latent, cache_k_rope, state_ssm, state_conv, norm_in_w, w_in,
           conv_w, conv_b, dt_bias, a_log, d_skip, ssm_norm_w, w_ssm_out, q_norm_w, w_q_up, kv_norm_w,
           w_kv_up, w_mla_out, w_out, final_norm_w):
    f = np.float32
    B, SEQ, _ = x_prompt.shape
    DB, DS, _ = x_sample.shape
    PAST = cache_kv_latent.shape[2]
    NTOK = SEQ + DB * DS
    x_all = np.concatenate([np.asarray(x_prompt, f).reshape(SEQ, D), np.asarray(x_sample, f).reshape(DB * DS, D)], 0)
    win = np.asarray(w_in, f)[0]
    FUSED = bool(int(os.environ.get("K_FUSED", "0")))
    nc = build_l1(SEQ, DB, PAST, fused=FUSED)
    extra = (conv_w, conv_b, dt_bias, a_log, d_skip, state_conv, state_ssm)
    ims = [_prep_l1_inputs(x_all, win, norm_in_w, q_norm_w, kv_norm_w, w_q_up, w_kv_up, cache_kv_latent, cache_k_rope, cidx, extra)
           for cidx in range(NCORES)]
    if FUSED:
        l2in = _prep_l2_weights(win, w_ssm_out, w_mla_out, w_out, norm_in_w, ssm_norm_w, final_norm_w)
        for im in ims:
            im.update(l2in)
    res = run_bass_kernel_spmd(nc, ims, core_ids=list(range(NCORES)))
    r0 = res.results[0]
    global _DBG
    _DBG = {}
    kvl = r0["kvlat"]
    krp = r0["krope"]
    conv_p = np.zeros((1, B, 3, 6144), f)
    conv_s = np.zeros((1, DB, 3, 6144), f)
    ssm_p = np.zeros((1, B, 64, 64, 128), f)
    ssm_s = np.zeros((1, DB, 64, 64, 128), f)
    for cidx, r in enumerate(res.results):
        idx = _conv_idx(cidx)
        conv_p[0, 0][:, idx] = r["conv_o"][0]
        conv_s[0][:, :, idx] = r["conv_o"][1:]
        ssm_p[0, 0, cidx * 8:(cidx + 1) * 8] = r["ssm_o"][0].reshape(8, 64, 128)
        ssm_s[0, :, cidx * 8:(cidx + 1) * 8] = r["ssm_o"][1:].reshape(DB, 8, 64, 128)
    NS = DB * DS
    if FUSED:
        y_prompt, y_sample = _assemble_y(res.results, "y2", SEQ, DB, DS, B)
        return (y_prompt, y_sample,
                kvl[:SEQ].reshape(1, B, SEQ, 512), krp[:SEQ].reshape(1, B, SEQ, 64), ssm_p, conv_p,
                kvl[SEQ:].reshape(1, DB, DS, 512), krp[SEQ:].reshape(1, DB, DS, 64), ssm_s, conv_s)
    GYh = np.concatenate([np.asarray(r["YTc"]) for r in res.results], 1)
    GAh = np.concatenate([np.asarray(r["ATc"]) for r in res.results], 1)
    l2in = _prep_l2_weights(win, w_ssm_out, w_mla_out, w_out, norm_in_w, ssm_norm_w, final_norm_w)
    l2in.update({"x_all": x_all, "GY": GYh, "GA": GAh})
    if os.environ.get("K_ONLY_L1"):
        y_prompt, y_sample = np.zeros((B, SEQ, D), f), np.zeros((DB, DS, D), f)
    else:
        print("L1 done", flush=True)
        nc2 = build_l2(SEQ, DB)
        res2 = run_bass_kernel_spmd(nc2, [l2in for _ in range(NCORES)], core_ids=list(range(NCORES)))
        y_prompt, y_sample = _assemble_y(res2.results, "y2", SEQ, DB, DS, B)
    outs = (y_prompt, y_sample,
            kvl[:SEQ].reshape(1, B, SEQ, 512), krp[:SEQ].reshape(1, B, SEQ, 64),
            ssm_p, conv_p,
            kvl[SEQ:].reshape(1, DB, DS, 512), krp[SEQ:].reshape(1, DB, DS, 64),
            ssm_s, conv_s)
    return outs
```

```python
import numpy as np
import ml_dtypes
from contextlib import ExitStack
import concourse.bass as bass
import concourse.mybir as mybir
from concourse.bass_utils import run_bass_kernel_spmd

F32 = mybir.dt.float32
BF16 = mybir.dt.bfloat16
I32 = mybir.dt.int32
AF = mybir.ActivationFunctionType
ALU = mybir.AluOpType
AX = mybir.AxisListType

import os
STOP = int(os.environ.get("K_STOP", "99"))
PH = os.environ.get("K_PH", "MCAS")
SAMEWAIT = bool(int(os.environ.get("K_SAMEWAIT", "1")))
SUB = int(os.environ.get("K_SUB", "99"))
NCORES = 8
D = 2048
KC = 16
EPS = 1e-6
TWO_PI = 6.283185307179586
PI = 3.141592653589793


class Tl:
    __slots__ = ("ap", "w", "r")

    def __init__(self, ap):
        self.ap = ap
        self.w = None
        self.r = {}

    def __getitem__(self, k):
        return self.ap[k]


class _Rec:
    def __init__(self):
        self.calls = []

    def __getattr__(self, name):
        def f(*a, **k):
            self.calls.append((name, a, k))
            return self
        return f


class Sch:
    def __init__(self, nc, ndma=40):
        self.nc = nc
        self.names = ("pe", "act", "dve", "pool", "sp")
        self.q = {k: [] for k in self.names}
        self.sem = {k: nc.alloc_semaphore("sm_" + k) for k in self.names}
        self.cnt = {k: 0 for k in self.names}
        self.seen = {k: {} for k in self.names}
        self.dsem = [nc.alloc_semaphore("sd%d" % i) for i in range(ndma)]
        self.dcnt = [0] * ndma
        self.rr = 0
        self.ccsem = nc.alloc_semaphore("sm_cc")
        self.cccnt = 0
        self.rank = {}

    def _wait(self, en, key, val):
        if val is None or val <= 0:
            return
        if key == en and (en == "pe" or not SAMEWAIT) and en in ("pe", "act", "dve"):
            return
        if self.seen[en].get(key, 0) >= val:
            return
        self.seen[en][key] = val
        sem = self.sem[key] if isinstance(key, str) else self.dsem[key]
        self.q[en].append(lambda e, sem=sem, val=val: e.wait_ge(sem, val))

    def _deps(self, en, reads, writes):
        for t in reads:
            if t.w is not None:
                self._wait(en, *t.w)
        for t in writes:
            if t.w is not None:
                self._wait(en, *t.w)
            for k, v in t.r.items():
                self._wait(en, k, v)

    def _mark(self, tk, reads, writes):
        for t in reads:
            if t.r.get(tk[0], 0) < tk[1]:
                t.r[tk[0]] = tk[1]
        for t in writes:
            t.w = tk
            t.r = {}

    def op(self, en, fn, reads=(), writes=()):
        self._deps(en, reads, writes)
        self.cnt[en] += 1
        sem = self.sem[en]
        rec = _Rec()
        fn(rec)
        assert len(rec.calls) == 1
        name, a, k = rec.calls[0]
        self.q[en].append(lambda e, name=name, a=a, k=k: getattr(e, name)(*a, **k).then_inc(sem, 1))
        tk = (en, self.cnt[en])
        self._mark(tk, reads, writes)
        return tk

    def dma(self, en, out, in_, reads=(), writes=(), **kw):
        self._deps(en, reads, writes)
        i = self.rr
        self.rr = (self.rr + 1) % len(self.dsem)
        self._wait(en, i, self.dcnt[i])
        self.dcnt[i] += 16
        dsem = self.dsem[i]

        def emit(e):
            o, n = out, in_
            if callable(o) or callable(n):
                if en not in self.rank:
                    self.rank[en] = {"r": e.partition_id()}
                if callable(o):
                    o = o(self.rank[en])
                if callable(n):
                    n = n(self.rank[en])
            try:
                e.dma_start(out=o, in_=n, **kw).then_inc(dsem, 16)
            except Exception:
                print("DMA FAIL", en, o, n, flush=True)
                raise
        self.q[en].append(emit)
        tk = (i, self.dcnt[i])
        self._mark(tk, reads, writes)
        return tk

    def coll(self, in_ap, out_ap, reads=(), writes=()):
        self._deps("pool", reads, writes)
        self.cccnt += 1
        n = self.cccnt
        sem = self.ccsem

        def emit(e):
            e.collective_compute("AllGather", ALU.bypass, replica_groups=[list(range(NCORES))],
                                 ins=[in_ap.opt()], outs=[out_ap.opt()]).then_inc(sem, 1)
            e.wait_ge(sem, n)
        self.q["pool"].append(emit)

    def barrier(self):
        for en in self.names:
            for k in self.names:
                self._wait(en, k, self.cnt[k])
            for i in range(len(self.dsem)):
                self._wait(en, i, self.dcnt[i])
            if self.cccnt and en != "pool" and self.seen[en].get("cc", 0) < self.cccnt:
                self.seen[en]["cc"] = self.cccnt
                self.q[en].append(lambda e, sem=self.ccsem, val=self.cccnt: e.wait_ge(sem, val))

    def emit(self, use_block=None):
        nc = self.nc
        if use_block is None:
            use_block = bool(int(os.environ.get("K_BLOCK", "1")))
        if not use_block:
            eng = dict(pe=nc.tensor, act=nc.scalar, dve=nc.vector, pool=nc.gpsimd, sp=nc.sync)
            for en in self.names:
                for f in self.q[en]:
                    f(eng[en])
            return
        with nc.Block() as block:
            starters = dict(pe=block.tensor, act=block.scalar, dve=block.vector, pool=block.gpsimd, sp=block.sync)
            for en in self.names:
                fl = self.q[en]
                if not fl:
                    continue

                def body(e, fl=fl):
                    for f in fl:
                        f(e)
                starters[en](body)


class Ctx:
    ARENA = 206 * 1024

    def __init__(self, nc):
        self.nc = nc
        self.s = Sch(nc)
        self.banks = [Tl(nc.alloc_psum_tensor("pb%d" % i, [128, 512], F32).ap()) for i in range(8)]
        self.bi = 0
        self.flip = 0
        self.arena = nc.alloc_sbuf_tensor("arena", [128, self.ARENA], mybir.dt.uint8).ap()
        self.ptr = 0

    pool = list(range(8))

    def bank(self):
        self.bi = (self.bi + 1) % len(self.pool)
        return self.banks[self.pool[self.bi]]

    def _restore(self, p):
        self.ptr = p

    def sb(self, es, name, shape, dt):
        shape = list(shape)
        esz = {F32: 4, BF16: 2, I32: 4}[dt]
        n = 1
        for d in shape[1:]:
            n *= d
        nbytes = ((n * esz + 31) // 32) * 32
        old = self.ptr
        assert old + nbytes <= self.ARENA, "SBUF arena overflow at %s: %d + %d" % (name, old, nbytes)
        self.ptr = old + nbytes
        es.callback(self._restore, old)
        ap = self.arena[:shape[0], old:old + n * esz].bitcast(dt)
        if len(shape) == 3:
            ap = ap.rearrange("p (a b) -> p a b", a=shape[1])
        elif len(shape) == 4:
            ap = ap.rearrange("p (a b c) -> p a b c", a=shape[1], b=shape[2])
        return Tl(ap)

    force_evac = None

    def evac_eng(self):
        if self.force_evac:
            return self.force_evac
        self.flip ^= 1
        return "act" if self.flip else "dve"


def make_consts(cx, es):
    nc, s = cx.nc, cx.s
    c = {}
    io = cx.sb(es, "c_io", [128, 128], I32)
    s.op("pool", lambda e: e.iota(io.ap, pattern=[[1, 128]], base=0, channel_multiplier=-1), writes=[io])
    iof = cx.sb(es, "c_iof", [128, 128], F32)
    s.op("dve", lambda e: e.tensor_copy(out=iof.ap, in_=io.ap), reads=[io], writes=[iof])
    c["identf"] = cx.sb(es, "c_identf", [128, 128], F32)
    s.op("dve", lambda e: e.tensor_single_scalar(out=c["identf"].ap, in_=iof.ap, scalar=0.0, op=ALU.is_equal),
         reads=[iof], writes=[c["identf"]])
    c["ident"] = cx.sb(es, "c_ident", [128, 128], BF16)
    s.op("dve", lambda e: e.tensor_copy(out=c["ident"].ap, in_=c["identf"].ap), reads=[c["identf"]], writes=[c["ident"]])
    c["U"] = cx.sb(es, "c_U", [128, 128], F32)
    s.op("dve", lambda e: e.tensor_single_scalar(out=c["U"].ap, in_=iof.ap, scalar=0.0, op=ALU.is_ge),
         reads=[iof], writes=[c["U"]])
    c["onesb"] = cx.sb(es, "c_onesb", [128, 128], BF16)
    s.op("dve", lambda e: e.memset(c["onesb"].ap, 1.0), writes=[c["onesb"]])
    c["onesf"] = cx.sb(es, "c_onesf", [128, 128], F32)
    s.op("dve", lambda e: e.memset(c["onesf"].ap, 1.0), writes=[c["onesf"]])
    c["iof"] = iof
    return c


def load_norm_T(cx, c, x_rows_ap, ntok, xt, xs, junk, hT, col0, small):
    s = cx.s
    s.dma("sp", xt.ap[:ntok, :], x_rows_ap, writes=[xt])
    ssq, rstd = small
    s.op("act", lambda e: e.activation(out=junk.ap[:ntok, :], in_=xt.ap[:ntok, :], func=AF.Square, accum_out=ssq.ap[:ntok, :]),
         reads=[xt], writes=[junk, ssq])
    s.op("act", lambda e: e.activation(out=rstd.ap[:ntok, :], in_=ssq.ap[:ntok, :], func=AF.Sqrt, scale=1.0 / D, bias=c["eps"].ap[:ntok, :]),
         reads=[ssq, c["eps"]], writes=[rstd])
    s.op("dve", lambda e: e.reciprocal(out=rstd.ap[:ntok, :], in_=rstd.ap[:ntok, :]), reads=[rstd], writes=[rstd])
    s.op("dve", lambda e: e.tensor_scalar(out=xs.ap[:ntok, :], in0=xt.ap[:ntok, :], scalar1=rstd.ap[:ntok, :], scalar2=None, op0=ALU.mult),
         reads=[xt, rstd], writes=[xs])
    for g in range(2):
        b = cx.bank()
        bv = b.ap.bitcast(BF16)
        for j in range(8):
            kc = g * 8 + j
            s.op("pe", lambda e, j=j, kc=kc: e.transpose(out=bv[:, j * 128:j * 128 + ntok], in_=xs.ap[:ntok, kc * 128:(kc + 1) * 128],
                                                         identity=c["ident"].ap[:ntok, :ntok]),
                 reads=[xs, c["ident"]], writes=[b])
        en = cx.evac_eng()
        src = bv.rearrange("p (j t) -> p j t", j=8)[:, :, :ntok]
        dst = hT.ap[:, g * 8:(g + 1) * 8, col0:col0 + ntok]
        if en == "act":
            s.op("act", lambda e: e.copy(out=dst, in_=src), reads=[b], writes=[hT])
        else:
            s.op("dve", lambda e: e.tensor_copy(out=dst, in_=src), reads=[b], writes=[hT])


def load_weights_bf16(cx, es, name, w_dram, nkc, ncols, scale_tl=None, const_scale=1.0, stage=None):
    s = cx.s
    wt = cx.sb(es, name, [128, nkc, ncols], BF16)
    CH = stage.ap.shape[1]
    for kc in range(nkc):
        for c0 in range(0, ncols, CH):
            cw = min(CH, ncols - c0)
            s.dma("sp", stage.ap[:, :cw], w_dram[:, kc, c0:c0 + cw], writes=[stage])
            if scale_tl is not None:
                s.op("act", lambda e, kc=kc, c0=c0, cw=cw: e.activation(out=wt.ap[:, kc, c0:c0 + cw], in_=stage.ap[:, :cw], func=AF.Copy,
                                                                        scale=scale_tl.ap[:, kc:kc + 1]),
                     reads=[stage, scale_tl], writes=[wt])
            else:
                s.op("act", lambda e, kc=kc, c0=c0, cw=cw: e.activation(out=wt.ap[:, kc, c0:c0 + cw], in_=stage.ap[:, :cw], func=AF.Copy,
                                                                        scale=float(const_scale)),
                     reads=[stage], writes=[wt])
    return wt


def mm_acc(cx, bank_ap, pairs, reads, bank):
    n = len(pairs)
    for i, (l, r) in enumerate(pairs):
        cx.s.op("pe", lambda e, l=l, r=r, i=i: e.matmul(bank_ap, lhsT=l, rhs=r, start=(i == 0), stop=(i == n - 1)),
                reads=reads, writes=[bank])


def rope_tables(cx, c, tabs, pos_tl, nsub):
    s = cx.s
    ang, kf, ki, cos, sin, m1 = tabs
    for (dst, shift) in ((sin, 0.0), (cos, PI / 2)):
        s.op("dve", lambda e: e.tensor_tensor(out=ang.ap[:, :nsub, :], in0=pos_tl.ap[:, :nsub].unsqueeze(2).to_broadcast([128, nsub, 32]),
                                              in1=c["inv"].ap.unsqueeze(1).to_broadcast([128, nsub, 32]), op=ALU.mult),
             reads=[pos_tl, c["inv"]], writes=[ang])
        if shift:
            s.op("dve", lambda e: e.tensor_scalar(out=ang.ap[:, :nsub, :], in0=ang.ap[:, :nsub, :], scalar1=float(shift), scalar2=None, op0=ALU.add),
                 reads=[ang], writes=[ang])
        s.op("dve", lambda e: e.tensor_scalar(out=kf.ap[:, :nsub, :], in0=ang.ap[:, :nsub, :], scalar1=1.0 / TWO_PI, scalar2=None, op0=ALU.mult),
             reads=[ang], writes=[kf])
        s.op("dve", lambda e: e.tensor_copy(out=ki.ap[:, :nsub, :], in_=kf.ap[:, :nsub, :]), reads=[kf], writes=[ki])
        s.op("dve", lambda e: e.tensor_copy(out=kf.ap[:, :nsub, :], in_=ki.ap[:, :nsub, :]), reads=[ki], writes=[kf])
        s.op("dve", lambda e: e.scalar_tensor_tensor(out=ang.ap[:, :nsub, :], in0=kf.ap[:, :nsub, :], scalar=-TWO_PI, in1=ang.ap[:, :nsub, :],
                                                     op0=ALU.mult, op1=ALU.add), reads=[kf, ang], writes=[ang])
        s.op("dve", lambda e: e.tensor_scalar(out=m1.ap[:, :nsub, :], in0=ang.ap[:, :nsub, :], scalar1=PI, scalar2=-TWO_PI, op0=ALU.is_gt, op1=ALU.mult),
             reads=[ang], writes=[m1])
        s.op("dve", lambda e: e.tensor_tensor(out=ang.ap[:, :nsub, :], in0=ang.ap[:, :nsub, :], in1=m1.ap[:, :nsub, :], op=ALU.add),
             reads=[ang, m1], writes=[ang])
        s.op("dve", lambda e: e.tensor_scalar(out=m1.ap[:, :nsub, :], in0=ang.ap[:, :nsub, :], scalar1=-PI, scalar2=TWO_PI, op0=ALU.is_lt, op1=ALU.mult),
             reads=[ang], writes=[m1])
        s.op("dve", lambda e: e.tensor_tensor(out=ang.ap[:, :nsub, :], in0=ang.ap[:, :nsub, :], in1=m1.ap[:, :nsub, :], op=ALU.add),
             reads=[ang, m1], writes=[ang])
        s.op("dve", lambda e: e.tensor_scalar(out=ang.ap[:, :nsub, :], in0=ang.ap[:, :nsub, :], scalar1=PI, scalar2=-PI, op0=ALU.min, op1=ALU.max),
             reads=[ang], writes=[ang])
        s.op("act", lambda e, dst=dst: e.activation(out=dst.ap[:, :nsub, :], in_=ang.ap[:, :nsub, :], func=AF.Sin), reads=[ang], writes=[dst])


def apply_rope(cx, src_ap, src_tl, nh, cos_ap, sin_ap, tabs_tl, out_tl, tmp_tl, ntok):
    s = cx.s
    x1 = src_ap[:, :, 0:32]
    x2 = src_ap[:, :, 32:64]
    cb = cos_ap.unsqueeze(1).to_broadcast([ntok, nh, 32])
    sb_ = sin_ap.unsqueeze(1).to_broadcast([ntok, nh, 32])
    o = out_tl.ap[:ntok].rearrange("p (h d) -> p h d", h=nh)
    t = tmp_tl.ap[:ntok].rearrange("p (h d) -> p h d", h=nh)
    rd = [src_tl] + list(tabs_tl)
    s.op("dve", lambda e: e.tensor_tensor(out=o[:, :, 0:32], in0=x1, in1=cb, op=ALU.mult), reads=rd, writes=[out_tl])
    s.op("dve", lambda e: e.tensor_tensor(out=t[:, :, 0:32], in0=x2, in1=sb_, op=ALU.mult), reads=rd, writes=[tmp_tl])
    s.op("dve", lambda e: e.tensor_tensor(out=o[:, :, 32:64], in0=x1, in1=sb_, op=ALU.mult), reads=rd, writes=[out_tl])
    s.op("dve", lambda e: e.tensor_tensor(out=t[:, :, 32:64], in0=x2, in1=cb, op=ALU.mult), reads=rd, writes=[tmp_tl])
    s.op("dve", lambda e: e.tensor_tensor(out=o[:, :, 0:32], in0=o[:, :, 0:32], in1=t[:, :, 0:32], op=ALU.subtract),
         reads=[out_tl, tmp_tl], writes=[out_tl])
    s.op("dve", lambda e: e.tensor_tensor(out=o[:, :, 32:64], in0=o[:, :, 32:64], in1=t[:, :, 32:64], op=ALU.add),
         reads=[out_tl, tmp_tl], writes=[out_tl])


def setup_small_consts(cx, es, c):
    s = cx.s
    c["eps"] = cx.sb(es, "c_eps", [128, 1], F32)
    s.op("dve", lambda e: e.memset(c["eps"].ap, EPS), writes=[c["eps"]])
    ji = cx.sb(es, "c_ji", [128, 32], I32)
    s.op("pool", lambda e: e.iota(ji.ap, pattern=[[1, 32]], base=0, channel_multiplier=0), writes=[ji])
    jf = cx.sb(es, "c_jf", [128, 32], F32)
    s.op("dve", lambda e: e.tensor_copy(out=jf.ap, in_=ji.ap), reads=[ji], writes=[jf])
    c["inv"] = cx.sb(es, "c_inv", [128, 32], F32)
    s.op("act", lambda e: e.activation(out=c["inv"].ap, in_=jf.ap, func=AF.Exp, scale=-float(np.log(10000.0)) / 32.0),
         reads=[jf], writes=[c["inv"]])
    pi_ = cx.sb(es, "c_pi", [128, 1], I32)
    s.op("pool", lambda e: e.iota(pi_.ap, pattern=[[0, 1]], base=0, channel_multiplier=1), writes=[pi_])
    c["pf"] = cx.sb(es, "c_pf", [128, 1], F32)
    s.op("dve", lambda e: e.tensor_copy(out=c["pf"].ap, in_=pi_.ap), reads=[pi_], writes=[c["pf"]])
    pm = cx.sb(es, "c_pm", [128, 1], I32)
    s.op("dve", lambda e: e.tensor_single_scalar(out=pm.ap, in_=pi_.ap, scalar=15, op=ALU.bitwise_and), reads=[pi_], writes=[pm])
    c["pm16"] = cx.sb(es, "c_pm16", [128, 1], F32)
    s.op("dve", lambda e: e.tensor_copy(out=c["pm16"].ap, in_=pm.ap), reads=[pm], writes=[c["pm16"]])


def evac(cx, dst_ap, dst_tl, src_ap, src_tl, en=None, extra_reads=()):
    en = en or cx.evac_eng()
    if en == "act":
        cx.s.op("act", lambda e: e.copy(out=dst_ap, in_=src_ap), reads=[src_tl] + list(extra_reads), writes=[dst_tl])
    else:
        cx.s.op("dve", lambda e: e.tensor_copy(out=dst_ap, in_=src_ap), reads=[src_tl] + list(extra_reads), writes=[dst_tl])


def rms_rstd(cx, c, src_ap, src_tl, ntok, n, junkf, ssq, rstd):
    s = cx.s
    s.op("act", lambda e: e.activation(out=junkf.ap[:ntok, :n], in_=src_ap, func=AF.Square, accum_out=ssq.ap[:ntok, :]),
         reads=[src_tl], writes=[junkf, ssq])
    s.op("act", lambda e: e.activation(out=rstd.ap[:ntok, :], in_=ssq.ap[:ntok, :], func=AF.Sqrt, scale=1.0 / n, bias=c["eps"].ap[:ntok, :]),
         reads=[ssq, c["eps"]], writes=[rstd])
    s.op("dve", lambda e: e.reciprocal(out=rstd.ap[:ntok, :], in_=rstd.ap[:ntok, :]), reads=[rstd], writes=[rstd])


def transpose_to(cx, c, src_tl, src_ap_fn, nblk, ntok, dst_fn, dst_tl, rows=128, dst_all=None):
    s = cx.s
    b = cx.bank()
    bv = b.ap.bitcast(BF16)
    for j in range(nblk):
        s.op("pe", lambda e, j=j: e.transpose(out=bv[:rows, j * 128:j * 128 + ntok], in_=src_ap_fn(j), identity=c["ident"].ap[:ntok, :ntok]),
             reads=[src_tl, c["ident"]], writes=[b])
    if dst_all is not None:
        evac(cx, dst_all, dst_tl, bv[:rows, :nblk * 128].rearrange("p (j t) -> p j t", j=nblk)[:, :, :ntok], b)
        return
    for j in range(nblk):
        evac(cx, dst_fn(j), dst_tl, bv[:rows, j * 128:j * 128 + ntok], b)


def kv_from_latent(cx, c, ckvT, krT_unused, wkv, ST, nsub, kst, vst, KT_dram, V_dram, tok0):
    s = cx.s
    for h in range(2):
        b = cx.bank()
        mm_acc(cx, b.ap[:, :ST], [(wkv.ap[:, kc, h * 128:(h + 1) * 128], ckvT.ap[:, kc, :ST]) for kc in range(4)], [wkv, ckvT], b)
        evac(cx, kst.ap[:, h, :ST], kst, b.ap[:, :ST], b)
    s.dma("pool", KT_dram[:, :, tok0:tok0 + ST], kst.ap[:, :, :ST], reads=[kst])
    for sub in range(nsub):
        b = cx.bank()
        mm_acc(cx, b.ap[:, :256], [(ckvT.ap[:, kc, sub * 128:(sub + 1) * 128], wkv.ap[:, kc, 256:512]) for kc in range(4)], [wkv, ckvT], b)
        evac(cx, vst.ap[:, sub, :], vst, b.ap[:, :256], b)
    s.dma("pool", V_dram[tok0:tok0 + ST, :].rearrange("(n p) f -> p n f", p=128), vst.ap[:, :nsub, :], reads=[vst])


def attention_phase(cx, c, es, SEQ, DB, PAST, QT, QRT, KT, KRT, V, GT, KTc, KRTc, Vc, AT):
    s = cx.s
    qn = [cx.sb(es, "a_qn%d" % i, [128, 2, 512], BF16) for i in range(2)]
    qr = [cx.sb(es, "a_qr%d" % i, [64, 2, 512], BF16) for i in range(2)]
    gg = [cx.sb(es, "a_g%d" % i, [128, 2, 512], BF16) for i in range(2)]
    kt = [cx.sb(es, "a_kt%d" % i, [128, 2, 512], BF16) for i in range(3)]
    kr = [cx.sb(es, "a_kr%d" % i, [64, 512], BF16) for i in range(3)]
    vv = [cx.sb(es, "a_v%d" % i, [128, 4, 256], BF16) for i in range(3)]
    pT = [cx.sb(es, "a_p%d" % i, [128, 512], BF16) for i in range(4)]
    msk = [cx.sb(es, "a_m%d" % i, [128, 512], BF16) for i in range(4)]
    rec = cx.sb(es, "a_rec", [128, 512], F32)
    ot = cx.sb(es, "a_o", [128, 512], F32)
    ao = [cx.sb(es, "a_ao%d" % i, [128, 2, 512], BF16) for i in range(2)]
    for j in range(4):
        s.op("dve", lambda e, j=j: e.memset(msk[j].ap, 0.0), writes=[msk[j]])
        if 128 * j < 512:
            s.op("dve", lambda e, j=j: e.memset(msk[j].ap[0:64, 128 * j:], 1.0), writes=[msk[j]])
        if 128 * j + 64 < 512:
            s.op("dve", lambda e, j=j: e.memset(msk[j].ap[64:128, 128 * j + 64:], 1.0), writes=[msk[j]])
    O = [cx.banks[0], cx.banks[1]]
    Dn = [cx.banks[2], cx.banks[3]]
    st = {"srot": 0, "prot": 0, "kld": 0, "qld": 0}

    pend = []

    def flush():
        while pend:
            pend.pop(0)()

    def block(ktl, krl, vl, kt_ap, kr_ap, v_ap, nk, qtl, qrl, qn_ap, qr_ap, nq, first, last, mask):
        for h in range(2):
            S = cx.banks[4 + st["srot"] % 4]
            st["srot"] += 1
            s.op("pe", lambda e: e.matmul(S.ap[:nk, :nq], lhsT=kt_ap(h), rhs=qn_ap(h), start=True, stop=False), reads=[ktl, qtl], writes=[S])
            s.op("pe", lambda e: e.matmul(S.ap[:nk, :nq], lhsT=kr_ap, rhs=qr_ap(h), start=False, stop=True), reads=[krl, qrl], writes=[S])
            p = pT[st["prot"] % 4]
            st["prot"] += 1
            s.op("act", lambda e: e.activation(out=p.ap[:nk, :nq], in_=S.ap[:nk, :nq], func=AF.Exp), reads=[S], writes=[p])
            if mask is not None:
                s.op("dve", lambda e: e.tensor_tensor(out=p.ap[:nk, :nq], in0=p.ap[:nk, :nq], in1=mask.ap[:nk, :nq], op=ALU.mult), reads=[p, mask], writes=[p])

            def pv(h=h, p=p, vap=v_ap(h)):
                s.op("pe", lambda e: e.matmul(O[h].ap[:, :nq], lhsT=vap, rhs=p.ap[:nk, :nq], start=first, stop=last), reads=[vl, p], writes=[O[h]])
                s.op("pe", lambda e: e.matmul(Dn[h].ap[:, :nq], lhsT=c["onesb"].ap[:nk, :], rhs=p.ap[:nk, :nq], start=first, stop=last),
                     reads=[c["onesb"], p], writes=[Dn[h]])
            if len(pend) >= 2:
                pend.pop(0)()
            pend.append(pv)

    NB = (SEQ + DB * 16) // 256

    def finalize(gtl, g_ap, nq, out_tl, q0):
        flush()
        for h in range(2):
            s.op("dve", lambda e: e.reciprocal(out=rec.ap[:, :nq], in_=Dn[h].ap[:, :nq]), reads=[Dn[h]], writes=[rec])
            s.op("dve", lambda e: e.tensor_tensor(out=ot.ap[:, :nq], in0=O[h].ap[:, :nq], in1=rec.ap[:, :nq], op=ALU.mult), reads=[O[h], rec], writes=[ot])
            s.op("dve", lambda e: e.tensor_tensor(out=out_tl.ap[:, h, :nq], in0=ot.ap[:, :nq], in1=g_ap(h), op=ALU.mult), reads=[ot, gtl], writes=[out_tl])
        tb, off = q0 // 256, q0 % 256
        for h in range(2):
            if nq >= 256:
                s.dma("pool", AT[h * NB + tb:h * NB + tb + nq // 256].rearrange("n p t -> p n t"),
                      out_tl.ap[:, h, :nq].rearrange("p (n t) -> p n t", t=256), reads=[out_tl])
            else:
                s.dma("pool", AT[h * NB + tb][:, off:off + nq], out_tl.ap[:, h, :nq], reads=[out_tl])

    def load_keys(KTd, KRTd, Vd, k0, nk):
        i = st["kld"] % 3
        st["kld"] += 1
        s.dma("sp", kt[i].ap[:, :, :nk], KTd[:, :, k0:k0 + nk], writes=[kt[i]])
        s.dma("sp", kr[i].ap[:, :nk], KRTd[:, k0:k0 + nk], writes=[kr[i]])
        if nk >= 128:
            s.dma("sp", vv[i].ap[:, :nk // 128, :], Vd[k0:k0 + nk, :].rearrange("(n p) f -> p n f", p=128), writes=[vv[i]])
        else:
            s.dma("sp", vv[i].ap[:nk, 0, :], Vd[k0:k0 + nk, :], writes=[vv[i]])
        return kt[i], kr[i], vv[i]

    def load_q(q0, nq):
        i = st["qld"] % 2
        st["qld"] += 1
        s.dma("sp", qn[i].ap[:, :, :nq], QT[:, :, q0:q0 + nq], writes=[qn[i]])
        s.dma("sp", qr[i].ap[:, :, :nq], QRT[:, :, q0:q0 + nq], writes=[qr[i]])
        s.dma("sp", gg[i].ap[:, :, :nq], GT[:, :, q0:q0 + nq], writes=[gg[i]])
        return qn[i], qr[i], gg[i], ao[i]

    for qb in range(SEQ // 512):
        q_n, q_r, g_, a_ = load_q(qb * 512, 512)
        for ksb in range(qb + 1):
            k_, r_, v_ = load_keys(KT, KRT, V, ksb * 512, 512)
            for j in range(4):
                block(k_, r_, v_, lambda h, j=j: k_.ap[:, h, j * 128:(j + 1) * 128], r_.ap[:, j * 128:(j + 1) * 128],
                      lambda h, j=j: v_.ap[:, j, h * 128:(h + 1) * 128], 128,
                      q_n, q_r, lambda h: q_n.ap[:, h, :], lambda h: q_r.ap[:, h, :], 512,
                      first=(ksb == 0 and j == 0), last=(ksb == qb and j == 3), mask=(msk[j] if ksb == qb else None))
        finalize(g_, lambda h: g_.ap[:, h, :], 512, a_, qb * 512)
    for b_ in range(DB):
        q0 = SEQ + b_ * 16
        q_n, q_r, g_, a_ = load_q(q0, 16)
        nsb = PAST // 512
        for ksb in range(nsb):
            k_, r_, v_ = load_keys(KTc, KRTc, Vc, b_ * PAST + ksb * 512, 512)
            for j in range(4):
                block(k_, r_, v_, lambda h, j=j: k_.ap[:, h, j * 128:(j + 1) * 128], r_.ap[:, j * 128:(j + 1) * 128],
                      lambda h, j=j: v_.ap[:, j, h * 128:(h + 1) * 128], 128,
                      q_n, q_r, lambda h: q_n.ap[:, h, :16], lambda h: q_r.ap[:, h, :16], 16,
                      first=(ksb == 0 and j == 0), last=False, mask=None)
        k_, r_, v_ = load_keys(KT, KRT, V, q0, 16)
        block(k_, r_, v_, lambda h: k_.ap[:, h, :16], r_.ap[:, :16], lambda h: v_.ap[:16, 0, h * 128:(h + 1) * 128], 16,
              q_n, q_r, lambda h: q_n.ap[:, h, :16], lambda h: q_r.ap[:, h, :16], 16, first=(nsb == 0), last=True, mask=None)
        finalize(g_, lambda h: g_.ap[:, h, :16], 16, a_, q0)


def split_bf(cx, src_ap, src_tl, parts, shape_fn, nterms=2):
    s = cx.s
    s.op("dve", lambda e: e.tensor_copy(out=shape_fn(parts[0]), in_=src_ap), reads=[src_tl], writes=[parts[0]])
    if nterms == 2:
        s.op("dve", lambda e: e.tensor_tensor(out=shape_fn(parts[1]), in0=src_ap, in1=shape_fn(parts[0]), op=ALU.subtract),
             reads=[src_tl, parts[0]], writes=[parts[1]])
    else:
        r = parts[-1]
        s.op("dve", lambda e: e.tensor_tensor(out=shape_fn(r), in0=src_ap, in1=shape_fn(parts[0]), op=ALU.subtract),
             reads=[src_tl, parts[0]], writes=[r])
        s.op("dve", lambda e: e.tensor_copy(out=shape_fn(parts[1]), in_=shape_fn(r)), reads=[r], writes=[parts[1]])
        s.op("dve", lambda e: e.tensor_tensor(out=shape_fn(r), in0=shape_fn(r), in1=shape_fn(parts[1]), op=ALU.subtract),
             reads=[r, parts[1]], writes=[r])
        s.op("dve", lambda e: e.tensor_copy(out=shape_fn(parts[2]), in_=shape_fn(r)), reads=[r], writes=[parts[2]])


def transpose_f32(cx, c, TW, src_ap, src_tl, R, C, dst_ap, dst_tl):
    s = cx.s
    parts = TW["parts"]
    sf = lambda t: t.ap[:R, :C]
    split_bf(cx, src_ap, src_tl, parts, sf, nterms=3)
    b = cx.bank()
    bv = b.ap.bitcast(BF16)
    for i in range(3):
        s.op("pe", lambda e, i=i: e.transpose(out=bv[:C, i * 128:i * 128 + R], in_=parts[i].ap[:R, :C], identity=c["ident"].ap[:R, :R]),
             reads=[parts[i], c["ident"]], writes=[b])
    s.op("act", lambda e: e.copy(out=dst_ap, in_=bv[:C, 256:256 + R]), reads=[b], writes=[dst_tl])
    s.op("dve", lambda e: e.tensor_tensor(out=dst_ap, in0=bv[:C, 128:128 + R], in1=dst_ap, op=ALU.add), reads=[b, dst_tl], writes=[dst_tl])
    s.op("dve", lambda e: e.tensor_tensor(out=dst_ap, in0=bv[:C, 0:R], in1=dst_ap, op=ALU.add), reads=[b, dst_tl], writes=[dst_tl])


def ssd_consts(cx, es, c):
    s = cx.s
    di = cx.sb(es, "c_di", [8, 8, 128], I32)
    s.op("pool", lambda e: e.iota(di.ap, pattern=[[1, 8], [0, 128]], base=0, channel_multiplier=-1), writes=[di])
    df = cx.sb(es, "c_df", [8, 8, 128], F32)
    s.op("dve", lambda e: e.tensor_copy(out=df.ap, in_=di.ap), reads=[di], writes=[df])
    c["Delta"] = cx.sb(es, "c_Delta", [8, 8, 128], F32)
    s.op("dve", lambda e: e.tensor_single_scalar(out=c["Delta"].ap, in_=df.ap, scalar=0.0, op=ALU.is_equal), reads=[df], writes=[c["Delta"]])
    c["Deltab"] = cx.sb(es, "c_Deltab", [8, 8, 128], BF16)
    s.op("dve", lambda e: e.tensor_copy(out=c["Deltab"].ap, in_=c["Delta"].ap), reads=[c["Delta"]], writes=[c["Deltab"]])
    c["Ub"] = cx.sb(es, "c_Ub", [128, 128], BF16)
    s.op("dve", lambda e: e.tensor_copy(out=c["Ub"].ap, in_=c["U"].ap), reads=[c["U"]], writes=[c["Ub"]])
    for T in (128, 16):
        t = cx.sb(es, "c_sel%d" % T, [128, 128], BF16)
        s.op("dve", lambda e, t=t, T=T: e.tensor_scalar(out=t.ap, in0=c["onesf"].ap, scalar1=c["pf"].ap, scalar2=float(T - 1), op0=ALU.mult, op1=ALU.is_equal),
             reads=[c["onesf"], c["pf"]], writes=[t])
        c["sel%d" % T] = t


def ssd_tile(cx, c, W, T, x_tm, B_tm, BT_ap, CT_ap, bc_tl, dt_sb, zs, S, S_bf, yn_dram):
    s = cx.s
    Ab, Dsk = W["Ab"], W["Dsk"]
    dtA, acs_sb, eacs, acsT_sb = W["dtA"], W["acs_sb"], W["eacs"], W["acsT_sb"]
    s.op("dve", lambda e: e.tensor_tensor(out=dtA.ap[:T, :], in0=dt_sb.ap[:T, :], in1=Ab.ap[:T, :], op=ALU.mult), reads=[dt_sb, Ab], writes=[dtA])
    dth, dtl = W["dtA_h"], W["dtA_l"]
    split_bf(cx, dtA.ap[:T, :], dtA, [dth, dtl], lambda t: t.ap[:T, :])
    b_acs = cx.bank()
    s.op("pe", lambda e: e.matmul(b_acs.ap[:T, :8], lhsT=c["Ub"].ap[:T, :T], rhs=dth.ap[:T, :], start=True, stop=False), reads=[c["Ub"], dth], writes=[b_acs])
    s.op("pe", lambda e: e.matmul(b_acs.ap[:T, :8], lhsT=c["Ub"].ap[:T, :T], rhs=dtl.ap[:T, :], start=False, stop=True), reads=[c["Ub"], dtl], writes=[b_acs])
    s.op("dve", lambda e: e.tensor_copy(out=acs_sb.ap[:T, :], in_=b_acs.ap[:T, :8]), reads=[b_acs], writes=[acs_sb])
    s.op("act", lambda e: e.activation(out=eacs.ap[:T, :], in_=b_acs.ap[:T, :8], func=AF.Exp), reads=[b_acs], writes=[eacs])
    if SUB < 1:
        return
    b_at = cx.bank()
    s.op("pe", lambda e: e.matmul(b_at.ap[:8, :T], lhsT=dth.ap[:T, :], rhs=c["Ub"].ap[:T, :T], start=True, stop=False), reads=[c["Ub"], dth], writes=[b_at])
    s.op("pe", lambda e: e.matmul(b_at.ap[:8, :T], lhsT=dtl.ap[:T, :], rhs=c["Ub"].ap[:T, :T], start=False, stop=True), reads=[c["Ub"], dtl], writes=[b_at])
    s.op("dve", lambda e: e.tensor_copy(out=acsT_sb.ap[:, :T], in_=b_at.ap[:8, :T]), reads=[b_at], writes=[acsT_sb])
    ath, atl, nth, ntl = W["acsT_h"], W["acsT_l"], W["nacsT_h"], W["nacsT_l"]
    split_bf(cx, acsT_sb.ap[:, :T], acsT_sb, [ath, atl], lambda t: t.ap[:, :T])
    s.op("dve", lambda e: e.tensor_scalar(out=nth.ap[:, :T], in0=ath.ap[:, :T], scalar1=-1.0, scalar2=None, op0=ALU.mult), reads=[ath], writes=[nth])
    s.op("dve", lambda e: e.tensor_scalar(out=ntl.ap[:, :T], in0=atl.ap[:, :T], scalar1=-1.0, scalar2=None, op0=ALU.mult), reads=[atl], writes=[ntl])
    if SUB < 2:
        return
    Dmh, Dml = W["Dm_h"], W["Dm_l"]
    for (dm, at_) in ((Dmh, ath), (Dml, atl)):
        s.op("dve", lambda e, dm=dm, at_=at_: e.tensor_tensor(out=dm.ap[:, :8 * T].rearrange("h (g i) -> h g i", g=8), in0=c["Deltab"].ap[:, :, :T],
                                                              in1=at_.ap[:, :T].unsqueeze(1).to_broadcast([8, 8, T]), op=ALU.mult),
             reads=[c["Deltab"], at_], writes=[dm])
    if SUB < 3:
        return
    b_cb = cx.bank()
    s.op("pe", lambda e: e.matmul(b_cb.ap[:T, :T], lhsT=BT_ap, rhs=CT_ap, start=True, stop=True), reads=[bc_tl], writes=[b_cb])
    cbU = W["cbU"]
    s.op("dve", lambda e: e.tensor_tensor(out=cbU.ap[:T, :T], in0=b_cb.ap[:T, :T], in1=c["U"].ap[:T, :T], op=ALU.mult), reads=[b_cb, c["U"]], writes=[cbU])
    if SUB < 4:
        return
    Em, Ee, wts = W["Em"], W["Ee"], W["wts"]
    hp = 4 if 8 * T > 512 else 8
    for h0 in range(0, 8, hp):
        b_sg = cx.bank()
        ncol = hp * T
        s.op("pe", lambda e: e.matmul(b_sg.ap[:T, :ncol], lhsT=c["onesb"].ap[:8, :T], rhs=Dmh.ap[:, h0 * T:(h0 + hp) * T], start=True, stop=False),
             reads=[c["onesb"], Dmh], writes=[b_sg])
        s.op("pe", lambda e: e.matmul(b_sg.ap[:T, :ncol], lhsT=c["onesb"].ap[:8, :T], rhs=Dml.ap[:, h0 * T:(h0 + hp) * T], start=False, stop=False),
             reads=[c["onesb"], Dml], writes=[b_sg])
        s.op("pe", lambda e: e.matmul(b_sg.ap[:T, :ncol], lhsT=nth.ap[:, :T], rhs=c["Deltab"].ap[:, h0:h0 + hp, :T], start=False, stop=False),
             reads=[nth, c["Deltab"]], writes=[b_sg])
        s.op("pe", lambda e: e.matmul(b_sg.ap[:T, :ncol], lhsT=ntl.ap[:, :T], rhs=c["Deltab"].ap[:, h0:h0 + hp, :T], start=False, stop=True),
             reads=[ntl, c["Deltab"]], writes=[b_sg])
        s.op("dve", lambda e: e.tensor_scalar(out=Em.ap[:T, :ncol], in0=b_sg.ap[:T, :ncol], scalar1=0.0, scalar2=None, op0=ALU.min), reads=[b_sg], writes=[Em])
        s.op("act", lambda e: e.activation(out=Ee.ap[:T, :ncol], in_=Em.ap[:T, :ncol], func=AF.Exp), reads=[Em], writes=[Ee])
        s.op("dve", lambda e: e.tensor_tensor(out=wts.ap[:T, h0 * T:(h0 + hp) * T].rearrange("p (g i) -> p g i", g=hp),
                                              in0=Ee.ap[:T, :ncol].rearrange("p (g i) -> p g i", g=hp),
                                              in1=cbU.ap[:T, :T].unsqueeze(1).to_broadcast([T, hp, T]), op=ALU.mult), reads=[Ee, cbU], writes=[wts])
    if SUB < 5:
        return
    xdt, xdtt = W["xdt"], W["xdtt"]
    s.op("dve", lambda e: e.tensor_tensor(out=xdt.ap[:T, :].rearrange("p (h d) -> p h d", h=8), in0=x_tm.ap[:T, :].rearrange("p (h d) -> p h d", h=8),
                                          in1=dt_sb.ap[:T, :].unsqueeze(2).to_broadcast([T, 8, 64]), op=ALU.mult), reads=[x_tm, dt_sb], writes=[xdt])
    b_yd = cx.bank()
    for h in range(8):
        s.op("pe", lambda e, h=h: e.matmul(b_yd.ap[:T, h * 64:(h + 1) * 64], lhsT=wts.ap[:T, h * T:(h + 1) * T], rhs=xdt.ap[:T, h * 64:(h + 1) * 64],
                                           start=True, stop=True), reads=[wts, xdt], writes=[b_yd])
    if SUB < 6:
        return
    b_yo = cx.bank()
    s.op("pe", lambda e: e.matmul(b_yo.ap[:T, :], lhsT=CT_ap, rhs=S_bf.ap, start=True, stop=True), reads=[bc_tl, S_bf], writes=[b_yo])
    if SUB < 7:
        return
    t1, t2 = W["t1"], W["t2"]
    v3 = lambda ap: ap.rearrange("p (h d) -> p h d", h=8)
    s.op("dve", lambda e: e.tensor_tensor(out=v3(t1.ap[:T, :]), in0=v3(b_yo.ap[:T, :]), in1=eacs.ap[:T, :].unsqueeze(2).to_broadcast([T, 8, 64]), op=ALU.mult),
         reads=[b_yo, eacs], writes=[t1])
    s.op("dve", lambda e: e.tensor_tensor(out=t2.ap[:T, :], in0=b_yd.ap[:T, :], in1=t1.ap[:T, :], op=ALU.add), reads=[b_yd, t1], writes=[t2])
    s.op("dve", lambda e: e.tensor_tensor(out=v3(t1.ap[:T, :]), in0=v3(x_tm.ap[:T, :]), in1=Dsk.ap[:T, :].unsqueeze(2).to_broadcast([T, 8, 64]), op=ALU.mult),
         reads=[x_tm, Dsk], writes=[t1])
    s.op("dve", lambda e: e.tensor_tensor(out=t2.ap[:T, :], in0=t2.ap[:T, :], in1=t1.ap[:T, :], op=ALU.add), reads=[t2, t1], writes=[t2])
    s.op("dve", lambda e: e.tensor_tensor(out=t2.ap[:T, :], in0=t2.ap[:T, :], in1=zs.ap[:T, :], op=ALU.mult), reads=[t2, zs], writes=[t2])
    if SUB < 8:
        return
    rms_rstd(cx, c, t2.ap[:T, :], t2, T, 512, W["junkf"], W["ssq"], W["rstd"])
    yn = W["yn"][W["ynrot"][0] % 2]
    W["ynrot"][0] += 1
    s.op("dve", lambda e: e.tensor_scalar(out=yn.ap[:T, :], in0=t2.ap[:T, :], scalar1=W["rstd"].ap[:T, :], scalar2=None, op0=ALU.mult), reads=[t2, W["rstd"]], writes=[yn])
    Yd, NBv, tok0 = yn_dram
    bt_ = cx.bank()
    btv = bt_.ap.bitcast(BF16)
    for j in range(4):
        s.op("pe", lambda e, j=j: e.transpose(out=btv[:, j * 128:j * 128 + T], in_=yn.ap[:T, j * 128:(j + 1) * 128], identity=c["ident"].ap[:T, :T]),
             reads=[yn, c["ident"]], writes=[bt_])
    ynT = W["ynT"][W["ynrot"][0] % 2]
    evac(cx, ynT.ap[:, :, :T], ynT, btv[:, :512].rearrange("p (j t) -> p j t", j=4)[:, :, :T], bt_)
    tb, off = tok0 // 256, tok0 % 256
    s.dma("pool", Yd.rearrange("(k n) p t -> p k n t", k=4)[:, :, tb, off:off + T], ynT.ap[:, :, :T], reads=[ynT])
    if SUB < 9:
        return
    b_al = cx.bank()
    ach, acl = W["acs_h"], W["acs_l"]
    split_bf(cx, acs_sb.ap[:T, :], acs_sb, [ach, acl], lambda t: t.ap[:T, :])
    s.op("pe", lambda e: e.matmul(b_al.ap[:, :8], lhsT=c["sel%d" % T].ap[:T, :], rhs=ach.ap[:T, :], start=True, stop=False), reads=[c["sel%d" % T], ach], writes=[b_al])
    s.op("pe", lambda e: e.matmul(b_al.ap[:, :8], lhsT=c["sel%d" % T].ap[:T, :], rhs=acl.ap[:T, :], start=False, stop=True), reads=[c["sel%d" % T], acl], writes=[b_al])
    cd, tl = W["cd"], W["tl"]
    s.op("act", lambda e: e.activation(out=cd.ap, in_=b_al.ap[:, :8], func=AF.Exp), reads=[b_al], writes=[cd])
    s.op("dve", lambda e: e.tensor_tensor(out=tl.ap[:T, :], in0=b_al.ap[:T, :8], in1=acs_sb.ap[:T, :], op=ALU.subtract), reads=[b_al, acs_sb], writes=[tl])
    s.op("act", lambda e: e.activation(out=tl.ap[:T, :], in_=tl.ap[:T, :], func=AF.Exp), reads=[tl], writes=[tl])
    s.op("dve", lambda e: e.tensor_tensor(out=v3(xdtt.ap[:T, :]), in0=v3(xdt.ap[:T, :]), in1=tl.ap[:T, :].unsqueeze(2).to_broadcast([T, 8, 64]), op=ALU.mult),
         reads=[xdt, tl], writes=[xdtt])
    if SUB < 10:
        return
    b_st = cx.bank()
    s.op("pe", lambda e: e.matmul(b_st.ap, lhsT=B_tm.ap[:T, :], rhs=xdtt.ap[:T, :], start=True, stop=True), reads=[B_tm, xdtt], writes=[b_st])
    s.op("dve", lambda e: e.tensor_tensor(out=v3(S.ap), in0=v3(S.ap), in1=cd.ap.unsqueeze(2).to_broadcast([128, 8, 64]), op=ALU.mult), reads=[S, cd], writes=[S])
    s.op("dve", lambda e: e.tensor_tensor(out=S.ap, in0=S.ap, in1=b_st.ap, op=ALU.add), reads=[S, b_st], writes=[S])
    s.op("act", lambda e: e.copy(out=S_bf.ap, in_=S.ap), reads=[S], writes=[S_bf])


def pass_s(cx, c, SEQ, DB, x_all, w_s, nrm, cw_d, cb_d, dtb_d, alog_d, dsk_d, sconv_d, sssm_d, Yn, conv_o, ssm_o):
    s = cx.s
    NTOK = SEQ + DB * 16
    with ExitStack() as es:
        ssd_consts(cx, es, c)
        nrm_t = cx.sb(es, "s_nrm", [128, KC], F32)
        s.dma("sp", nrm_t.ap, nrm, writes=[nrm_t])
        ws = cx.sb(es, "ws_pre", [128, KC, 1288], BF16)
        with ExitStack() as es_st:
            stage = cx.sb(es_st, "s_stage", [128, 1288], F32)
            for kc in range(KC):
                s.dma("sp", stage.ap, w_s[:, kc, :], writes=[stage])
                s.op("act", lambda e, kc=kc: e.activation(out=ws.ap[:, kc, :], in_=stage.ap, func=AF.Copy, scale=nrm_t.ap[:, kc:kc + 1]),
                     reads=[stage, nrm_t], writes=[ws])
        s.barrier()
        cw = cx.sb(es, "s_cw", [128, 6, 4], F32)
        s.dma("sp", cw.ap, cw_d, writes=[cw])
        cb = cx.sb(es, "s_cb", [128, 6], F32)
        s.dma("sp", cb.ap, cb_d, writes=[cb])
        W = {}
        dtb = cx.sb(es, "s_dtb", [128, 8], F32)
        s.dma("sp", dtb.ap, dtb_d[0:1, :].to_broadcast([128, 8]), writes=[dtb])
        W["Ab"] = cx.sb(es, "s_Ab", [128, 8], F32)
        s.dma("sp", W["Ab"].ap, alog_d[0:1, :].to_broadcast([128, 8]), writes=[W["Ab"]])
        s.op("act", lambda e: e.activation(out=W["Ab"].ap, in_=W["Ab"].ap, func=AF.Exp), reads=[W["Ab"]], writes=[W["Ab"]])
        s.op("dve", lambda e: e.tensor_scalar(out=W["Ab"].ap, in0=W["Ab"].ap, scalar1=-1.0, scalar2=None, op0=ALU.mult), reads=[W["Ab"]], writes=[W["Ab"]])
        W["Dsk"] = cx.sb(es, "s_Dsk", [128, 8], F32)
        s.dma("sp", W["Dsk"].ap, dsk_d[0:1, :].to_broadcast([128, 8]), writes=[W["Dsk"]])
        W2 = [dict(W), dict(W)]
        for nm, shp, dt_ in (("dtA", [128, 8], F32), ("acs_sb", [128, 8], F32), ("eacs", [128, 8], F32), ("acsT_sb", [8, 128], F32),
                             ("cbU", [128, 128], F32), ("Em", [128, 512], F32),
                             ("Ee", [128, 512], F32), ("wts", [128, 1024], BF16), ("xdt", [128, 512], BF16), ("xdtt", [128, 512], BF16),
                             ("t1", [128, 512], F32), ("t2", [128, 512], F32), ("junkf", [128, 512], F32), ("ssq", [128, 1], F32),
                             ("rstd", [128, 1], F32), ("cd", [128, 8], F32), ("tl", [128, 8], F32),
                             ("dtA_h", [128, 8], BF16), ("dtA_l", [128, 8], BF16), ("acs_h", [128, 8], BF16), ("acs_l", [128, 8], BF16),
                             ("acsT_h", [8, 128], BF16), ("acsT_l", [8, 128], BF16), ("nacsT_h", [8, 128], BF16), ("nacsT_l", [8, 128], BF16),
                             ("Dm_h", [8, 1024], BF16), ("Dm_l", [8, 1024], BF16)):
            for wi in range(2):
                W2[wi][nm] = cx.sb(es, "w%d_" % wi + nm, shp, dt_)
        yn_l = [cx.sb(es, "w_yn%d" % i, [128, 512], BF16) for i in range(2)]
        ynT_l = [cx.sb(es, "w_ynT%d" % i, [128, 4, 128], BF16) for i in range(2)]
        rot = [0]
        for wi in range(2):
            W2[wi]["yn"] = yn_l
            W2[wi]["ynT"] = ynT_l
            W2[wi]["ynrot"] = rot
        tsrot = [0]
        TW = {"parts": [cx.sb(es, "tw_p%d" % i, [128, 128], BF16) for i in range(3)] + [cx.sb(es, "tw_r", [128, 128], F32)]}
        xt = [cx.sb(es, "s_xt%d" % i, [128, D], F32) for i in range(2)]
        xs4 = [cx.sb(es, "s_xs%d" % i, [128, D], BF16) for i in range(2)] * 2
        junk = cx.sb(es, "s_junk", [128, D], BF16)
        hT2 = [cx.sb(es, "s_hT0", [128, KC, 512], BF16)] * 2
        hTc = [hT2[0]]
        ssq = cx.sb(es, "s_ssq", [128, 1], F32)
        rstd = cx.sb(es, "s_rstd", [128, 1], F32)
        xp = [cx.sb(es, "s_xp%d" % i, [128, 6, 515], F32) for i in range(2)]
        acc = cx.sb(es, "s_acc", [128, 512], F32)
        xcT = cx.sb(es, "s_xcT", [128, 6, 512], BF16)
        x_tm2 = [cx.sb(es, "s_xtm%d" % i, [128, 512], BF16) for i in range(2)]
        B_tm2 = [cx.sb(es, "s_Btm%d" % i, [128, 128], BF16) for i in range(2)]
        zs2 = [cx.sb(es, "s_zs%d" % i, [128, 512], F32) for i in range(2)]
        dt_sb2 = [cx.sb(es, "s_dt%d" % i, [128, 8], F32) for i in range(2)]
        S = cx.sb(es, "s_S", [128, 512], F32)
        S_bf = cx.sb(es, "s_Sbf", [128, 512], BF16)
        so = cx.sb(es, "s_so", [128, 4, 128], F32)
        sc = cx.sb(es, "s_sc", [128, 768], F32)
        cso = cx.sb(es, "s_cso", [128, 768], F32)
        csi = cx.sb(es, "s_csi", [128, 6, 48], F32)

        def proj_fm(ST):
            pass

        def conv_chunk(xv, ch, ncols_view, out_ap_fn):
            a = out_ap_fn("acc")
            s.op("dve", lambda e: e.tensor_scalar(out=a, in0=xv(0), scalar1=cw.ap[:, ch, 0:1], scalar2=None, op0=ALU.mult), reads=[xv.tl, cw], writes=[acc])
            for k in range(1, 4):
                s.op("dve", lambda e, k=k: e.scalar_tensor_tensor(out=a, in0=xv(k), scalar=cw.ap[:, ch, k:k + 1], in1=a, op0=ALU.mult, op1=ALU.add),
                     reads=[xv.tl, cw, acc], writes=[acc])
            s.op("act", lambda e: e.activation(out=out_ap_fn("out"), in_=a, func=AF.Silu, bias=cb.ap[:, ch:ch + 1]), reads=[acc, cb], writes=[xcT])

        def token_front(col0, T):
            pi_ = tsrot[0] % 2
            tsrot[0] += 1
            W, x_tm, B_tm, zs, dt_sb, hT = W2[pi_], x_tm2[pi_], B_tm2[pi_], zs2[pi_], dt_sb2[pi_], hTc[0]
            hs = lambda kc: hT.ap[:, kc, col0:col0 + T]
            bz = cx.bank()
            mm_acc(cx, bz.ap[:T, :], [(hs(kc), ws.ap[:, kc, 0:512]) for kc in range(KC)], [hT, ws], bz)
            bd = cx.bank()
            mm_acc(cx, bd.ap[:T, :8], [(hs(kc), ws.ap[:, kc, 1280:1288]) for kc in range(KC)], [hT, ws], bd)
            b = cx.bank()
            bv = b.ap.bitcast(BF16)
            for j in range(5):
                s.op("pe", lambda e, j=j: e.transpose(out=bv[:T, j * 128:(j + 1) * 128], in_=xcT.ap[:, j, col0:col0 + T], identity=c["ident"].ap),
                     reads=[xcT, c["ident"]], writes=[b])
            s.op("act", lambda e: e.activation(out=zs.ap[:T, :], in_=bz.ap[:T, :], func=AF.Silu), reads=[bz], writes=[zs])
            s.op("dve", lambda e: e.tensor_tensor(out=dt_sb.ap[:T, :], in0=bd.ap[:T, :8], in1=dtb.ap[:T, :], op=ALU.add), reads=[bd, dtb], writes=[dt_sb])
            s.op("act", lambda e: e.activation(out=dt_sb.ap[:T, :], in_=dt_sb.ap[:T, :], func=AF.Exp), reads=[dt_sb], writes=[dt_sb])
            s.op("act", lambda e: e.activation(out=dt_sb.ap[:T, :], in_=dt_sb.ap[:T, :], func=AF.Ln, bias=1.0), reads=[dt_sb], writes=[dt_sb])
            evac(cx, x_tm.ap[:T, :], x_tm, bv[:T, 0:512], b)
            evac(cx, B_tm.ap[:T, :], B_tm, bv[:T, 512:640], b)
            return (W, x_tm, B_tm, zs, dt_sb, col0, T)

        def token_back(fr, yn_rows):
            W, x_tm, B_tm, zs, dt_sb, col0, T = fr
            if STOP < 4:
                return
            ssd_tile(cx, c, W, T, x_tm, B_tm, xcT.ap[:, 4, col0:col0 + T], xcT.ap[:, 5, col0:col0 + T], xcT, dt_sb, zs, S, S_bf, yn_rows)

        def token_side(col0, T, yn_rows):
            token_back(token_front(col0, T), yn_rows)

        def state_out(idx):
            for j in range(4):
                transpose_f32(cx, c, TW, S.ap[:, j * 128:(j + 1) * 128], S, 128, 128, so.ap[:, j, :], so)
            s.dma("pool", ssm_o[idx].rearrange("(j p) n -> p j n", p=128), so.ap, reads=[so])

        if STOP < 2:
            return
        s.op("dve", lambda e: e.memset(S.ap, 0.0), writes=[S])
        s.op("dve", lambda e: e.memset(S_bf.ap, 0.0), writes=[S_bf])
        s.op("dve", lambda e: e.memset(xp[1].ap[:, :, 0:515], 0.0), writes=[xp[1]])
        nstp = SEQ // 512
        for st in range(nstp):
            T0 = st * 512
            cur, prv = xp[st % 2], xp[(st + 1) % 2]
            hT = hT2[st % 2]
            hTc[0] = hT
            for sub in range(4):
                load_norm_T(cx, c, x_all[T0 + sub * 128:T0 + (sub + 1) * 128, :], 128, xt[sub % 2], xs4[sub], junk, hT, sub * 128, (ssq, rstd))
            s.op("dve", lambda e: e.tensor_copy(out=cur.ap[:, :, 0:3], in_=prv.ap[:, :, 512:515]), reads=[prv], writes=[cur])
            for ch in range(6):
                b = cx.bank()
                mm_acc(cx, b.ap, [(ws.ap[:, kc, 512 + ch * 128:512 + (ch + 1) * 128], hT.ap[:, kc, :]) for kc in range(KC)], [ws, hT], b)
                evac(cx, cur.ap[:, ch, 3:515], cur, b.ap, b)
            for ch in range(6):
                xv = lambda k, ch=ch: cur.ap[:, ch, k:k + 512]
                xv.tl = cur
                conv_chunk(xv, ch, 512, lambda w, ch=ch: acc.ap if w == "acc" else xcT.ap[:, ch, :])
            if STOP < 3:
                continue
            fr = token_front(0, 128)
            for sub in range(4):
                nxt = token_front((sub + 1) * 128, 128) if sub < 3 else None
                token_back(fr, (Yn, NTOK // 256, T0 + sub * 128))
                fr = nxt
        last = xp[(nstp - 1) % 2]
        for ch in range(6):
            s.dma("pool", conv_o[0][:, ch * 128:(ch + 1) * 128].rearrange("k p -> p k"), last.ap[:, ch, 512:515], reads=[last], allow_slow_non_contiguous=True)
        state_out(0)
        if STOP < 6:
            return
        NS = DB * 16
        nsub = (NS + 127) // 128
        hT = hT2[0]
        hTc[0] = hT
        for sub in range(nsub):
            nt = min(128, NS - sub * 128)
            load_norm_T(cx, c, x_all[SEQ + sub * 128:SEQ + sub * 128 + nt, :], nt, xt[sub % 2], xs4[sub], junk, hT, sub * 128, (ssq, rstd))
        xq = xp[0]
        xq4 = xq.ap[:, :, 0:DB * 19].rearrange("p c (b t) -> p c b t", t=19)
        s.dma("sp", sc.ap[:DB * 3, :], sconv_d.rearrange("b k f -> (b k) f"), writes=[sc])
        for ch in range(6):
            transpose_f32(cx, c, TW, sc.ap[:DB * 3, ch * 128:(ch + 1) * 128], sc, DB * 3, 128, csi.ap[:, ch, :DB * 3], csi)
            evac(cx, xq4[:, ch, :, 0:3], xq, csi.ap[:, ch, :DB * 3].rearrange("p (b k) -> p b k", k=3), csi)
        for ch in range(6):
            b = cx.bank()
            mm_acc(cx, b.ap[:, :NS], [(ws.ap[:, kc, 512 + ch * 128:512 + (ch + 1) * 128], hT.ap[:, kc, :NS]) for kc in range(KC)], [ws, hT], b)
            evac(cx, xq4[:, ch, :, 3:19], xq, b.ap[:, :NS].rearrange("p (b t) -> p b t", t=16), b)
        for ch in range(6):
            xv = lambda k, ch=ch: xq4[:, ch, :, k:k + 16]
            xv.tl = xq
            conv_chunk(xv, ch, NS, lambda w, ch=ch: (acc.ap[:, :NS].rearrange("p (b t) -> p b t", t=16) if w == "acc"
                                                      else xcT.ap[:, ch, :NS].rearrange("p (b t) -> p b t", t=16)))
        for ch in range(6):
            evac(cx, csi.ap[:, ch, :DB * 3].rearrange("p (b k) -> p b k", k=3), csi, xq4[:, ch, :, 16:19], xq)
            transpose_f32(cx, c, TW, csi.ap[:, ch, :DB * 3], csi, 128, DB * 3, cso.ap[:DB * 3, ch * 128:(ch + 1) * 128], cso)
        s.dma("pool", conv_o[1:1 + DB].rearrange("b k f -> (b k) f"), cso.ap[:DB * 3, :], reads=[cso])
        for b_ in range(DB):
            s.dma("sp", so.ap, sssm_d[b_].rearrange("(j p) n -> p j n", p=128), writes=[so])
            for j in range(4):
                transpose_f32(cx, c, TW, so.ap[:, j, :], so, 128, 128, S.ap[:, j * 128:(j + 1) * 128], S)
            s.op("act", lambda e: e.copy(out=S_bf.ap, in_=S.ap), reads=[S], writes=[S_bf])
            token_side(b_ * 16, 16, (Yn, NTOK // 256, SEQ + b_ * 16))
            state_out(1 + b_)


def build_l1(SEQ, DB, PAST, fused=False):
    NTOK = SEQ + DB * 16
    nc = bass.Bass("TRN2", target_bir_lowering=False)
    x_all = nc.dram_tensor("x_all", [NTOK, D], F32, kind="ExternalInput").ap()
    w_m = nc.dram_tensor("w_m", [128, KC, 1344], F32, kind="ExternalInput").ap()
    w_q = nc.dram_tensor("w_q", [128, 4, 384], F32, kind="ExternalInput").ap()
    w_kv = nc.dram_tensor("w_kv", [128, 4, 512], F32, kind="ExternalInput").ap()
    nrm = nc.dram_tensor("nrm", [128, KC], F32, kind="ExternalInput").ap()
    qnw = nc.dram_tensor("qnw", [128, 4], F32, kind="ExternalInput").ap()
    kvw = nc.dram_tensor("kvw", [1, 512], F32, kind="ExternalInput").ap()
    cache_kv = nc.dram_tensor("cache_kv", [DB, PAST, 512], F32, kind="ExternalInput").ap()
    cache_kr = nc.dram_tensor("cache_kr", [DB, PAST, 64], F32, kind="ExternalInput").ap()
    kvlat = nc.dram_tensor("kvlat", [NTOK, 512], F32, kind="ExternalOutput").ap()
    krope = nc.dram_tensor("krope", [NTOK, 64], F32, kind="ExternalOutput").ap()
    NB = NTOK // 256
    AT = (nc.dram_tensor("ATc", [2 * NB, 128, 256], BF16).ap() if fused
          else nc.dram_tensor("ATc", [2 * NB, 128, 256], BF16, kind="ExternalOutput").ap())
    w_s = nc.dram_tensor("w_s", [128, KC, 1288], F32, kind="ExternalInput").ap()
    cw_d = nc.dram_tensor("cw", [128, 6, 4], F32, kind="ExternalInput").ap()
    cb_d = nc.dram_tensor("cb", [128, 6], F32, kind="ExternalInput").ap()
    dtb_d = nc.dram_tensor("dtb", [1, 8], F32, kind="ExternalInput").ap()
    alog_d = nc.dram_tensor("alog", [1, 8], F32, kind="ExternalInput").ap()
    dsk_d = nc.dram_tensor("dsk", [1, 8], F32, kind="ExternalInput").ap()
    sconv_d = nc.dram_tensor("sconv", [DB, 3, 768], F32, kind="ExternalInput").ap()
    sssm_d = nc.dram_tensor("sssm", [DB, 512, 128], F32, kind="ExternalInput").ap()
    Yn = (nc.dram_tensor("YTc", [4 * NB, 128, 256], BF16).ap() if fused
          else nc.dram_tensor("YTc", [4 * NB, 128, 256], BF16, kind="ExternalOutput").ap())
    if fused:
        GY = nc.dram_tensor("GY", [4 * NB, 1024, 256], BF16).ap()
        GA = nc.dram_tensor("GA", [2 * NB, 1024, 256], BF16).ap()
        y2 = nc.dram_tensor("y2", [SEQ // NCORES + DB * 16 // NCORES, D], F32, kind="ExternalOutput").ap()
        l2w = declare_l2_weights(nc)
        l2s = declare_l2_scratch(nc)
    conv_o = nc.dram_tensor("conv_o", [1 + DB, 3, 768], F32, kind="ExternalOutput").ap()
    ssm_o = nc.dram_tensor("ssm_o", [1 + DB, 512, 128], F32, kind="ExternalOutput").ap()
    QT = nc.dram_tensor("QT", [128, 2, NTOK], BF16).ap()
    QRT = nc.dram_tensor("QRT", [64, 2, NTOK], BF16).ap()
    KT = nc.dram_tensor("KT", [128, 2, NTOK], BF16).ap()
    KRT = nc.dram_tensor("KRT", [64, NTOK], BF16).ap()
    V = nc.dram_tensor("V", [NTOK, 256], BF16).ap()
    GT = nc.dram_tensor("GT", [128, 2, NTOK], BF16).ap()
    KTc = nc.dram_tensor("KTc", [128, 2, DB * PAST], BF16).ap()
    KRTc = nc.dram_tensor("KRTc", [64, DB * PAST], BF16).ap()
    Vc = nc.dram_tensor("Vc", [DB * PAST, 256], BF16).ap()
    cx = Ctx(nc)
    s = cx.s
    SCALE = 192.0 ** -0.5
    with ExitStack() as es0:
        c = make_consts(cx, es0)
        setup_small_consts(cx, es0, c)
        with ExitStack() as es:
            nrm_t = cx.sb(es, "nrm_t", [128, KC], F32)
            s.dma("sp", nrm_t.ap, nrm, writes=[nrm_t])
            qnw_t = cx.sb(es, "qnw_t", [128, 4], F32)
            s.dma("sp", qnw_t.ap, qnw, writes=[qnw_t])
            s.op("dve", lambda e: e.tensor_scalar(out=qnw_t.ap, in0=qnw_t.ap, scalar1=SCALE, scalar2=None, op0=ALU.mult), reads=[qnw_t], writes=[qnw_t])
            kvw_b = cx.sb(es, "kvw_b", [128, 512], F32)
            s.dma("sp", kvw_b.ap, kvw[0:1, :].to_broadcast([128, 512]), writes=[kvw_b])
            stage = cx.sb(es, "stage", [128, 1344], F32)
            wm = load_weights_bf16(cx, es, "wm", w_m, KC, 1344, scale_tl=nrm_t, stage=stage)
            wq = load_weights_bf16(cx, es, "wq", w_q, 4, 384, scale_tl=qnw_t, stage=stage)
            wkv = load_weights_bf16(cx, es, "wkv", w_kv, 4, 512, stage=stage)
            xt = [cx.sb(es, "xt%d" % i, [128, D], F32) for i in range(2)]
            xs4 = [cx.sb(es, "xs%d" % i, [128, D], BF16) for i in range(4)]
            junk = cx.sb(es, "junk", [128, D], BF16)
            junkf = cx.sb(es, "junkf", [128, 512], F32)
            hT2 = [cx.sb(es, "hT%d" % i, [128, KC, 512], BF16) for i in range(2)]
            ssq = cx.sb(es, "ssq", [128, 1], F32)
            rstd = cx.sb(es, "rstd", [128, 1], F32)
            ssq2 = cx.sb(es, "ssq2", [128, 1], F32)
            rstd2 = cx.sb(es, "rstd2", [128, 1], F32)
            pos = cx.sb(es, "pos", [128, 4], F32)
            tabs = [cx.sb(es, "tab%d" % i, [128, 4, 32], F32 if i != 2 else I32) for i in range(6)]
            cos, sin = tabs[3], tabs[4]
            ckvn = [cx.sb(es, "ckvn%d" % i, [128, 512], F32) for i in range(2)]
            ckvb2 = [cx.sb(es, "ckvb%d" % i, [128, 512], BF16) for i in range(2)]
            cqnb2 = [cx.sb(es, "cqnb%d" % i, [128, 512], BF16) for i in range(2)]
            krb2 = [cx.sb(es, "krb2_%d" % i, [128, 64], BF16) for i in range(2)]
            krt2 = cx.sb(es, "krt2", [128, 128], F32)
            kro = [cx.sb(es, "kro%d" % i, [128, 64], F32) for i in range(2)]
            krt = cx.sb(es, "krt", [128, 128], F32)
            krb = cx.sb(es, "krb", [128, 64], BF16)
            qro = cx.sb(es, "qro", [128, 128], F32)
            qrb = cx.sb(es, "qrb", [128, 128], BF16)
            cqnT = cx.sb(es, "cqnT", [128, 4, 512], BF16)
            ckvT = cx.sb(es, "ckvT", [128, 4, 512], BF16)
            krT = cx.sb(es, "krT", [64, 512], BF16)
            qrT = cx.sb(es, "qrT", [64, 2, 512], BF16)
            qst = cx.sb(es, "qst", [128, 2, 512], BF16)
            kst = cx.sb(es, "kst", [128, 2, 512], BF16)
            gst = cx.sb(es, "gst", [128, 2, 512], BF16)
            vst = cx.sb(es, "vst", [128, 4, 256], BF16)
            cst = [cx.sb(es, "cst%d" % i, [128, 4, 512], F32) for i in range(2)]
            cstb = cx.sb(es, "cstb", [128, 4, 512], BF16)
            ckr = cx.sb(es, "ckr", [128, 4, 64], F32)
            ckrb = cx.sb(es, "ckrb", [128, 4, 64], BF16)
            nst = (NTOK + 511) // 512
            for st in range(nst if "M" in PH else 0):
                T0 = st * 512
                ST = min(512, NTOK - T0)
                nsub = ST // 128
                is_prompt = T0 < SEQ
                hT = hT2[st % 2]
                for sub in range(nsub):
                    load_norm_T(cx, c, x_all[T0 + sub * 128:T0 + (sub + 1) * 128, :], 128, xt[sub % 2], xs4[sub], junk, hT, sub * 128, (ssq, rstd))
                if is_prompt:
                    for sub in range(nsub):
                        s.op("dve", lambda e, sub=sub: e.tensor_scalar(out=pos.ap[:, sub:sub + 1], in0=c["pf"].ap, scalar1=float(T0 + sub * 128), scalar2=None, op0=ALU.add),
                             reads=[c["pf"]], writes=[pos])
                else:
                    for sub in range(nsub):
                        s.op("dve", lambda e, sub=sub: e.tensor_scalar(out=pos.ap[:, sub:sub + 1], in0=c["pm16"].ap, scalar1=float(PAST), scalar2=None, op0=ALU.add),
                             reads=[c["pm16"]], writes=[pos])
                rope_tables(cx, c, tabs, pos, nsub)
                def stA(sub):
                    hs = lambda kc: hT.ap[:, kc, sub * 128:(sub + 1) * 128]
                    o3 = 3 * (sub % 2)
                    bq = cx.banks[o3]
                    mm_acc(cx, bq.ap, [(hs(kc), wm.ap[:, kc, 0:512]) for kc in range(KC)], [hT, wm], bq)
                    b1 = cx.banks[o3 + 1]
                    mm_acc(cx, b1.ap, [(hs(kc), wm.ap[:, kc, 512:1024]) for kc in range(KC)], [hT, wm], b1)
                    b2 = cx.banks[o3 + 2]
                    mm_acc(cx, b2.ap[:, :64], [(hs(kc), wm.ap[:, kc, 1024:1088]) for kc in range(KC)], [hT, wm], b2)
                    return bq, b1, b2

                def stB(sub, banks):
                    bq, b1, b2 = banks
                    t0 = T0 + sub * 128
                    cqb, ckb, krb_ = cqnb2[sub % 2], ckvb2[sub % 2], krb2[sub % 2]
                    rms_rstd(cx, c, bq.ap, bq, 128, 512, junkf, ssq2, rstd2)
                    s.op("dve", lambda e: e.tensor_scalar(out=cqb.ap, in0=bq.ap, scalar1=rstd2.ap, scalar2=None, op0=ALU.mult),
                         reads=[bq, rstd2], writes=[cqb])
                    rms_rstd(cx, c, b1.ap, b1, 128, 512, junkf, ssq2, rstd2)
                    ck = ckvn[sub % 2]
                    s.op("dve", lambda e: e.scalar_tensor_tensor(out=ck.ap, in0=b1.ap, scalar=rstd2.ap, in1=kvw_b.ap, op0=ALU.mult, op1=ALU.mult),
                         reads=[b1, rstd2, kvw_b], writes=[ck])
                    s.dma("pool", kvlat[t0:t0 + 128, :], ck.ap, reads=[ck])
                    s.op("act", lambda e: e.copy(out=ckb.ap, in_=ck.ap), reads=[ck], writes=[ckb])
                    ko = kro[sub % 2]
                    apply_rope(cx, b2.ap[:, :64].rearrange("p (h d) -> p h d", h=1), b2, 1, cos.ap[:, sub, :], sin.ap[:, sub, :], [cos, sin], ko, krt, 128)
                    s.dma("pool", krope[t0:t0 + 128, :], ko.ap, reads=[ko])
                    s.op("act", lambda e: e.copy(out=krb_.ap, in_=ko.ap), reads=[ko], writes=[krb_])

                def stC(sub):
                    cqb, ckb, krb_ = cqnb2[sub % 2], ckvb2[sub % 2], krb2[sub % 2]
                    transpose_to(cx, c, cqb, lambda j: cqb.ap[:, j * 128:(j + 1) * 128], 4, 128,
                                 None, cqnT, dst_all=cqnT.ap[:, :, sub * 128:(sub + 1) * 128])
                    transpose_to(cx, c, ckb, lambda j: ckb.ap[:, j * 128:(j + 1) * 128], 4, 128,
                                 None, ckvT, dst_all=ckvT.ap[:, :, sub * 128:(sub + 1) * 128])
                    transpose_to(cx, c, krb_, lambda j: krb_.ap[:, :], 1, 128, lambda j: krT.ap[:, sub * 128:(sub + 1) * 128], krT, rows=64)
                    b3 = cx.bank()
                    mm_acc(cx, b3.ap[:, :128], [(cqnT.ap[:, kc, sub * 128:(sub + 1) * 128], wq.ap[:, kc, 256:384]) for kc in range(4)], [cqnT, wq], b3)
                    apply_rope(cx, b3.ap[:, :128].rearrange("p (h d) -> p h d", h=2), b3, 2, cos.ap[:, sub, :], sin.ap[:, sub, :], [cos, sin], qro, krt2, 128)
                    s.op("act", lambda e: e.copy(out=qrb.ap, in_=qro.ap), reads=[qro], writes=[qrb])
                    transpose_to(cx, c, qrb, lambda j: qrb.ap[:, j * 64:(j + 1) * 64], 2, 128,
                                 lambda j: qrT.ap[:, j, sub * 128:(sub + 1) * 128], qrT, rows=64)

                cx.pool = [6, 7]
                bk_ = stA(0)
                for sub in range(nsub):
                    nxt = stA(sub + 1) if sub + 1 < nsub else None
                    stB(sub, bk_)
                    stC(sub)
                    bk_ = nxt
                cx.pool = list(range(8))
                for h in range(2):
                    b = cx.bank()
                    mm_acc(cx, b.ap[:, :ST], [(wq.ap[:, kc, h * 128:(h + 1) * 128], cqnT.ap[:, kc, :ST]) for kc in range(4)], [wq, cqnT], b)
                    evac(cx, qst.ap[:, h, :ST], qst, b.ap[:, :ST], b)
                s.dma("pool", QT[:, :, T0:T0 + ST], qst.ap[:, :, :ST], reads=[qst])
                s.dma("pool", QRT[:, :, T0:T0 + ST], qrT.ap[:, :, :ST], reads=[qrT])
                s.dma("pool", KRT[:, T0:T0 + ST], krT.ap[:, :ST], reads=[krT])
                for h in range(2):
                    b = cx.bank()
                    mm_acc(cx, b.ap[:, :ST], [(wm.ap[:, kc, 1088 + h * 128:1088 + (h + 1) * 128], hT.ap[:, kc, :ST]) for kc in range(KC)], [wm, hT], b)
                    s.op("act", lambda e, h=h, b=b: e.activation(out=gst.ap[:, h, :ST], in_=b.ap[:, :ST], func=AF.Silu), reads=[b], writes=[gst])
                s.dma("pool", GT[:, :, T0:T0 + ST], gst.ap[:, :, :ST], reads=[gst])
                kv_from_latent(cx, c, ckvT, None, wkv, ST, nsub, kst, vst, KT, V, T0)
            for b_ in range(DB if "C" in PH else 0):
                for st in range(PAST // 512):
                    r0 = st * 512
                    cs = cst[st % 2]
                    s.dma("sp", cs.ap, cache_kv[b_, r0:r0 + 512, :].rearrange("(n p) f -> p n f", p=128), writes=[cs])
                    s.op("act", lambda e, cs=cs: e.copy(out=cstb.ap, in_=cs.ap), reads=[cs], writes=[cstb])
                    for sub in range(4):
                        transpose_to(cx, c, cstb, lambda j, sub=sub: cstb.ap[:, sub, j * 128:(j + 1) * 128], 4, 128,
                                     None, ckvT, dst_all=ckvT.ap[:, :, sub * 128:(sub + 1) * 128])
                    kv_from_latent(cx, c, ckvT, None, wkv, 512, 4, kst, vst, KTc, Vc, b_ * PAST + r0)
                    s.dma("sp", ckr.ap, cache_kr[b_, r0:r0 + 512, :].rearrange("(n p) f -> p n f", p=128), writes=[ckr])
                    s.op("act", lambda e: e.copy(out=ckrb.ap, in_=ckr.ap), reads=[ckr], writes=[ckrb])
                    transpose_to(cx, c, ckrb, lambda j: ckrb.ap[:, j, :], 4, 128, None, krT, rows=64,
                                 dst_all=krT.ap[:, :512].rearrange("p (j t) -> p j t", j=4))
                    s.dma("pool", KRTc[:, b_ * PAST + r0:b_ * PAST + r0 + 512], krT.ap[:, :512], reads=[krT])
        s.barrier()
        with ExitStack() as es:
            if "A" in PH:
                attention_phase(cx, c, es, SEQ, DB, PAST, QT, QRT, KT, KRT, V, GT, KTc, KRTc, Vc, AT)
        s.barrier()
        if "S" in PH:
            cx.force_evac = "act"
            pass_s(cx, c, SEQ, DB, x_all, w_s, nrm, cw_d, cb_d, dtb_d, alog_d, dsk_d, sconv_d, sssm_d, Yn, conv_o, ssm_o)
            cx.force_evac = None
        s.barrier()
        if fused:
            for i in range(2 * NB):
                s.coll(AT[i], GA[i])
            for i in range(4 * NB):
                s.coll(Yn[i], GY[i])
            s.barrier()
            l2_phase(cx, c, es0, SEQ, DB, x_all, GY, GA, *l2w, y2, *l2s)
    s.emit()
    return nc


def l2_phase(cx, c, es0, SEQ, DB, x_all, GY, GA, wg, wssm, wmla, wout, nrm, snw, fnw, y2, WGS, WGM, WS, WM):
    nc, s = cx.nc, cx.s
    NS = DB * 16
    NTOK = SEQ + NS
    NB = NTOK // 256
    PB = SEQ // 256 // NCORES
    SPC = NS // NCORES
    tbs = SEQ // 256
    nrm_t = cx.sb(es0, "l2_nrm", [128, KC], F32)
    s.dma("sp", nrm_t.ap, nrm, writes=[nrm_t])
    snw_t = cx.sb(es0, "l2_snw", [128, 32], F32)
    s.dma("sp", snw_t.ap, snw, writes=[snw_t])
    wo = cx.sb(es0, "l2_wo", [128, 16, 2048], BF16)
    with ExitStack() as es:
        stg = [cx.sb(es, "stg%d" % i, [128, 32, 128], F32) for i in range(2)]
        stb = [cx.sb(es, "stb%d" % i, [128, 32, 128], BF16) for i in range(2)]
        k = 0
        for jj in range(16):
            for (src, c0, nk, scl, dst) in ((wg, jj * 128, 16, nrm_t, WGS), (wg, 2048 + jj * 128, 16, nrm_t, WGM),
                                           (wssm, jj * 128, 32, snw_t, WS), (wmla, jj * 128, 16, None, WM)):
                sg, sbb = stg[k % 2], stb[k % 2]
                k += 1
                s.dma("sp", sg.ap[:, :nk, :], src[:, :, c0:c0 + 128], writes=[sg])
                if scl is not None:
                    s.op("dve", lambda e, sg=sg, sbb=sbb, nk=nk, scl=scl: e.tensor_tensor(out=sbb.ap[:, :nk, :], in0=sg.ap[:, :nk, :],
                                                                                            in1=scl.ap[:, :nk].unsqueeze(2).to_broadcast([128, nk, 128]), op=ALU.mult),
                         reads=[sg, scl], writes=[sbb])
                else:
                    s.op("act", lambda e, sg=sg, sbb=sbb, nk=nk: e.copy(out=sbb.ap[:, :nk, :], in_=sg.ap[:, :nk, :]), reads=[sg], writes=[sbb])
                s.dma("pool", dst[jj], sbb.ap[:, :nk, :], reads=[sbb])
        for kc in range(16):
            sg = stg[kc % 2]
            s.dma("sp", sg.ap.rearrange("p a b -> p (a b)")[:, :2048], wout[:, kc, :], writes=[sg])
            s.op("act", lambda e, sg=sg, kc=kc: e.copy(out=wo.ap[:, kc, :], in_=sg.ap.rearrange("p a b -> p (a b)")[:, :2048]), reads=[sg], writes=[wo])
    s.barrier()
    with ExitStack() as es:
        fnw_b = cx.sb(es, "fnw_b", [128, D], F32)
        s.dma("sp", fnw_b.ap, fnw[0:1, :].to_broadcast([128, D]), writes=[fnw_b])
        xt = [cx.sb(es, "l2_xt0", [128, D], F32)]
        xs = cx.sb(es, "l2_xs", [128, D], BF16)
        junk = cx.sb(es, "l2_junk", [128, D], BF16)
        xr = cx.sb(es, "l2_xr", [128, D], F32)
        xo = cx.sb(es, "l2_xo", [128, D], F32)
        hT = cx.sb(es, "l2_hT", [128, KC, 256], BF16)
        yT = cx.sb(es, "l2_yT", [128, 32, 256], BF16)
        aT = cx.sb(es, "l2_aT", [128, 16, 256], BF16)
        mixT = cx.sb(es, "l2_mixT", [128, 16, 256], BF16)
        wgs = [cx.sb(es, "wgs%d" % i, [128, 16, 128], BF16) for i in range(2)]
        wgm = [cx.sb(es, "wgm%d" % i, [128, 16, 128], BF16) for i in range(2)]
        wsb = [cx.sb(es, "wsb%d" % i, [128, 32, 128], BF16) for i in range(2)]
        wmb = [cx.sb(es, "wmb%d" % i, [128, 16, 128], BF16) for i in range(2)]
        sg1 = cx.sb(es, "sg1", [128, 256], F32)
        sg2 = cx.sb(es, "sg2", [128, 256], F32)
        ssq = cx.sb(es, "l2_ssq", [128, 1], F32)
        rstd = cx.sb(es, "l2_rstd", [128, 1], F32)
        ssq2 = cx.sb(es, "l2_ssq2", [128, 1], F32)
        rstd2 = cx.sb(es, "l2_rstd2", [128, 1], F32)
        def rv(rk, key, mul, add=0):
            if key not in rk:
                rk[key] = rk["r"] * mul + add if add else rk["r"] * mul
            return rk[key]
        GYr = nc.dram_tensor("GYr", [4, PB, 1024 * 256], BF16).ap()
        GAr = nc.dram_tensor("GAr", [2, PB, 1024 * 256], BF16).ap()
        GYs = nc.dram_tensor("GYs", [4, 1024, SPC], BF16).ap()
        GAs = nc.dram_tensor("GAs", [2, 1024, SPC], BF16).ap()
        X2 = nc.dram_tensor("X2", [PB * 256 + SPC, D], F32).ap()
        s.dma("sp", GYr, lambda rk: GY.rearrange("(k n) r t -> k n (r t)", k=4)[:, bass.ds(rv(rk, "pb", PB), PB), :])
        s.dma("sp", GAr, lambda rk: GA.rearrange("(k n) r t -> k n (r t)", k=2)[:, bass.ds(rv(rk, "pb", PB), PB), :])
        s.dma("sp", GYs, lambda rk: GY.rearrange("(k n) r t -> k n r t", k=4)[:, tbs, :, bass.ds(rv(rk, "sp", SPC), SPC)])
        s.dma("sp", GAs, lambda rk: GA.rearrange("(k n) r t -> k n r t", k=2)[:, tbs, :, bass.ds(rv(rk, "sp", SPC), SPC)])
        s.dma("sp", X2[0:PB * 256, :], lambda rk: x_all[bass.ds(rv(rk, "xr", PB * 256), PB * 256), :])
        s.dma("sp", X2[PB * 256:PB * 256 + SPC, :], lambda rk: x_all[bass.ds(rv(rk, "xs", SPC, SEQ), SPC), :])
        s.barrier()
        it = 0
        items = [("p", i) for i in range(PB)] + [("s", 0)]
        for (kind, i) in items:
            if kind == "p":
                ST, subs = 256, [(0, 128), (128, 128)]
                xrow = lambda o, n, i=i: X2[i * 256 + o:i * 256 + o + n, :]
                ysrc = lambda kc, i=i: GYr[kc, i].rearrange("(r p t) -> p r t", p=128, t=256)
                asrc = lambda h, i=i: GAr[h, i].rearrange("(r p t) -> p r t", p=128, t=256)
                yrow0 = i * 256
            else:
                ST, subs = SPC, [(0, SPC)]
                xrow = lambda o, n: X2[PB * 256 + o:PB * 256 + o + n, :]
                ysrc = lambda kc: GYs[kc].rearrange("(r p) t -> p r t", p=128)
                asrc = lambda h: GAs[h].rearrange("(r p) t -> p r t", p=128)
                yrow0 = PB * 256
            for (o, n) in subs:
                load_norm_T(cx, c, xrow(o, n), n, xt[0], xs, junk, hT, o, (ssq, rstd))
            yT4 = yT.ap.rearrange("p (r k) t -> p r k t", k=4)
            for kc in range(4):
                s.dma("sp", yT4[:, :, kc, :ST], ysrc(kc), writes=[yT])
            aT4 = aT.ap.rearrange("p (r h) t -> p r h t", h=2)
            for h in range(2):
                s.dma("sp", aT4[:, :, h, :ST], asrc(h), writes=[aT])
            for jj in range(16):
                ib = it % 2
                it += 1
                s.dma("sp", wgs[ib].ap, WGS[jj], writes=[wgs[ib]])
                s.dma("sp", wgm[ib].ap, WGM[jj], writes=[wgm[ib]])
                s.dma("sp", wsb[ib].ap, WS[jj], writes=[wsb[ib]])
                s.dma("sp", wmb[ib].ap, WM[jj], writes=[wmb[ib]])
                bgs = cx.bank()
                mm_acc(cx, bgs.ap[:, :ST], [(wgs[ib].ap[:, kc, :], hT.ap[:, kc, :ST]) for kc in range(16)], [wgs[ib], hT], bgs)
                bys = cx.bank()
                mm_acc(cx, bys.ap[:, :ST], [(wsb[ib].ap[:, kc, :], yT.ap[:, kc, :ST]) for kc in range(32)], [wsb[ib], yT], bys)
                bgm = cx.bank()
                mm_acc(cx, bgm.ap[:, :ST], [(wgm[ib].ap[:, kc, :], hT.ap[:, kc, :ST]) for kc in range(16)], [wgm[ib], hT], bgm)
                bym = cx.bank()
                mm_acc(cx, bym.ap[:, :ST], [(wmb[ib].ap[:, kc, :], aT.ap[:, kc, :ST]) for kc in range(16)], [wmb[ib], aT], bym)
                s.op("act", lambda e: e.activation(out=sg1.ap[:, :ST], in_=bgs.ap[:, :ST], func=AF.Sigmoid), reads=[bgs], writes=[sg1])
                s.op("act", lambda e: e.activation(out=sg2.ap[:, :ST], in_=bgm.ap[:, :ST], func=AF.Sigmoid), reads=[bgm], writes=[sg2])
                s.op("dve", lambda e: e.tensor_tensor(out=sg1.ap[:, :ST], in0=bys.ap[:, :ST], in1=sg1.ap[:, :ST], op=ALU.mult), reads=[bys, sg1], writes=[sg1])
                s.op("dve", lambda e: e.tensor_tensor(out=sg2.ap[:, :ST], in0=bym.ap[:, :ST], in1=sg2.ap[:, :ST], op=ALU.mult), reads=[bym, sg2], writes=[sg2])
                s.op("dve", lambda e, jj=jj: e.tensor_tensor(out=mixT.ap[:, jj, :ST], in0=sg1.ap[:, :ST], in1=sg2.ap[:, :ST], op=ALU.add), reads=[sg1, sg2], writes=[mixT])
            for (o, n) in subs:
                s.dma("sp", xr.ap[:n, :], xrow(o, n), writes=[xr])
                for cg in range(4):
                    b = cx.bank()
                    mm_acc(cx, b.ap[:n, :], [(mixT.ap[:, kc, o:o + n], wo.ap[:, kc, cg * 512:(cg + 1) * 512]) for kc in range(16)], [mixT, wo], b)
                    s.op("dve", lambda e, cg=cg, b=b: e.tensor_tensor(out=xo.ap[:n, cg * 512:(cg + 1) * 512], in0=b.ap[:n, :], in1=xr.ap[:n, cg * 512:(cg + 1) * 512], op=ALU.add),
                         reads=[b, xr], writes=[xo])
                s.op("act", lambda e: e.activation(out=junk.ap[:n, :], in_=xo.ap[:n, :], func=AF.Square, accum_out=ssq2.ap[:n, :]), reads=[xo], writes=[junk, ssq2])
                s.op("act", lambda e: e.activation(out=rstd2.ap[:n, :], in_=ssq2.ap[:n, :], func=AF.Sqrt, scale=1.0 / D, bias=c["eps"].ap[:n, :]), reads=[ssq2, c["eps"]], writes=[rstd2])
                s.op("dve", lambda e: e.reciprocal(out=rstd2.ap[:n, :], in_=rstd2.ap[:n, :]), reads=[rstd2], writes=[rstd2])
                s.op("dve", lambda e: e.scalar_tensor_tensor(out=xo.ap[:n, :], in0=xo.ap[:n, :], scalar=rstd2.ap[:n, :], in1=fnw_b.ap[:n, :], op0=ALU.mult, op1=ALU.mult),
                     reads=[xo, rstd2, fnw_b], writes=[xo])
                s.dma("pool", y2[yrow0 + o:yrow0 + o + n, :], xo.ap[:n, :], reads=[xo])
    s.barrier()


def build_l2(SEQ, DB):
    NTOK = SEQ + DB * 16
    NB = NTOK // 256
    NT2 = SEQ // NCORES + DB * 16 // NCORES
    nc = bass.Bass("TRN2", target_bir_lowering=False)
    x_all = nc.dram_tensor("x_all", [NTOK, D], F32, kind="ExternalInput").ap()
    GY = nc.dram_tensor("GY", [4 * NB, 1024, 256], BF16, kind="ExternalInput").ap()
    GA = nc.dram_tensor("GA", [2 * NB, 1024, 256], BF16, kind="ExternalInput").ap()
    y2 = nc.dram_tensor("y2", [NT2, D], F32, kind="ExternalOutput").ap()
    l2w = declare_l2_weights(nc)
    cx = Ctx(nc)
    with ExitStack() as es0:
        c = make_consts(cx, es0)
        setup_small_consts(cx, es0, c)
        l2_phase(cx, c, es0, SEQ, DB, x_all, GY, GA, *l2w, y2, *declare_l2_scratch(nc))
    cx.s.emit()
    return nc


def declare_l2_weights(nc):
    wg = nc.dram_tensor("wg", [128, 16, 4096], F32, kind="ExternalInput").ap()
    wssm = nc.dram_tensor("wssm", [128, 32, 2048], F32, kind="ExternalInput").ap()
    wmla = nc.dram_tensor("wmla", [128, 16, 2048], F32, kind="ExternalInput").ap()
    wout = nc.dram_tensor("wout", [128, 16, 2048], F32, kind="ExternalInput").ap()
    nrm = nc.dram_tensor("nrm2", [128, KC], F32, kind="ExternalInput").ap()
    snw = nc.dram_tensor("snw", [128, 32], F32, kind="ExternalInput").ap()
    fnw = nc.dram_tensor("fnw", [1, D], F32, kind="ExternalInput").ap()
    return wg, wssm, wmla, wout, nrm, snw, fnw


def declare_l2_scratch(nc):
    WGS = nc.dram_tensor("WGS", [16, 128, 16, 128], BF16).ap()
    WGM = nc.dram_tensor("WGM", [16, 128, 16, 128], BF16).ap()
    WS = nc.dram_tensor("WS", [16, 128, 32, 128], BF16).ap()
    WM = nc.dram_tensor("WM", [16, 128, 16, 128], BF16).ap()
    return WGS, WGM, WS, WM


def _arr_k(w):
    K, C = w.shape
    return np.ascontiguousarray(w.reshape(K // 128, 128, C).transpose(1, 0, 2))


def _conv_idx(core):
    return np.concatenate([np.arange(core * 512, (core + 1) * 512), 4096 + np.arange(core * 128, (core + 1) * 128),
                           5120 + np.arange(core * 128, (core + 1) * 128)])


def _prep_l1_inputs(x_all, win, norm_in_w, q_norm_w, kv_norm_w, w_q_up, w_kv_up, cache_kv_latent, cache_k_rope, core, extra):
    f = np.float32
    offs = np.cumsum([0, 4096, 6144, 64, 512, 512, 64, 2048, 2048, 2048])
    h0 = 2 * core
    g_cols = win[:, offs[6] + h0 * 128: offs[6] + (h0 + 2) * 128]
    w_m = _arr_k(np.concatenate([win[:, offs[3]:offs[4]], win[:, offs[4]:offs[5]], win[:, offs[5]:offs[6]], g_cols], 1))
    wq = np.asarray(w_q_up, f)[0].reshape(512, 16, 192)
    w_q = np.concatenate([wq[:, h0, :128], wq[:, h0 + 1, :128], wq[:, h0, 128:], wq[:, h0 + 1, 128:]], 1)
    wkv = np.asarray(w_kv_up, f)[0].reshape(512, 16, 256)
    w_kv = np.concatenate([wkv[:, h0, :128], wkv[:, h0 + 1, :128], wkv[:, h0, 128:], wkv[:, h0 + 1, 128:]], 1)
    conv_w, conv_b, dt_bias, a_log, d_skip, state_conv, state_ssm = extra
    idx = _conv_idx(core)
    w_s = _arr_k(np.concatenate([win[:, core * 512:(core + 1) * 512], win[:, offs[1] + idx], win[:, offs[2] + core * 8: offs[2] + (core + 1) * 8]], 1))
    cwv = np.asarray(conv_w, f)[0][:, idx]
    DBn = state_conv.shape[1]
    return {
        "w_s": w_s, "cw": np.ascontiguousarray(cwv.reshape(4, 6, 128).transpose(2, 1, 0)),
        "cb": np.ascontiguousarray(np.asarray(conv_b, f)[0][idx].reshape(6, 128).T),
        "dtb": np.asarray(dt_bias, f)[0][core * 8:(core + 1) * 8].reshape(1, 8).copy(),
        "alog": np.asarray(a_log, f)[0][core * 8:(core + 1) * 8].reshape(1, 8).copy(),
        "dsk": np.asarray(d_skip, f)[0][core * 8:(core + 1) * 8].reshape(1, 8).copy(),
        "sconv": np.ascontiguousarray(np.asarray(state_conv, f)[0][:, :, idx]),
        "sssm": np.ascontiguousarray(np.asarray(state_ssm, f)[0][:, core * 8:(core + 1) * 8].reshape(DBn, 512, 128)),
        "x_all": x_all, "w_m": w_m, "w_q": _arr_k(w_q), "w_kv": _arr_k(w_kv),
        "nrm": np.ascontiguousarray(np.asarray(norm_in_w, f)[0].reshape(16, 128).T),
        "qnw": np.ascontiguousarray(np.asarray(q_norm_w, f)[0].reshape(4, 128).T),
        "kvw": np.asarray(kv_norm_w, f).reshape(1, 512),
        "cache_kv": np.ascontiguousarray(np.asarray(cache_kv_latent, f)[0]),
        "cache_kr": np.ascontiguousarray(np.asarray(cache_k_rope, f)[0]),
    }


def _prep_l2_weights(win, w_ssm_out, w_mla_out, w_out, norm_in_w, ssm_norm_w, final_norm_w):
    f = np.float32
    offs = np.cumsum([0, 4096, 6144, 64, 512, 512, 64, 2048, 2048, 2048])
    return {
        "wg": _arr_k(np.ascontiguousarray(win[:, offs[7]:offs[9]])),
        "wssm": _arr_k(np.asarray(w_ssm_out, f)[0]), "wmla": _arr_k(np.asarray(w_mla_out, f)[0]), "wout": _arr_k(np.asarray(w_out, f)[0]),
        "nrm2": np.ascontiguousarray(np.asarray(norm_in_w, f)[0].reshape(16, 128).T),
        "snw": np.ascontiguousarray(np.asarray(ssm_norm_w, f)[0].reshape(32, 128).T),
        "fnw": np.asarray(final_norm_w, f).reshape(1, D),
    }


def _assemble_y(results, key, SEQ, DB, DS, B):
    f = np.float32
    pp, sp_ = SEQ // NCORES, DB * DS // NCORES
    yp = np.concatenate([np.asarray(r[key])[:pp] for r in results], 0).reshape(B, SEQ, D).astype(f)
    ys = np.concatenate([np.asarray(r[key])[pp:pp + sp_] for r in results], 0).reshape(DB, DS, D).astype(f)
    return yp, ys


def kernel(x_prompt, x_sample, cache_kv_latent, cache_k_rope, state_ssm, state_conv, norm_in_w, w_in,
           conv_w, conv_b, dt_bias, a_log, d_skip, ssm_norm_w, w_ssm_out, q_norm_w, w_q_up, kv_norm_w,
           w_kv_up, w_mla_out, w_out, final_norm_w):
    f = np.float32
    B, SEQ, _ = x_prompt.shape
    DB, DS, _ = x_sample.shape
    PAST = cache_kv_latent.shape[2]
    NTOK = SEQ + DB * DS
    x_all = np.concatenate([np.asarray(x_prompt, f).reshape(SEQ, D), np.asarray(x_sample, f).reshape(DB * DS, D)], 0)
    win = np.asarray(w_in, f)[0]
    FUSED = bool(int(os.environ.get("K_FUSED", "0")))
    nc = build_l1(SEQ, DB, PAST, fused=FUSED)
    extra = (conv_w, conv_b, dt_bias, a_log, d_skip, state_conv, state_ssm)
    ims = [_prep_l1_inputs(x_all, win, norm_in_w, q_norm_w, kv_norm_w, w_q_up, w_kv_up, cache_kv_latent, cache_k_rope, cidx, extra)
           for cidx in range(NCORES)]
    if FUSED:
        l2in = _prep_l2_weights(win, w_ssm_out, w_mla_out, w_out, norm_in_w, ssm_norm_w, final_norm_w)
        for im in ims:
            im.update(l2in)
    res = run_bass_kernel_spmd(nc, ims, core_ids=list(range(NCORES)))
    r0 = res.results[0]
    global _DBG
    _DBG = {}
    kvl = r0["kvlat"]
    krp = r0["krope"]
    conv_p = np.zeros((1, B, 3, 6144), f)
    conv_s = np.zeros((1, DB, 3, 6144), f)
    ssm_p = np.zeros((1, B, 64, 64, 128), f)
    ssm_s = np.zeros((1, DB, 64, 64, 128), f)
    for cidx, r in enumerate(res.results):
        idx = _conv_idx(cidx)
        conv_p[0, 0][:, idx] = r["conv_o"][0]
        conv_s[0][:, :, idx] = r["conv_o"][1:]
        ssm_p[0, 0, cidx * 8:(cidx + 1) * 8] = r["ssm_o"][0].reshape(8, 64, 128)
        ssm_s[0, :, cidx * 8:(cidx + 1) * 8] = r["ssm_o"][1:].reshape(DB, 8, 64, 128)
    NS = DB * DS
    if FUSED:
        y_prompt, y_sample = _assemble_y(res.results, "y2", SEQ, DB, DS, B)
        return (y_prompt, y_sample,
                kvl[:SEQ].reshape(1, B, SEQ, 512), krp[:SEQ].reshape(1, B, SEQ, 64), ssm_p, conv_p,
                kvl[SEQ:].reshape(1, DB, DS, 512), krp[SEQ:].reshape(1, DB, DS, 64), ssm_s, conv_s)
    GYh = np.concatenate([np.asarray(r["YTc"]) for r in res.results], 1)
    GAh = np.concatenate([np.asarray(r["ATc"]) for r in res.results], 1)
    l2in = _prep_l2_weights(win, w_ssm_out, w_mla_out, w_out, norm_in_w, ssm_norm_w, final_norm_w)
    l2in.update({"x_all": x_all, "GY": GYh, "GA": GAh})
    if os.environ.get("K_ONLY_L1"):
        y_prompt, y_sample = np.zeros((B, SEQ, D), f), np.zeros((DB, DS, D), f)
    else:
        print("L1 done", flush=True)
        nc2 = build_l2(SEQ, DB)
        res2 = run_bass_kernel_spmd(nc2, [l2in for _ in range(NCORES)], core_ids=list(range(NCORES)))
        y_prompt, y_sample = _assemble_y(res2.results, "y2", SEQ, DB, DS, B)
    outs = (y_prompt, y_sample,
            kvl[:SEQ].reshape(1, B, SEQ, 512), krp[:SEQ].reshape(1, B, SEQ, 64),
            ssm_p, conv_p,
            kvl[SEQ:].reshape(1, DB, DS, 512), krp[SEQ:].reshape(1, DB, DS, 64),
            ssm_s, conv_s)
    return outs
```
